# Optimizing a Trainium2 kernel written in Bass

```python
import math
import jax, jax.numpy as jnp
from jax import lax
import numpy as np

D_MODEL = 1024
BATCH = 8
SEQ = 2048
DEPTH = 2

GRID_W = 64
CTX_LEN = 256
HEAD_DIM = 64
BLOCK = 128
A_Q_HEADS = 8
A_KV_HEADS = 2
WINDOW = 128
B_HEADS = 4
B_V_DIM = 2 * HEAD_DIM
A_Q_W = A_Q_HEADS * HEAD_DIM
A_KV_W = A_KV_HEADS * HEAD_DIM
B_QK_W = B_HEADS * 2 * HEAD_DIM
B_V_W = B_HEADS * B_V_DIM
ATTN_IN = A_Q_W + 2 * A_KV_W + 2 * B_QK_W + B_V_W
ATTN_OUT = A_Q_W + B_V_W
S5_WIDTH = D_MODEL
S5_GROUP = 16
S5_GROUPS = S5_WIDTH // S5_GROUP
S5_STATE = 64
D_FF = 4 * D_MODEL
ROPE_BASE = 10000.0
EPS = 1e-6
NEG_INF = -1e30
F32 = jnp.float32

kernel_name = "hybrid_prefix_dit_swa_diff_s5"


def rms_norm(x, g):
    xf = x.astype(F32)
    y = xf * lax.rsqrt(jnp.mean(xf * xf, axis=-1, keepdims=True) + EPS)
    return (y * g.astype(F32)).astype(x.dtype)


def axial_rope_tables(rows_n):
    row = jnp.repeat(jnp.arange(rows_n, dtype=F32), GRID_W)
    col = jnp.tile(jnp.arange(GRID_W, dtype=F32), rows_n)
    n_freq = HEAD_DIM // 4
    inv = ROPE_BASE ** (-jnp.arange(n_freq, dtype=F32) / n_freq)
    ang = jnp.concatenate([row[:, None] * inv, col[:, None] * inv], axis=-1)
    return jnp.cos(ang), jnp.sin(ang)


def apply_rope(x, cos, sin):
    shape = (cos.shape[0],) + (1,) * (x.ndim - 3) + (cos.shape[1],)
    cs = cos.reshape(shape).astype(x.dtype)
    sn = sin.reshape(shape).astype(x.dtype)
    x1, x2 = jnp.split(x, 2, axis=-1)
    return jnp.concatenate([x1 * cs - x2 * sn, x2 * cs + x1 * sn], axis=-1)


def split_attn_proj(z):
    bn, t, _ = z.shape
    cuts = np.cumsum([A_Q_W, A_KV_W, A_KV_W, B_QK_W, B_QK_W]).tolist()
    qa, ka, va, qb, kb, vb = jnp.split(z, cuts, axis=-1)
    return (qa.reshape(bn, t, A_Q_HEADS, HEAD_DIM),
            ka.reshape(bn, t, A_KV_HEADS, HEAD_DIM),
            va.reshape(bn, t, A_KV_HEADS, HEAD_DIM),
            qb.reshape(bn, t, B_HEADS, 2, HEAD_DIM),
            kb.reshape(bn, t, B_HEADS, 2, HEAD_DIM),
            vb.reshape(bn, t, B_HEADS, B_V_DIM))


def window_attention(q, k, v, kc, vc, sink):
    bn, L, hq, d = q.shape
    hkv = k.shape[2]
    g = hq // hkv
    nb = L // BLOCK
    scale = d ** -0.5
    qb = q.reshape(bn, nb, BLOCK, hkv, g, d)
    pad = ((0, 0), (BLOCK, BLOCK), (0, 0), (0, 0))
    kp = jnp.pad(k, pad).reshape(bn, nb + 2, BLOCK, hkv, d)
    vp = jnp.pad(v, pad).reshape(bn, nb + 2, BLOCK, hkv, d)
    kw = jnp.concatenate([kp[:, :-2], kp[:, 1:-1], kp[:, 2:]], axis=2)
    vw = jnp.concatenate([vp[:, :-2], vp[:, 1:-1], vp[:, 2:]], axis=2)
    s_win = jnp.einsum('bnqhgd,bnkhd->bnhgqk', qb, kw).astype(F32) * scale
    qi = jnp.arange(BLOCK)[:, None]
    kj = jnp.arange(3 * BLOCK)[None, :]
    jpos = jnp.arange(nb)[:, None, None] * BLOCK + kj[None] - BLOCK
    valid = (jnp.abs(kj - BLOCK - qi) <= WINDOW)[None] & (jpos >= 0) & (jpos < L)
    s_win = jnp.where(valid[None, :, None, None], s_win, NEG_INF)
    s_ctx = jnp.einsum('bnqhgd,bkhd->bnhgqk', qb, kc).astype(F32) * scale
    s_sink = jnp.broadcast_to(sink.astype(F32).reshape(1, 1, hkv, g, 1, 1), s_win.shape[:-1] + (1,))
    p = jax.nn.softmax(jnp.concatenate([s_win, s_ctx, s_sink], axis=-1), axis=-1).astype(v.dtype)
    nw = 3 * BLOCK
    nc = kc.shape[1]
    o = (jnp.einsum('bnhgqk,bnkhd->bnqhgd', p[..., :nw], vw)
         + jnp.einsum('bnhgqk,bkhd->bnqhgd', p[..., nw:nw + nc], vc))
    return o.reshape(bn, L, hq * d)


def gqa_sink_dense(q, k, v, sink):
    bn, t, hq, d = q.shape
    hkv = k.shape[2]
    g = hq // hkv
    qg = q.reshape(bn, t, hkv, g, d)
    s = jnp.einsum('bqhgd,bkhd->bhgqk', qg, k).astype(F32) * d ** -0.5
    sk = jnp.broadcast_to(sink.astype(F32).reshape(1, hkv, g, 1, 1), s.shape[:-1] + (1,))
    p = jax.nn.softmax(jnp.concatenate([s, sk], axis=-1), axis=-1)[..., :-1].astype(v.dtype)
    o = jnp.einsum('bhgqk,bkhd->bqhgd', p, v)
    return o.reshape(bn, t, hq * d)


def diff_attend(q, k, v, lam):
    s = jnp.einsum('bqhmd,bkhmd->bhmqk', q, k).astype(F32) * HEAD_DIM ** -0.5
    p = jax.nn.softmax(s, axis=-1)
    w = p[:, :, 0] - lam * p[:, :, 1]
    return jnp.einsum('bhqk,bkhe->bqhe', w.astype(v.dtype), v)


def diff_attention_latent(q, k, v, kc, vc, lam):
    bn, L, h, _, d = q.shape
    nb = L // BLOCK
    k_all = jnp.concatenate([kc, k], axis=1)
    v_all = jnp.concatenate([vc, v], axis=1)
    qb = jnp.moveaxis(q.reshape(bn, nb, BLOCK, h, 2, d), 1, 0)
    o = lax.map(lambda qblk: diff_attend(qblk, k_all, v_all, lam), qb)
    return jnp.moveaxis(o, 0, 1).reshape(bn, L, h, v.shape[-1])


def attn_mixer(h_lat, h_ctx, w_in, w_out, a_qn, a_kn, a_sink, b_qn, b_kn, lq1, lk1, lq2, lk2, subln,
               cos, sin, lambda_init, need_ctx):
    qa, ka, va, qb, kb, vb = split_attn_proj(h_lat @ w_in)
    qa_c, ka_c, va_c, qb_c, kb_c, vb_c = split_attn_proj(h_ctx @ w_in)
    qa = apply_rope(rms_norm(qa, a_qn), cos, sin)
    ka = apply_rope(rms_norm(ka, a_kn), cos, sin)
    qb = apply_rope(rms_norm(qb, b_qn), cos, sin)
    kb = apply_rope(rms_norm(kb, b_kn), cos, sin)
    ka_c = rms_norm(ka_c, a_kn)
    kb_c = rms_norm(kb_c, b_kn)
    lam = (jnp.exp(jnp.sum(lq1.astype(F32) * lk1.astype(F32)))
           - jnp.exp(jnp.sum(lq2.astype(F32) * lk2.astype(F32))) + lambda_init)

    def merge(ya, yb):
        yb = rms_norm(yb, subln) * (1.0 - lambda_init)
        yb = yb.reshape(yb.shape[0], yb.shape[1], B_V_W)
        return jnp.concatenate([ya, yb], axis=-1) @ w_out

    y_lat = merge(window_attention(qa, ka, va, ka_c, va_c, a_sink),
                  diff_attention_latent(qb, kb, vb, kb_c, vb_c, lam))
    y_ctx = None
    if need_ctx:
        qa_c = rms_norm(qa_c, a_qn)
        qb_c = rms_norm(qb_c, b_qn)
        y_ctx = merge(gqa_sink_dense(qa_c, ka_c, va_c, a_sink), diff_attend(qb_c, kb_c, vb_c, lam))
    return y_lat, y_ctx


def s5_discretize(lam_re, lam_im, log_step, b_re, b_im):
    lam_re = lam_re.astype(F32)
    lam_im = lam_im.astype(F32)
    b_re = b_re.astype(F32)
    b_im = b_im.astype(F32)
    dt = jnp.exp(log_step.astype(F32))[:, None]
    mag = jnp.exp(lam_re * dt)
    ab_re = mag * jnp.cos(lam_im * dt)
    ab_im = mag * jnp.sin(lam_im * dt)
    den = lam_re * lam_re + lam_im * lam_im
    nr = ab_re - 1.0
    f_re = (nr * lam_re + ab_im * lam_im) / den
    f_im = (ab_im * lam_re - nr * lam_im) / den
    bb_re = f_re[..., None] * b_re - f_im[..., None] * b_im
    bb_im = f_re[..., None] * b_im + f_im[..., None] * b_re
    return ab_re, ab_im, bb_re, bb_im


def _cplx_combine(e1, e2):
    a1r, a1i, b1r, b1i = e1
    a2r, a2i, b2r, b2i = e2
    return (a2r * a1r - a2i * a1i,
            a2r * a1i + a2i * a1r,
            a2r * b1r - a2i * b1i + b2r,
            a2r * b1i + a2i * b1r + b2i)


def s5_scan(u, ab_re, ab_im, bb_re, bb_im, h0, reverse):
    t = u.shape[1]
    bu_re = jnp.einsum('btgh,gph->btgp', u, bb_re)
    bu_im = jnp.einsum('btgh,gph->btgp', u, bb_im)
    if h0 is not None:
        h0_re, h0_im = h0
        t0 = t - 1 if reverse else 0
        bu_re = bu_re.at[:, t0].add(ab_re * h0_re - ab_im * h0_im)
        bu_im = bu_im.at[:, t0].add(ab_re * h0_im + ab_im * h0_re)
    a_re = jnp.broadcast_to(ab_re[None, None], (1, t) + ab_re.shape)
    a_im = jnp.broadcast_to(ab_im[None, None], (1, t) + ab_im.shape)
    _, _, h_re, h_im = lax.associative_scan(_cplx_combine, (a_re, a_im, bu_re, bu_im), reverse=reverse, axis=1)
    return h_re, h_im


def s5_readout(h_re, h_im, c_re, c_im):
    return jnp.einsum('btgp,ghp->btgh', h_re, c_re) - jnp.einsum('btgp,ghp->btgh', h_im, c_im)


def s5_mixer(h_lat, h_ctx, w_in, lam_re, lam_im, log_step, b_re, b_im, c_re, c_im, d_skip, glu_w, glu_b, w_out,
             need_ctx):
    bn, L, _ = h_lat.shape
    nc = h_ctx.shape[1]
    u_lat = (h_lat @ w_in).astype(F32)
    u_ctx = (h_ctx @ w_in).astype(F32)
    ul = u_lat.reshape(bn, L, S5_GROUPS, S5_GROUP)
    uc = u_ctx.reshape(bn, nc, S5_GROUPS, S5_GROUP)
    dsk = d_skip.astype(F32)
    y_lat = u_lat * dsk
    y_ctx = u_ctx * dsk if need_ctx else None
    for direction in range(2):
        reverse = direction == 1
        ab_re, ab_im, bb_re, bb_im = s5_discretize(lam_re[direction], lam_im[direction], log_step[direction],
                                                   b_re[direction], b_im[direction])
        cr = c_re[direction].astype(F32)
        ci = c_im[direction].astype(F32)
        hc_re, hc_im = s5_scan(uc, ab_re, ab_im, bb_re, bb_im, None, reverse)
        t_end = 0 if reverse else nc - 1
        hl_re, hl_im = s5_scan(ul, ab_re, ab_im, bb_re, bb_im, (hc_re[:, t_end], hc_im[:, t_end]), reverse)
        y_lat = y_lat + s5_readout(hl_re, hl_im, cr, ci).reshape(bn, L, S5_WIDTH)
        if need_ctx:
            y_ctx = y_ctx + s5_readout(hc_re, hc_im, cr, ci).reshape(bn, nc, S5_WIDTH)

    def out_map(y):
        g = jax.nn.gelu(y).astype(h_lat.dtype)
        g = g * jax.nn.sigmoid(g @ glu_w + glu_b)
        return g @ w_out

    return out_map(y_lat), (out_map(y_ctx) if need_ctx else None)


def sq_relu_mlp(h, w1, w2):
    return jnp.square(jax.nn.relu(h @ w1)) @ w2


def setup_inputs(seed: int = 0) -> dict:
    key = jax.random.key(seed)
    ks = iter(jax.random.split(key, 40))
    n_attn = (DEPTH + 1) // 2
    n_ssm = DEPTH // 2

    def nrm(shape, scale):
        return scale * jax.random.normal(next(ks), shape, F32)

    def gain(shape):
        return 1.0 + nrm(shape, 0.02)

    n_idx = jnp.arange(S5_STATE, dtype=F32)
    s5_state_shape = (n_ssm, 2, S5_GROUPS, S5_STATE)
    return {
        "x": nrm((BATCH, SEQ, D_MODEL), 1.0),
        "c": nrm((BATCH, D_MODEL), 1.0),
        "ctx": nrm((BATCH, CTX_LEN, D_MODEL), 1.0),
        "c_ctx": nrm((D_MODEL,), 1.0),
        "norm1_g": gain((DEPTH, D_MODEL)),
        "norm2_g": gain((DEPTH, D_MODEL)),
        "mod_w": nrm((DEPTH, D_MODEL, 6 * D_MODEL), 0.5 * D_MODEL ** -0.5),
        "mod_b": nrm((DEPTH, 6 * D_MODEL), 0.02),
        "mlp_w1": nrm((DEPTH, D_MODEL, D_FF), D_MODEL ** -0.5),
        "mlp_w2": nrm((DEPTH, D_FF, D_MODEL), D_FF ** -0.5),
        "attn_w_in": nrm((n_attn, D_MODEL, ATTN_IN), D_MODEL ** -0.5),
        "attn_w_out": nrm((n_attn, ATTN_OUT, D_MODEL), ATTN_OUT ** -0.5),
        "a_q_norm": gain((n_attn, HEAD_DIM)),
        "a_k_norm": gain((n_attn, HEAD_DIM)),
        "a_sink": nrm((n_attn, A_Q_HEADS), 0.5),
        "b_q_norm": gain((n_attn, HEAD_DIM)),
        "b_k_norm": gain((n_attn, HEAD_DIM)),
        "b_lq1": nrm((n_attn, HEAD_DIM), 0.1),
        "b_lk1": nrm((n_attn, HEAD_DIM), 0.1),
        "b_lq2": nrm((n_attn, HEAD_DIM), 0.1),
        "b_lk2": nrm((n_attn, HEAD_DIM), 0.1),
        "b_subln": gain((n_attn, B_V_DIM)),
        "s5_w_in": nrm((n_ssm, D_MODEL, S5_WIDTH), D_MODEL ** -0.5),
        "s5_lambda_re": -0.5 + nrm(s5_state_shape, 0.01),
        "s5_lambda_im": jnp.pi * n_idx + nrm(s5_state_shape, 0.01),
        "s5_log_step": jax.random.uniform(next(ks), (n_ssm, 2, S5_GROUPS), dtype=F32,
                                          minval=math.log(1e-3), maxval=math.log(1e-1)),
        "s5_b_re": nrm((n_ssm, 2, S5_GROUPS, S5_STATE, S5_GROUP), (2 * S5_GROUP) ** -0.5),
        "s5_b_im": nrm((n_ssm, 2, S5_GROUPS, S5_STATE, S5_GROUP), (2 * S5_GROUP) ** -0.5),
        "s5_c_re": nrm((n_ssm, 2, S5_GROUPS, S5_GROUP, S5_STATE), (2 * S5_STATE) ** -0.5),
        "s5_c_im": nrm((n_ssm, 2, S5_GROUPS, S5_GROUP, S5_STATE), (2 * S5_STATE) ** -0.5),
        "s5_d": nrm((n_ssm, S5_WIDTH), 1.0),
        "s5_glu_w": nrm((n_ssm, S5_WIDTH, S5_WIDTH), S5_WIDTH ** -0.5),
        "s5_glu_b": nrm((n_ssm, S5_WIDTH), 0.02),
        "s5_w_out": nrm((n_ssm, S5_WIDTH, D_MODEL), S5_WIDTH ** -0.5),
    }


def reference(x, c, ctx, c_ctx, norm1_g, norm2_g, mod_w, mod_b, mlp_w1, mlp_w2, attn_w_in, attn_w_out,
              a_q_norm, a_k_norm, a_sink, b_q_norm, b_k_norm, b_lq1, b_lk1, b_lq2, b_lk2, b_subln,
              s5_w_in, s5_lambda_re, s5_lambda_im, s5_log_step, s5_b_re, s5_b_im, s5_c_re, s5_c_im, s5_d,
              s5_glu_w, s5_glu_b, s5_w_out):
    bn, L, _ = x.shape
    ROWS = L // GRID_W
    cos, sin = axial_rope_tables(ROWS)
    s_lat = jax.nn.silu(c)
    s_ctx = jax.nn.silu(c_ctx)
    h_lat, h_ctx = x, ctx
    for i in range(DEPTH):
        last = i == DEPTH - 1
        j = i // 2
        m_lat = jnp.split((s_lat @ mod_w[i] + mod_b[i])[:, None, :], 6, axis=-1)
        m_ctx = jnp.split((s_ctx @ mod_w[i] + mod_b[i])[None, None, :], 6, axis=-1)
        a_lat = rms_norm(h_lat, norm1_g[i]) * (1.0 + m_lat[1]) + m_lat[0]
        a_ctx = rms_norm(h_ctx, norm1_g[i]) * (1.0 + m_ctx[1]) + m_ctx[0]
        if i % 2 == 0:
            lambda_init = 0.8 - 0.6 * math.exp(-0.3 * i)
            y_lat, y_ctx = attn_mixer(a_lat, a_ctx, attn_w_in[j], attn_w_out[j], a_q_norm[j], a_k_norm[j],
                                      a_sink[j], b_q_norm[j], b_k_norm[j], b_lq1[j], b_lk1[j], b_lq2[j],
                                      b_lk2[j], b_subln[j], cos, sin, lambda_init, not last)
        else:
            y_lat, y_ctx = s5_mixer(a_lat, a_ctx, s5_w_in[j], s5_lambda_re[j], s5_lambda_im[j], s5_log_step[j],
                                    s5_b_re[j], s5_b_im[j], s5_c_re[j], s5_c_im[j], s5_d[j], s5_glu_w[j],
                                    s5_glu_b[j], s5_w_out[j], not last)
        h_lat = h_lat + m_lat[2] * y_lat
        f_lat = rms_norm(h_lat, norm2_g[i]) * (1.0 + m_lat[4]) + m_lat[3]
        h_lat = h_lat + m_lat[5] * sq_relu_mlp(f_lat, mlp_w1[i], mlp_w2[i])
        if not last:
            h_ctx = h_ctx + m_ctx[2] * y_ctx
            f_ctx = rms_norm(h_ctx, norm2_g[i]) * (1.0 + m_ctx[4]) + m_ctx[3]
            h_ctx = h_ctx + m_ctx[5] * sq_relu_mlp(f_ctx, mlp_w1[i], mlp_w2[i])
    return h_lat
```

```python
import math
from contextlib import ExitStack

import numpy as np
import concourse.bass as bass
import concourse.mybir as mybir
from concourse.ap import AP
from concourse.bass_utils import run_bass_kernel_spmd

F32 = mybir.dt.float32
BF16 = mybir.dt.bfloat16
I32 = mybir.dt.int32
AF = mybir.ActivationFunctionType
ALU = mybir.AluOpType
AX = mybir.AxisListType

D = 1024
NT = 18
NTOK = NT * 128
EPS = 1e-6
SB_LO = 16640
SB_HI = 229376
SAME_ENG_SKIP_DIST = 2


def dsize(dt):
    return {F32: 4, BF16: 2, I32: 4}[dt]


class Buf:
    __slots__ = ("w", "r")

    def __init__(self):
        self.w = None
        self.r = {}


class KB:
    ENGS = ("pe", "dve", "act", "pool", "sp")
    NDS = 24

    def __init__(self, nc, es):
        self.nc = nc
        self.E = {"pe": nc.tensor, "dve": nc.vector, "act": nc.scalar, "pool": nc.gpsimd, "sp": nc.sync}
        self.sems = {}
        for e in self.ENGS:
            self.sems[e] = es.enter_context(nc.semaphore("s_" + e))
        for i in range(self.NDS):
            self.sems[("d", i)] = es.enter_context(nc.semaphore("s_d%d" % i))
        self.cnt = {e: 0 for e in self.ENGS}
        self.dcnt = [0] * self.NDS
        self.dma_i = 0
        self.seen = {e: {} for e in self.ENGS}
        self.nbuf = 0

    def _wait(self, eng, toks):
        need = {}
        for key, val in toks:
            if key == eng and (eng == "pe" or self.cnt[eng] - val >= SAME_ENG_SKIP_DIST):
                continue
            if self.seen[eng].get(key, 0) < val and need.get(key, 0) < val:
                need[key] = val
        for key, val in need.items():
            self.E[eng].wait_ge(self.sems[key], val)
            self.seen[eng][key] = val

    @staticmethod
    def _deps(r, w):
        toks = []
        for b in r:
            if b.w is not None:
                toks.append(b.w)
        for b in w:
            if b.w is not None:
                toks.append(b.w)
            toks.extend(b.r.items())
        return toks

    @staticmethod
    def _upd(tok, r, w):
        for b in r:
            if b.r.get(tok[0], 0) < tok[1]:
                b.r[tok[0]] = tok[1]
        for b in w:
            b.w = tok
            b.r = {}

    def op(self, eng, fn, r=(), w=(), nosync=False):
        toks = self._deps(r, w)
        if nosync:
            toks = [t for t in toks if t[0] != eng]
        self._wait(eng, toks)
        ins = fn(self.E[eng])
        self.cnt[eng] += 1
        ins.then_inc(self.sems[eng], 1)
        self._upd((eng, self.cnt[eng]), r, w)

    def dma(self, q, out, in_, r=(), w=()):
        toks = self._deps(r, w)
        i = self.dma_i % self.NDS
        self.dma_i += 1
        key = ("d", i)
        if self.dcnt[i] > 0:
            toks.append((key, self.dcnt[i]))
        self._wait(q, toks)
        ins = self.E[q].dma_start(out=out, in_=in_)
        self.dcnt[i] += 16
        ins.then_inc(self.sems[key], 16)
        self._upd((key, self.dcnt[i]), r, w)

    def barrier(self):
        toks = [(e, self.cnt[e]) for e in self.ENGS if self.cnt[e] > 0]
        toks += [(("d", i), self.dcnt[i]) for i in range(self.NDS) if self.dcnt[i] > 0]
        for e in self.ENGS:
            self._wait(e, toks)


class T:
    def __init__(self, kb, name, shape, dt, off, nslots=1):
        self.t = kb.nc.alloc_sbuf_tensor_at(name, list(shape), dt, offset=off)
        self.shape = list(shape)
        self.dt = dt
        self.row = int(np.prod(shape[1:]))
        self.bytes = self.row * dsize(dt)
        self.b = [Buf() for _ in range(nslots)]
        self.off = off
        self.end = off + ((self.bytes + 63) // 64) * 64

    def ap(self, off, free, p0=0, np_=128):
        return AP(self.t, p0 * self.row + off, [[self.row, np_]] + [list(f) for f in free])

    def __getitem__(self, k):
        return self.t[k]


class Arena:
    def __init__(self, lo, hi):
        self.lo, self.hi, self.top = lo, hi, lo

    def take(self, nbytes):
        off = self.top
        self.top += ((nbytes + 63) // 64) * 64
        assert self.top <= self.hi, ("arena overflow", self.top, self.hi)
        return off


def pap(t, off, free, p0=0, np_=128, row=512):
    return AP(t, p0 * row + off, [[row, np_]] + [list(f) for f in free])


def build_program(layers, n_out_rows_last=2048, debug=None):
    nc = bass.Bass("TRN2", target_bir_lowering=False)
    dr = {}

    def din(name, shape, dt=F32):
        dr[name] = nc.dram_tensor(name, list(shape), dt, kind="ExternalInput").ap()
        return dr[name]

    xin = din("xin", [NTOK, D])
    cvec = din("cvec", [128, 8, 2])
    mod_w = din("mod_w", [2, D, 6 * D])
    modb = din("modb", [2, 128, 48])
    g1d = din("g1", [2, 128, 8])
    g2d = din("g2", [2, 128, 8])
    w1d = din("mlp_w1", [2, D, 4 * D])
    w2d = din("mlp_w2", [2, 4 * D, D])
    identd = din("ident", [128, 128])
    if 0 in layers:
        wind = din("attn_w_in", [D, 2304])
        woutd = din("attn_w_out", [D, D])
        ropec = din("rope_cos", [128, 16, 32])
        ropes = din("rope_sin", [128, 16, 32])
        gaind = din("qk_gain", [128, 4, 64])
        sinkd = din("a_sink", [128, 8])
        lqkd = din("b_lqk", [128, 4, 64])
        sublnd = din("b_subln", [128, 128])
        maskd = din("masks", [2, 128, 128])
    if 1 in layers:
        s5wind = din("s5_w_in", [D, D])
        s5woutd = din("s5_w_out", [D, D])
        gluwd = din("s5_glu_w", [D, D])
        glubd = din("s5_glu_b", [1, D])
        lamnd = din("s5_lam_n", [2, 3, 128, 32])
        ktabd = din("s5_ktab", [128, 6, 8])
        bnd = din("s5_bn", [2, 2, 128, 32, 16])
        cnd = din("s5_cn", [2, 2, 128, 32, 16])
        bsjd = din("s5_bsj", [2, 2, 128, 4096])
        dskd = din("s5_dsk", [128, 1024])
        m3d = din("s5_masks", [3, 128, 128])
        escr = nc.dram_tensor("escr", [2, 2, 8, 4096], F32, kind="Internal").ap()
    last_layer = layers[-1]
    n_rows_out = NTOK if last_layer != 1 else 2048
    hout = nc.dram_tensor("hout", [n_rows_out, D], F32, kind="ExternalOutput").ap()
    hmid = None
    if len(layers) > 1:
        hmid = nc.dram_tensor("hmid", [NTOK, D], F32, kind="Internal").ap()
    dbg = None
    if debug is not None:
        dbg = nc.dram_tensor("dbg", list(debug), F32, kind="ExternalOutput").ap()

    with ExitStack() as es:
        kb = KB(nc, es)
        op, dma = kb.op, kb.dma
        PS = [es.enter_context(nc.psum_tensor("bank%d" % i, [128, 512], F32)) for i in range(8)]
        PB = [Buf() for _ in range(8)]

        ar = Arena(SB_LO, SB_HI)

        def mk(name, shape, dt, nslots=1, arena=None):
            a = arena or ar
            row = int(np.prod(shape[1:]))
            off = a.take(row * dsize(dt))
            return T(kb, name, shape, dt, off, nslots)

        ident_f = mk("ident_f", [128, 128], F32)
        ident_b = mk("ident_b", [128, 128], BF16)
        ones_f = mk("ones_f", [128, 128], F32)
        m_all = mk("m_all", [128, 48, 2], F32)
        gam = mk("gam", [128, 2, 8, 2], F32)
        gate_rep = mk("gate_rep", [128, 2, 2, 1024], F32, nslots=4)
        small = mk("small", [128, 64], F32, nslots=8)
        cv = mk("cv", [128, 8, 2], F32)
        sv = mk("sv", [128, 8, 2], F32)
        g1s = mk("g1s", [128, 8], F32)
        g2s = mk("g2s", [128, 8], F32)
        modbs = mk("modbs", [128, 48], F32)
        macc = mk("macc", [128, 96], F32)
        junk = mk("junk", [128, 1024], BF16)
        P_END = ar.top
        R_H = (P_END, P_END + 73728)
        R_A = (R_H[1], R_H[1] + 36864)
        R_B = (R_A[1], SB_HI)
        assert R_B[1] - R_B[0] >= 78500, (R_B, "region B too small")

        dma("sp", ident_f[:], identd[:, :], w=[ident_f.b[0]])
        dma("pool", ident_b[:], identd[:, :], w=[ident_b.b[0]])
        op("dve", lambda e: e.memset(ones_f[:], 1.0), w=[ones_f.b[0]])
        dma("sp", cv[:], cvec[:, :, :], w=[cv.b[0]])
        op("act", lambda e: e.activation(out=sv[:], in_=cv[:], func=AF.Silu), r=[cv.b[0]], w=[sv.b[0]])

        stat_i = [0]

        def stat_slot():
            i = stat_i[0] % 8
            stat_i[0] += 1
            return i

        def rstd_from_ss(ss_ap, out_ap, n, bufs_r, bufs_w, cols=1):
            op("dve", lambda e: e.tensor_scalar(out_ap, ss_ap, 1.0 / n, EPS, ALU.mult, ALU.add), r=bufs_r, w=bufs_w)
            op("act", lambda e: e.activation(out=out_ap, in_=out_ap, func=AF.Sqrt), r=bufs_w, w=bufs_w)
            op("dve", lambda e: e.reciprocal(out_ap, out_ap), r=bufs_w, w=bufs_w)

        def phase_mod(li):
            a = Arena(*R_B)
            mw = [mk("mw%d" % i, [128, 6144], F32, arena=a) for i in range(2)]
            dg = [mk("dg%d" % i, [128, 128], F32, arena=a) for i in range(2)]
            dma("sp", g1s[:], g1d[li, :, :], w=[g1s.b[0]])
            dma("sp", g2s[:], g2d[li, :, :], w=[g2s.b[0]])
            dma("sp", modbs[:], modb[li, :, :], w=[modbs.b[0]])
            for kc in range(8):
                m = mw[kc % 2]
                dma("sp", m[:], mod_w[li, kc * 128:(kc + 1) * 128, :], w=[m.b[0]])
                for oc in range(48):
                    op("pe", lambda e, oc=oc, m=m, kc=kc: e.matmul(
                        PS[0][:, oc * 2:oc * 2 + 2], lhsT=m[:, oc * 128:(oc + 1) * 128], rhs=sv[:, kc, :],
                        start=True, stop=True), r=[m.b[0], sv.b[0]], w=[PB[0]])
                if kc == 0:
                    op("dve", lambda e: e.tensor_copy(out=macc[:], in_=PS[0][:, 0:96]), r=[PB[0]], w=[macc.b[0]])
                else:
                    op("dve", lambda e: e.tensor_tensor(out=macc[:], in0=PS[0][:, 0:96], in1=macc[:], op=ALU.add),
                       r=[PB[0], macc.b[0]], w=[macc.b[0]])
            op("dve", lambda e: e.tensor_tensor(
                out=m_all[:], in0=macc.ap(0, [[2, 48], [1, 2]]), in1=modbs.ap(0, [[1, 48], [0, 2]]), op=ALU.add),
               r=[macc.b[0], modbs.b[0]], w=[m_all.b[0]])
            for ni, (gs, oc0) in enumerate(((g1s, 8), (g2s, 32))):
                op("dve", lambda e, ni=ni, gs=gs, oc0=oc0: e.scalar_tensor_tensor(
                    out=gam.ap(ni * 16, [[2, 8], [1, 2]]), in0=m_all.ap(oc0 * 2, [[2, 8], [1, 2]]), scalar=1.0,
                    in1=gs.ap(0, [[1, 8], [0, 2]]), op0=ALU.add, op1=ALU.mult),
                   r=[m_all.b[0], gs.b[0]], w=[gam.b[0]])
            k = 0
            for gi, oc0 in enumerate((16, 40)):
                for r_ in range(2):
                    for half in range(2):
                        bank = 1 + (k % 2)
                        k += 1
                        for c4 in range(4):
                            c = half * 4 + c4
                            d_ = dg[c % 2]
                            op("dve", lambda e, d_=d_, c=c, oc0=oc0, r_=r_: e.tensor_scalar(
                                d_[:], ident_f[:], m_all[:, oc0 + c, r_:r_ + 1], None, ALU.mult),
                               r=[ident_f.b[0], m_all.b[0]], w=[d_.b[0]])
                            op("pe", lambda e, d_=d_, c4=c4, bank=bank: e.matmul(
                                PS[bank][:, c4 * 128:(c4 + 1) * 128], lhsT=ones_f[:], rhs=d_[:], start=True, stop=True),
                               r=[ones_f.b[0], d_.b[0]], w=[PB[bank]])
                        sl = gi * 2 + r_
                        op("act", lambda e, gi=gi, r_=r_, half=half, bank=bank: e.activation(
                            out=gate_rep[:, gi, r_, half * 512:(half + 1) * 512], in_=PS[bank][:, :], func=AF.Identity),
                           r=[PB[bank]], w=[gate_rep.b[sl]])
            kb.barrier()

        def phase_norm(li, ni, src_dram, hT, aT, tiles):
            a = Arena(*R_B)
            xs = [mk("nx%d" % i, [128, 1024], F32, arena=a) for i in range(3)]
            xh = [mk("nxh%d" % i, [128, 1024], F32, arena=a) for i in range(2)]
            for it, tt in enumerate(tiles):
                r_ = 1 if tt < 2 else 0
                if src_dram is not None:
                    x = xs[it % 3]
                    dma("sp", x[:], src_dram[tt * 128:(tt + 1) * 128, :], w=[x.b[0]])
                    xap, xb = x[:], x.b[0]
                else:
                    xap, xb = hT[:, tt, :], hT.b[tt]
                si = stat_slot()
                ss = small[:, si * 8:si * 8 + 1]
                sb_ = small.b[si]
                op("act", lambda e, xap=xap, ss=ss: e.activation(out=junk[:], in_=xap, func=AF.Square, accum_out=ss),
                   r=[xb], w=[junk.b[0], sb_])
                rstd_from_ss(ss, ss, 1024.0, [sb_], [sb_])
                h_ = xh[it % 2]
                op("dve", lambda e, h_=h_, xap=xap, ss=ss: e.tensor_scalar(h_[:], xap, ss, None, ALU.mult),
                   r=[xb, sb_], w=[h_.b[0]])
                for half in range(2):
                    bank = 5 + ((it * 2 + half) % 3)
                    for c4 in range(4):
                        c = half * 4 + c4
                        op("pe", lambda e, h_=h_, c=c, c4=c4, bank=bank: e.transpose(
                            PS[bank][:, c4 * 128:(c4 + 1) * 128], h_[:, c * 128:(c + 1) * 128], ident_f[:]),
                           r=[h_.b[0], ident_f.b[0]], w=[PB[bank]])
                    for c4 in range(4):
                        c = half * 4 + c4
                        g_ap = gam[:, ni, c, r_:r_ + 1]
                        oc_shift = (0 if ni == 0 else 24) + c
                        b_ap = m_all[:, oc_shift, r_:r_ + 1]
                        o_ap = aT[:, c, tt * 128:(tt + 1) * 128]
                        i_ap = PS[bank][:, c4 * 128:(c4 + 1) * 128]
                        if c4 % 2 == 0:
                            op("act", lambda e, o_ap=o_ap, i_ap=i_ap, g_ap=g_ap, b_ap=b_ap: e.activation(
                                out=o_ap, in_=i_ap, func=AF.Identity, bias=b_ap, scale=g_ap),
                               r=[PB[bank], gam.b[0], m_all.b[0]], w=[aT.b[tt]])
                        else:
                            op("dve", lambda e, o_ap=o_ap, i_ap=i_ap, g_ap=g_ap, b_ap=b_ap: e.tensor_scalar(
                                o_ap, i_ap, g_ap, b_ap, ALU.mult, ALU.add),
                               r=[PB[bank], gam.b[0], m_all.b[0]], w=[aT.b[tt]])
            kb.barrier()

        def phase_attn(aT, y):
            a = Arena(*R_B)
            QKT = mk("QKT", [128, 13, NTOK], BF16, nslots=NT, arena=a)
            VB = mk("VB", [128, NT, 4, 129], BF16, nslots=NT, arena=a)
            ah = Arena(*R_H)
            VA = mk("VA", [128, NT, 2, 65], BF16, nslots=NT, arena=ah)
            win = mk("win", [128, 8, 2304], BF16, arena=ah)
            cosT = mk("cosT", [128, 16, 32], F32, arena=ah)
            sinT = mk("sinT", [128, 16, 32], F32, arena=ah)
            gains = mk("gains", [128, 4, 64], F32, arena=ah)
            scr = mk("scr", [128, 26, 64], F32, arena=ah)
            xraw = [mk("xraw%d" % i, [128, 26, 64], F32, arena=ah) for i in range(2)]
            qkt = [mk("qkt%d" % i, [128, 26, 64], BF16, arena=ah) for i in range(2)]
            ssb = mk("ssb", [128, 2, 32], F32, nslots=2, arena=ah)
            for kc in range(8):
                for b2 in range(2):
                    dma("pool", win.ap(kc * 2304 + b2 * 64, [[128, 4], [1, 64]]),
                        AP(wind.tensor, kc * 128 * 2304 + b2 * 256, [[2304, 128], [64, 4], [1, 64]]), w=[win.b[0]])
                dma("pool", win[:, kc, 512:2304], wind[kc * 128:(kc + 1) * 128, 512:2304], w=[win.b[0]])
            dma("sp", cosT[:], ropec[:, :, :], w=[cosT.b[0]])
            dma("sp", sinT[:], ropes[:, :, :], w=[sinT.b[0]])
            dma("sp", gains[:], gaind[:, :, :], w=[gains.b[0]])
            op("pool", lambda e: e.memset(VA.ap(64, [[130, NT], [65, 2], [1, 1]]), 1.0), w=VA.b)
            op("pool", lambda e: e.memset(VB.ap(128, [[516, NT], [129, 4], [1, 1]]), 1.0), w=VB.b)

            qk_ranges = [(0, 0, 512, 0), (1, 0, 128, 8), (1, 256, 256, 10), (2, 0, 512, 14), (3, 0, 256, 22)]
            gtypes = [(0, 8), (8, 2), (10, 8), (18, 8)]
            ncols = [512, 512, 512, 512, 256]

            def proj_mm(tt):
                for nb in range(5):
                    for kc in range(8):
                        op("pe", lambda e, nb=nb, kc=kc, tt=tt: e.matmul(
                            PS[nb][:, 0:ncols[nb]], lhsT=aT[:, kc, tt * 128:(tt + 1) * 128],
                            rhs=win[:, kc, nb * 512:nb * 512 + ncols[nb]], start=(kc == 0), stop=(kc == 7)),
                           r=[aT.b[tt], win.b[0]], w=[PB[nb]])

            def proj_evac(tt):
                x_ = xraw[tt % 2]
                for (bk, c0, n, g0) in qk_ranges:
                    op("act", lambda e, bk=bk, c0=c0, n=n, g0=g0: e.activation(
                        out=x_.ap(g0 * 64, [[1, n]]), in_=PS[bk][:, c0:c0 + n], func=AF.Identity),
                       r=[PB[bk]], w=[x_.b[0]])
                op("act", lambda e, tt=tt: e.activation(
                    out=VA.ap(tt * 130, [[65, 2], [1, 64]]), in_=pap(PS[1], 128, [[64, 2], [1, 64]]), func=AF.Identity),
                   r=[PB[1]], w=[VA.b[tt]])
                op("act", lambda e, tt=tt: e.activation(
                    out=VB.ap(tt * 516, [[129, 2], [1, 128]]), in_=pap(PS[3], 256, [[128, 2], [1, 128]]),
                    func=AF.Identity), r=[PB[3]], w=[VB.b[tt]])
                op("act", lambda e, tt=tt: e.activation(
                    out=VB.ap(tt * 516 + 258, [[129, 2], [1, 128]]), in_=pap(PS[4], 0, [[128, 2], [1, 128]]),
                    func=AF.Identity), r=[PB[4]], w=[VB.b[tt]])

            def proj_post(tt):
                s_ = scr
                x_ = xraw[tt % 2]
                q_ = qkt[tt % 2]
                sl = tt % 2
                op("act", lambda e: e.activation(out=s_[:], in_=x_[:], func=AF.Square), r=[x_.b[0]], w=[s_.b[0]])
                ssap = ssb.ap(sl * 32, [[1, 26]])
                op("dve", lambda e, ssap=ssap: e.tensor_reduce(out=ssap, in_=s_[:], axis=AX.X, op=ALU.add),
                   r=[s_.b[0]], w=[ssb.b[sl]])
                rstd_from_ss(ssap, ssap, 64.0, [ssb.b[sl]], [ssb.b[sl]])
                op("dve", lambda e: e.tensor_tensor(out=x_[:], in0=x_[:], in1=ssb.ap(sl * 32, [[1, 26], [0, 64]]),
                                                    op=ALU.mult), r=[x_.b[0], ssb.b[sl]], w=[x_.b[0]])
                for ty, (g0, ng) in enumerate(gtypes):
                    op("dve", lambda e, ty=ty, g0=g0, ng=ng: e.tensor_tensor(
                        out=x_.ap(g0 * 64, [[64, ng], [1, 64]]), in0=x_.ap(g0 * 64, [[64, ng], [1, 64]]),
                        in1=gains.ap(ty * 64, [[0, ng], [1, 64]]), op=ALU.mult),
                       r=[x_.b[0], gains.b[0]], w=[x_.b[0]])
                if tt >= 2:
                    lt = tt - 2
                    cb = cosT.ap(lt * 32, [[0, 26], [0, 2], [1, 32]])
                    sb2 = sinT.ap(lt * 32, [[0, 26], [1, 32]])
                    op("dve", lambda e, sb2=sb2: e.tensor_tensor(
                        out=scr.ap(0, [[32, 26], [1, 32]]), in0=x_.ap(32, [[64, 26], [1, 32]]), in1=sb2, op=ALU.mult),
                       r=[x_.b[0], sinT.b[0]], w=[scr.b[0]])
                    op("dve", lambda e, sb2=sb2: e.tensor_tensor(
                        out=scr.ap(832, [[32, 26], [1, 32]]), in0=x_.ap(0, [[64, 26], [1, 32]]), in1=sb2, op=ALU.mult),
                       r=[x_.b[0], sinT.b[0]], w=[scr.b[0]])
                    op("dve", lambda e, cb=cb: e.tensor_tensor(
                        out=x_.ap(0, [[64, 26], [32, 2], [1, 32]]), in0=x_.ap(0, [[64, 26], [32, 2], [1, 32]]),
                        in1=cb, op=ALU.mult), r=[x_.b[0], cosT.b[0]], w=[x_.b[0]])
                    op("dve", lambda e: e.tensor_tensor(
                        out=q_.ap(0, [[64, 26], [1, 32]]), in0=x_.ap(0, [[64, 26], [1, 32]]),
                        in1=scr.ap(0, [[32, 26], [1, 32]]), op=ALU.subtract), r=[x_.b[0], scr.b[0]], w=[q_.b[0]])
                    op("dve", lambda e: e.tensor_tensor(
                        out=q_.ap(32, [[64, 26], [1, 32]]), in0=x_.ap(32, [[64, 26], [1, 32]]),
                        in1=scr.ap(832, [[32, 26], [1, 32]]), op=ALU.add), r=[x_.b[0], scr.b[0]], w=[q_.b[0]])
                else:
                    op("dve", lambda e: e.tensor_copy(out=q_[:], in_=x_[:]), r=[x_.b[0]], w=[q_.b[0]])
                srcs = [q_.ap(h * 128, [[1, 128]]) for h in range(4)]
                srcs.append(q_.ap(8 * 64, [[1, 128]]))
                for h in range(4):
                    srcs.append(q_.ap((10 + 2 * h) * 64, [[1, 128]]))
                    srcs.append(q_.ap((18 + 2 * h) * 64, [[1, 128]]))
                for s0 in range(0, 13, 4):
                    bank = 5 + ((tt * 4 + s0 // 4) % 3)
                    ns = min(4, 13 - s0)
                    for j in range(ns):
                        op("pe", lambda e, j=j, s0=s0, bank=bank: e.matmul(
                            PS[bank][:, j * 128:(j + 1) * 128], lhsT=srcs[s0 + j], rhs=ident_b[:],
                            start=True, stop=True), r=[q_.b[0], ident_b.b[0]], w=[PB[bank]])
                    op("act", lambda e, s0=s0, ns=ns, bank=bank, tt=tt: e.activation(
                        out=QKT.ap(s0 * NTOK + tt * 128, [[NTOK, ns], [1, 128]]),
                        in_=pap(PS[bank], 0, [[128, ns], [1, 128]]), func=AF.Identity),
                       r=[PB[bank]], w=[QKT.b[tt]])

            proj_mm(0)
            for tt in range(NT):
                proj_evac(tt)
                if tt + 1 < NT:
                    proj_mm(tt + 1)
                proj_post(tt)
            kb.barrier()

            ah = Arena(R_H[0] + VA.end - VA.off, R_H[1])
            PTA = mk("PTA", [128, 2, 5, 512], BF16, nslots=2, arena=ah)
            PTB = mk("PTB", [128, 2, NT, 512], BF16, nslots=2, arena=ah)
            mlo = mk("mlo", [128, 128], BF16, arena=ah)
            mhi = mk("mhi", [128, 128], BF16, arena=ah)
            esink = mk("esink", [128, 8], F32, arena=ah)
            lqk = mk("lqk", [128, 4, 64], F32, arena=ah)
            lam = mk("lam", [128, 8], F32, arena=ah)
            gsub = mk("gsub", [128, 128], F32, arena=ah)
            tmpy = [mk("tmpy%d" % i, [128, 128], F32, arena=ah) for i in range(2)]
            yb = [mk("yb%d" % i, [128, 128], F32, arena=ah) for i in range(2)]
            rr = mk("rr", [128, 8, 8], F32, nslots=8, arena=ah)
            dma("pool", mlo[:], maskd[0, :, :], w=[mlo.b[0]])
            dma("pool", mhi[:], maskd[1, :, :], w=[mhi.b[0]])
            dma("sp", esink[:], sinkd[:, :], w=[esink.b[0]])
            op("act", lambda e: e.activation(out=esink[:], in_=esink[:], func=AF.Exp), r=[esink.b[0]], w=[esink.b[0]])
            dma("sp", lqk[:], lqkd[:, :, :], w=[lqk.b[0]])
            dma("sp", gsub[:], sublnd[:, :], w=[gsub.b[0]])
            lam_init = 0.8 - 0.6 * math.exp(-0.3 * 0)
            op("dve", lambda e: e.tensor_scalar(gsub[:], gsub[:], 1.0 - lam_init, None, ALU.mult),
               r=[gsub.b[0]], w=[gsub.b[0]])
            op("dve", lambda e: e.tensor_tensor(out=tmpy[0].ap(0, [[64, 2], [1, 64]]), in0=lqk.ap(0, [[128, 2], [1, 64]]),
                                                in1=lqk.ap(64, [[128, 2], [1, 64]]), op=ALU.mult),
               r=[lqk.b[0]], w=[tmpy[0].b[0]])
            op("dve", lambda e: e.tensor_reduce(out=lam[:, 0:2], in_=tmpy[0].ap(0, [[64, 2], [1, 64]]), axis=AX.X,
                                                op=ALU.add), r=[tmpy[0].b[0]], w=[lam.b[0]])
            op("act", lambda e: e.activation(out=lam[:, 0:2], in_=lam[:, 0:2], func=AF.Exp), r=[lam.b[0]], w=[lam.b[0]])
            op("dve", lambda e: e.scalar_tensor_tensor(out=lam[:, 2:3], in0=lam[:, 1:2], scalar=-lam_init,
                                                       in1=lam[:, 0:1], op0=ALU.add, op1=ALU.subtract),
               r=[lam.b[0]], w=[lam.b[0]])

            sbank = [0]

            def next_sbank():
                b = sbank[0] % 5
                sbank[0] += 1
                return b

            itA = 0
            for kvh in range(2):
                p0 = 64 * kvh
                for qb in range(NT):
                    if qb < 2:
                        keys = [(0, None), (1, None)]
                    else:
                        keys = [(0, None), (1, None)]
                        if qb - 1 >= 2:
                            keys.append((qb - 1, mlo))
                        keys.append((qb, None))
                        if qb + 1 < NT:
                            keys.append((qb + 1, mhi))
                    sl = itA % 2
                    itA += 1
                    for j, (kt, msk) in enumerate(keys):
                        bk = next_sbank()
                        op("pe", lambda e, bk=bk, kt=kt, qb=qb, p0=p0: e.matmul(
                            PS[bk][:, :], lhsT=QKT.ap(4 * NTOK + kt * 128, [[1, 128]], p0=p0, np_=64),
                            rhs=QKT.ap(qb * 128, [[NTOK, 4], [1, 128]], p0=p0, np_=64), start=True, stop=True),
                           r=[QKT.b[kt], QKT.b[qb]], w=[PB[bk]])
                        pt_ap = PTA.ap((sl * 5 + j) * 512, [[1, 512]])
                        op("act", lambda e, bk=bk, pt_ap=pt_ap: e.activation(
                            out=pt_ap, in_=PS[bk][:, :], func=AF.Exp, scale=0.125), r=[PB[bk]], w=[PTA.b[sl]])
                        if msk is not None:
                            op("dve", lambda e, sl=sl, j=j, msk=msk: e.tensor_tensor(
                                out=PTA.ap((sl * 5 + j) * 512, [[128, 4], [1, 128]]),
                                in0=PTA.ap((sl * 5 + j) * 512, [[128, 4], [1, 128]]),
                                in1=msk.ap(0, [[0, 4], [1, 128]]), op=ALU.mult),
                               r=[PTA.b[sl], msk.b[0]], w=[PTA.b[sl]])
                    ob = 5 + (itA % 2)
                    for hq in range(4):
                        for j, (kt, msk) in enumerate(keys):
                            op("pe", lambda e, hq=hq, j=j, kt=kt, sl=sl, ob=ob, kvh=kvh: e.matmul(
                                PS[ob][:, hq * 65:(hq + 1) * 65],
                                lhsT=PTA.ap((sl * 5 + j) * 512 + hq * 128, [[1, 128]]),
                                rhs=VA.ap(kt * 130 + kvh * 65, [[1, 65]]),
                                start=(j == 0), stop=(j == len(keys) - 1)),
                               r=[PTA.b[sl], VA.b[kt]], w=[PB[ob]])
                    ri = stat_slot()
                    den = rr.ap(ri * 8, [[1, 4]])
                    op("dve", lambda e, ob=ob, den=den, kvh=kvh: e.tensor_tensor(
                        out=den, in0=pap(PS[ob], 64, [[65, 4]]), in1=esink[:, kvh * 4:kvh * 4 + 4], op=ALU.add),
                       r=[PB[ob], esink.b[0]], w=[rr.b[ri]])
                    op("dve", lambda e, den=den: e.reciprocal(den, den), r=[rr.b[ri]], w=[rr.b[ri]])
                    op("dve", lambda e, ob=ob, ri=ri, qb=qb, kvh=kvh: e.tensor_tensor(
                        out=y.ap(qb * 1024 + kvh * 256, [[64, 4], [1, 64]]), in0=pap(PS[ob], 0, [[65, 4], [1, 64]]),
                        in1=rr.ap(ri * 8, [[1, 4], [0, 64]]), op=ALU.mult),
                       r=[PB[ob], rr.b[ri]], w=[y.b[qb]])

            itB = 0
            groups = [([0, 1], [0, 1])] + [([2 + 4 * g + i for i in range(4)], list(range(NT))) for g in range(4)]
            for h in range(4):
                sq_, sk_ = 5 + 2 * h, 6 + 2 * h
                for (qbs, keys) in groups:
                    nq = len(qbs) * 128
                    q0 = qbs[0] * 128
                    for m in range(2):
                        for j, kt in enumerate(keys):
                            bk = next_sbank()
                            op("pe", lambda e, bk=bk, kt=kt, m=m, nq=nq, q0=q0, sq_=sq_, sk_=sk_: e.matmul(
                                PS[bk][:, 0:nq], lhsT=QKT.ap(sk_ * NTOK + kt * 128, [[1, 128]], p0=64 * m, np_=64),
                                rhs=QKT.ap(sq_ * NTOK + q0, [[1, nq]], p0=64 * m, np_=64), start=True, stop=True),
                               r=[QKT.b[kt]] + [QKT.b[q] for q in qbs], w=[PB[bk]])
                            op("act", lambda e, bk=bk, m=m, j=j, nq=nq: e.activation(
                                out=PTB.ap((m * NT + j) * 512, [[1, nq]]), in_=PS[bk][:, 0:nq], func=AF.Exp,
                                scale=0.125), r=[PB[bk]], w=[PTB.b[m]])
                    for qi, qb in enumerate(qbs):
                        ob = 5 + (itB % 3)
                        itB += 1
                        for m in range(2):
                            for j, kt in enumerate(keys):
                                op("pe", lambda e, ob=ob, m=m, j=j, kt=kt, qi=qi, h=h: e.matmul(
                                    PS[ob][:, m * 129:(m + 1) * 129],
                                    lhsT=PTB.ap((m * NT + j) * 512 + qi * 128, [[1, 128]]),
                                    rhs=VB.ap(kt * 516 + h * 129, [[1, 129]]),
                                    start=(j == 0), stop=(j == len(keys) - 1)),
                                   r=[PTB.b[m], VB.b[kt]], w=[PB[ob]])
                        ri = stat_slot()
                        r12 = rr.ap(ri * 8, [[1, 2]])
                        op("dve", lambda e, ob=ob, r12=r12: e.reciprocal(r12, pap(PS[ob], 128, [[129, 2]])),
                           r=[PB[ob]], w=[rr.b[ri]])
                        op("dve", lambda e, ri=ri: e.tensor_tensor(
                            out=rr.ap(ri * 8 + 1, [[1, 1]]), in0=rr.ap(ri * 8 + 1, [[1, 1]]), in1=lam[:, 2:3],
                            op=ALU.mult), r=[rr.b[ri], lam.b[0]], w=[rr.b[ri]])
                        t_ = tmpy[itB % 2]
                        y_ = yb[itB % 2]
                        op("dve", lambda e, ob=ob, t_=t_, ri=ri: e.tensor_scalar(
                            t_[:], PS[ob][:, 0:128], rr.ap(ri * 8, [[1, 1]]), None, ALU.mult),
                           r=[PB[ob], rr.b[ri]], w=[t_.b[0]])
                        op("dve", lambda e, ob=ob, t_=t_, y_=y_, ri=ri: e.scalar_tensor_tensor(
                            out=y_[:], in0=PS[ob][:, 129:257], scalar=rr.ap(ri * 8 + 1, [[1, 1]]), in1=t_[:],
                            op0=ALU.mult, op1=ALU.add), r=[PB[ob], rr.b[ri], t_.b[0]], w=[y_.b[0]])
                        ssq = rr.ap(ri * 8 + 2, [[1, 1]])
                        op("act", lambda e, y_=y_, ssq=ssq: e.activation(
                            out=junk[:, 0:128], in_=y_[:], func=AF.Square, accum_out=ssq),
                           r=[y_.b[0]], w=[junk.b[0], rr.b[ri]])
                        rstd_from_ss(ssq, ssq, 128.0, [rr.b[ri]], [rr.b[ri]])
                        op("dve", lambda e, y_=y_, ssq=ssq, qb=qb, h=h: e.scalar_tensor_tensor(
                            out=y.ap(qb * 1024 + 512 + h * 128, [[1, 128]]), in0=y_[:], scalar=ssq, in1=gsub[:],
                            op0=ALU.mult, op1=ALU.mult), r=[y_.b[0], rr.b[ri], gsub.b[0]], w=[y.b[qb]])
            kb.barrier()

        def rows_ap(dram, tt, cmajor):
            base = 0 if dram.shape[0] == NTOK else -256
            if tt < 2 or not cmajor:
                r0 = tt * 128 + base
                return dram[r0:r0 + 128, :]
            pt = tt - 2
            ct, t = pt // 8, pt % 8
            r0 = 256 + base + ct * 1024 + t
            return AP(dram.tensor, r0 * 1024, [[8 * 1024, 128], [1, 1024]])

        def phase_outproj(y, hT, wo_dram, resid_dram, tiles, cmajor=False):
            a = Arena(*R_B)
            wo = mk("wo", [128, 8, 1024], BF16, arena=a)
            wog = [mk("wog%d" % i, [128, 8, 1024], BF16, arena=a) for i in range(2)]
            yT = [mk("yT%d" % i, [128, 8, 128], BF16, arena=a) for i in range(2)]
            xs = [mk("ox%d" % i, [128, 1024], F32, arena=a) for i in range(3)]
            for kc in range(8):
                dma("pool", wo[:, kc, :], wo_dram[kc * 128:(kc + 1) * 128, :], w=[wo.b[0]])
            rs = sorted(set(1 if tt < 2 else 0 for tt in tiles))
            for r_ in rs:
                for half in range(2):
                    op("dve", lambda e, r_=r_, half=half: e.tensor_tensor(
                        out=wog[r_].ap(half * 512, [[1024, 8], [1, 512]]), in0=wo.ap(half * 512, [[1024, 8], [1, 512]]),
                        in1=gate_rep.ap((0 * 2 + r_) * 1024 + half * 512, [[0, 8], [1, 512]]), op=ALU.mult),
                       r=[wo.b[0], gate_rep.b[r_]], w=[wog[r_].b[0]])
            for it, tt in enumerate(tiles):
                r_ = 1 if tt < 2 else 0
                x = xs[it % 3]
                dma("sp", x[:], rows_ap(resid_dram, tt, cmajor), w=[x.b[0]])
                yt_ = yT[it % 2]
                for half in range(2):
                    bank = 5 + ((it * 2 + half) % 3)
                    for c4 in range(4):
                        c = half * 4 + c4
                        op("pe", lambda e, c=c, c4=c4, bank=bank, tt=tt: e.matmul(
                            PS[bank][:, c4 * 128:(c4 + 1) * 128], lhsT=y[:, tt, c * 128:(c + 1) * 128], rhs=ident_b[:],
                            start=True, stop=True), r=[y.b[tt], ident_b.b[0]], w=[PB[bank]])
                    eng = "act" if half == 0 else "dve"
                    if eng == "act":
                        op("act", lambda e, half=half, bank=bank: e.activation(
                            out=yt_.ap(half * 512, [[1, 512]]), in_=PS[bank][:, :], func=AF.Identity),
                           r=[PB[bank]], w=[yt_.b[0]])
                    else:
                        op("dve", lambda e, half=half, bank=bank: e.tensor_copy(
                            out=yt_.ap(half * 512, [[1, 512]]), in_=PS[bank][:, :]), r=[PB[bank]], w=[yt_.b[0]])
                for half in range(2):
                    bank = (it * 2 + half) % 4
                    for kc in range(8):
                        op("pe", lambda e, kc=kc, half=half, bank=bank, r_=r_: e.matmul(
                            PS[bank][:, :], lhsT=yt_[:, kc, :], rhs=wog[r_][:, kc, half * 512:(half + 1) * 512],
                            start=(kc == 0), stop=(kc == 7)), r=[yt_.b[0], wog[r_].b[0]], w=[PB[bank]])
                    op("dve", lambda e, half=half, bank=bank, tt=tt: e.tensor_tensor(
                        out=hT[:, tt, half * 512:(half + 1) * 512], in0=PS[bank][:, :],
                        in1=x[:, half * 512:(half + 1) * 512], op=ALU.add), r=[PB[bank], x.b[0]], w=[hT.b[tt]])
            kb.barrier()

        def phase_mlp(li, fT, hT, blocks):
            a = Arena(*R_B)
            w1s = [mk("w1s%d" % i, [128, 8, 512], BF16, arena=a) for i in range(2)]
            w2s = [mk("w2s%d" % i, [128, 4, 1024], BF16, arena=a) for i in range(2)]
            w2g = [[mk("w2g%d_%d" % (i, r_), [128, 4, 1024], BF16, arena=a) for r_ in range(2)] for i in range(2)]
            hid = [mk("hid%d" % i, [128, 4, 512], BF16, arena=a) for i in range(2)]
            rl = [mk("rl%d" % i, [128, 512], F32, arena=a) for i in range(2)]
            rs = sorted(set(b[2] for b in blocks))

            def load_w(hg):
                w1_, w2_ = w1s[hg % 2], w2s[hg % 2]
                for kc in range(8):
                    dma("pool", w1_[:, kc, :], w1d[li, kc * 128:(kc + 1) * 128, hg * 512:(hg + 1) * 512], w=[w1_.b[0]])
                for hc in range(4):
                    r0 = hg * 512 + hc * 128
                    dma("pool", w2_[:, hc, :], w2d[li, r0:r0 + 128, :], w=[w2_.b[0]])
                for r_ in rs:
                    for half in range(2):
                        op("dve", lambda e, r_=r_, half=half, w2_=w2_, hg=hg: e.tensor_tensor(
                            out=w2g[hg % 2][r_].ap(half * 512, [[1024, 4], [1, 512]]),
                            in0=w2_.ap(half * 512, [[1024, 4], [1, 512]]),
                            in1=gate_rep.ap((1 * 2 + r_) * 1024 + half * 512, [[0, 4], [1, 512]]), op=ALU.mult),
                           r=[w2_.b[0], gate_rep.b[2 + r_]], w=[w2g[hg % 2][r_].b[0]])

            ir = [0]

            def emit_hidden(k, hg, t0, ntl, r_):
                ntok = ntl * 128
                hd = hid[k % 2]
                w1_ = w1s[hg % 2]
                for hc in range(4):
                    bank = hc
                    for kc in range(8):
                        op("pe", lambda e, kc=kc, hc=hc, bank=bank, t0=t0, ntok=ntok, w1_=w1_: e.matmul(
                            PS[bank][:, 0:ntok], lhsT=w1_[:, kc, hc * 128:(hc + 1) * 128],
                            rhs=fT[:, kc, t0 * 128:t0 * 128 + ntok], start=(kc == 0), stop=(kc == 7)),
                           r=[w1_.b[0]] + [fT.b[t0 + i] for i in range(ntl)], w=[PB[bank]])
                    rl_ = rl[ir[0] % 2]
                    ir[0] += 1
                    op("act", lambda e, bank=bank, ntok=ntok, rl_=rl_: e.activation(
                        out=rl_[:, 0:ntok], in_=PS[bank][:, 0:ntok], func=AF.Relu), r=[PB[bank]], w=[rl_.b[0]])
                    op("act", lambda e, hc=hc, ntok=ntok, rl_=rl_, hd=hd: e.activation(
                        out=hd[:, hc, 0:ntok], in_=rl_[:, 0:ntok], func=AF.Square),
                       r=[rl_.b[0]], w=[hd.b[0]])

            def emit_out(k, hg, t0, ntl, r_):
                hd = hid[k % 2]
                for i in range(ntl):
                    tt = t0 + i
                    for half in range(2):
                        bank = 4 + ((i * 2 + half) % 4)
                        for hc in range(4):
                            op("pe", lambda e, hc=hc, half=half, bank=bank, i=i, hd=hd, r_=r_, hg=hg: e.matmul(
                                PS[bank][:, :], lhsT=hd[:, hc, i * 128:(i + 1) * 128],
                                rhs=w2g[hg % 2][r_][:, hc, half * 512:(half + 1) * 512],
                                start=(hc == 0), stop=(hc == 3)), r=[hd.b[0], w2g[hg % 2][r_].b[0]], w=[PB[bank]])
                        op("dve", lambda e, half=half, bank=bank, tt=tt: e.tensor_tensor(
                            out=hT[:, tt, half * 512:(half + 1) * 512], in0=PS[bank][:, :],
                            in1=hT[:, tt, half * 512:(half + 1) * 512], op=ALU.add),
                           r=[PB[bank], hT.b[tt]], w=[hT.b[tt]])

            work = [(hg,) + tuple(b) for hg in range(8) for b in blocks]
            load_w(0)
            loaded = 0
            emit_hidden(0, *work[0])
            for k in range(len(work)):
                if k + 1 < len(work):
                    nhg = work[k + 1][0]
                    if nhg > loaded:
                        load_w(nhg)
                        loaded = nhg
                    emit_hidden(k + 1, *work[k + 1])
                emit_out(k, *work[k])
            kb.barrier()

        def phase_s5(aT, gc):
            INV2PI = 1.0 / (2.0 * math.pi)
            NCH = 288
            sa = Arena(R_H[1] - 4096, R_H[1])
            lamn = mk("lamn", [128, 2, 3, 32], F32, arena=sa)
            ktab = mk("ktab", [128, 6, 8], F32, arena=sa)
            rhoe = mk("rhoe", [128, 2, 32], F32, arena=sa)
            omg = mk("omg", [128, 2, 32], F32, arena=sa)
            ff = mk("ff", [128, 2, 2, 32], F32, arena=sa)
            A12 = mk("A12", [128, 2, 2, 64], F32, arena=sa)
            tsm = mk("tsm", [128, 4, 32], F32, arena=sa)
            for d_ in range(2):
                for w_ in range(3):
                    dma("sp", lamn[:, d_, w_, :], lamnd[d_, w_, :, :], w=[lamn.b[0]])
            dma("sp", ktab[:], ktabd[:, :, :], w=[ktab.b[0]])
            op("act", lambda e: e.activation(out=lamn.ap(64, [[96, 2], [1, 32]]), in_=lamn.ap(64, [[96, 2], [1, 32]]),
                                             func=AF.Exp), r=[lamn.b[0]], w=[lamn.b[0]])
            op("dve", lambda e: e.tensor_tensor(out=rhoe[:], in0=lamn.ap(0, [[96, 2], [1, 32]]),
                                                in1=lamn.ap(64, [[96, 2], [1, 32]]), op=ALU.mult),
               r=[lamn.b[0]], w=[rhoe.b[0]])
            op("dve", lambda e: e.scalar_tensor_tensor(out=omg[:], in0=lamn.ap(32, [[96, 2], [1, 32]]), scalar=INV2PI,
                                                       in1=lamn.ap(64, [[96, 2], [1, 32]]), op0=ALU.mult, op1=ALU.mult),
               r=[lamn.b[0]], w=[omg.b[0]])

            def mk_scr(arena, name):
                return (mk(name + "x", [128, 8, 32], F32, arena=arena), mk(name + "xi", [128, 8, 32], I32, arena=arena),
                        mk(name + "t1", [128, 8, 32], F32, arena=arena), mk(name + "t2", [128, 8, 32], F32, arena=arena))

            def cpow(arena, d_, krow, name, scr):
                re_ = mk(name + "re", [128, 8, 32], F32, arena=arena)
                im_ = mk(name + "im", [128, 8, 32], F32, arena=arena)
                x_, xi_ = scr[0], scr[1]
                kap = ktab.ap(krow * 8, [[1, 8], [0, 32]])
                op("dve", lambda e: e.tensor_tensor(out=re_[:], in0=rhoe.ap(d_ * 32, [[0, 8], [1, 32]]), in1=kap,
                                                    op=ALU.mult), r=[rhoe.b[0], ktab.b[0]], w=[re_.b[0]])
                op("act", lambda e: e.activation(out=re_[:], in_=re_[:], func=AF.Exp), r=[re_.b[0]], w=[re_.b[0]])
                op("dve", lambda e: e.tensor_tensor(out=x_[:], in0=omg.ap(d_ * 32, [[0, 8], [1, 32]]), in1=kap,
                                                    op=ALU.mult), r=[omg.b[0], ktab.b[0]], w=[x_.b[0]])
                op("dve", lambda e: e.tensor_copy(out=xi_[:], in_=x_[:]), r=[x_.b[0]], w=[xi_.b[0]])
                op("dve", lambda e: e.tensor_tensor(out=x_[:], in0=x_[:], in1=xi_[:], op=ALU.subtract),
                   r=[x_.b[0], xi_.b[0]], w=[x_.b[0]])
                op("act", lambda e: e.activation(out=im_[:], in_=x_[:], func=AF.Sin, scale=2.0 * math.pi),
                   r=[x_.b[0]], w=[im_.b[0]])
                op("act", lambda e: e.activation(out=x_[:], in_=x_[:], func=AF.Sin, scale=math.pi),
                   r=[x_.b[0]], w=[x_.b[0]])
                op("act", lambda e: e.activation(out=x_[:], in_=x_[:], func=AF.Square), r=[x_.b[0]], w=[x_.b[0]])
                op("dve", lambda e: e.tensor_scalar(x_[:], x_[:], -2.0, 1.0, ALU.mult, ALU.add), r=[x_.b[0]], w=[x_.b[0]])
                op("dve", lambda e: e.tensor_tensor(out=im_[:], in0=im_[:], in1=re_[:], op=ALU.mult),
                   r=[im_.b[0], re_.b[0]], w=[im_.b[0]])
                op("dve", lambda e: e.tensor_tensor(out=re_[:], in0=re_[:], in1=x_[:], op=ALU.mult),
                   r=[re_.b[0], x_.b[0]], w=[re_.b[0]])
                return re_, im_

            def cmul_f(d_, re_, im_, scr):
                t1, t2 = scr[2], scr[3]
                fr = ff.ap((d_ * 2 + 0) * 32, [[0, 8], [1, 32]])
                fi = ff.ap((d_ * 2 + 1) * 32, [[0, 8], [1, 32]])
                op("dve", lambda e: e.tensor_tensor(out=t1[:], in0=re_[:], in1=fi, op=ALU.mult),
                   r=[re_.b[0], ff.b[0]], w=[t1.b[0]])
                op("dve", lambda e: e.tensor_tensor(out=t2[:], in0=im_[:], in1=fi, op=ALU.mult),
                   r=[im_.b[0], ff.b[0]], w=[t2.b[0]])
                op("dve", lambda e: e.tensor_tensor(out=re_[:], in0=re_[:], in1=fr, op=ALU.mult),
                   r=[re_.b[0], ff.b[0]], w=[re_.b[0]])
                op("dve", lambda e: e.tensor_tensor(out=im_[:], in0=im_[:], in1=fr, op=ALU.mult),
                   r=[im_.b[0], ff.b[0]], w=[im_.b[0]])
                op("dve", lambda e: e.tensor_tensor(out=re_[:], in0=re_[:], in1=t2[:], op=ALU.subtract),
                   r=[re_.b[0], t2.b[0]], w=[re_.b[0]])
                op("dve", lambda e: e.tensor_tensor(out=im_[:], in0=im_[:], in1=t1[:], op=ALU.add),
                   r=[im_.b[0], t1.b[0]], w=[im_.b[0]])

            U = T(kb, "U", [128, 64, NCH], BF16, R_H[0], nslots=64)
            WS = [T(kb, "WS%d" % d_, [128, 64, 2, 64], BF16, U.end + d_ * 16384, nslots=1) for d_ in range(2)]
            assert WS[1].end <= R_H[1] - 4096
            a = Arena(*R_B)
            wsi = mk("wsi", [128, 8, 1024], BF16, arena=a)
            Xc = [mk("Xc%d" % i, [128, 64, 8, 16], BF16, arena=a) for i in range(2)]
            scrU = mk_scr(a, "scrU")
            pw0 = [cpow(a, d_, 0, "pw0_%d" % d_, scrU) for d_ in range(2)]
            for d_ in range(2):
                re_, im_ = pw0[d_]
                lre = lamn[:, d_, 0, :]
                lim = lamn[:, d_, 1, :]
                a1r = re_[:, 0, :]
                a1i = im_[:, 0, :]
                nr, den, tA_, tB_ = tsm[:, 0, :], tsm[:, 1, :], tsm[:, 2, :], tsm[:, 3, :]
                rb = [re_.b[0], im_.b[0], lamn.b[0], tsm.b[0]]
                op("dve", lambda e, nr=nr, a1r=a1r: e.tensor_scalar(nr, a1r, -1.0, None, ALU.add), r=rb, w=[tsm.b[0]])
                op("dve", lambda e, tA_=tA_, lre=lre: e.tensor_tensor(out=tA_, in0=lre, in1=lre, op=ALU.mult),
                   r=rb, w=[tsm.b[0]])
                op("dve", lambda e, den=den, lim=lim: e.tensor_tensor(out=den, in0=lim, in1=lim, op=ALU.mult),
                   r=rb, w=[tsm.b[0]])
                op("dve", lambda e, den=den, tA_=tA_: e.tensor_tensor(out=den, in0=den, in1=tA_, op=ALU.add),
                   r=rb, w=[tsm.b[0]])
                op("dve", lambda e, den=den: e.reciprocal(den, den), r=rb, w=[tsm.b[0]])
                fr_ = ff[:, d_, 0, :]
                fi_ = ff[:, d_, 1, :]
                op("dve", lambda e, tA_=tA_, nr=nr, lre=lre: e.tensor_tensor(out=tA_, in0=nr, in1=lre, op=ALU.mult),
                   r=rb, w=[tsm.b[0]])
                op("dve", lambda e, tB_=tB_, a1i=a1i, lim=lim: e.tensor_tensor(out=tB_, in0=a1i, in1=lim, op=ALU.mult),
                   r=rb, w=[tsm.b[0]])
                op("dve", lambda e, tA_=tA_, tB_=tB_: e.tensor_tensor(out=tA_, in0=tA_, in1=tB_, op=ALU.add),
                   r=rb, w=[tsm.b[0]])
                op("dve", lambda e, fr_=fr_, tA_=tA_, den=den: e.tensor_tensor(out=fr_, in0=tA_, in1=den, op=ALU.mult),
                   r=rb, w=[ff.b[0]])
                op("dve", lambda e, tA_=tA_, a1i=a1i, lre=lre: e.tensor_tensor(out=tA_, in0=a1i, in1=lre, op=ALU.mult),
                   r=rb, w=[tsm.b[0]])
                op("dve", lambda e, tB_=tB_, nr=nr, lim=lim: e.tensor_tensor(out=tB_, in0=nr, in1=lim, op=ALU.mult),
                   r=rb, w=[tsm.b[0]])
                op("dve", lambda e, tA_=tA_, tB_=tB_: e.tensor_tensor(out=tA_, in0=tA_, in1=tB_, op=ALU.subtract),
                   r=rb, w=[tsm.b[0]])
                op("dve", lambda e, fi_=fi_, tA_=tA_, den=den: e.tensor_tensor(out=fi_, in0=tA_, in1=den, op=ALU.mult),
                   r=rb, w=[ff.b[0]])
                a8r = re_[:, 7, :]
                a8i = im_[:, 7, :]
                for hh in range(2):
                    op("dve", lambda e, hh=hh, a8r=a8r: e.tensor_copy(out=A12[:, d_, 0, hh * 32:(hh + 1) * 32], in_=a8r),
                       r=rb, w=[A12.b[0]])
                op("dve", lambda e, a8i=a8i: e.tensor_scalar(A12[:, d_, 1, 0:32], a8i, -1.0, None, ALU.mult),
                   r=rb, w=[A12.b[0]])
                op("dve", lambda e, a8i=a8i: e.tensor_copy(out=A12[:, d_, 1, 32:64], in_=a8i), r=rb, w=[A12.b[0]])
            ET = [mk("ET%d" % i, [128, 128], F32, arena=a) for i in range(2)]
            ie = 0
            for d_ in range(2):
                ere, eim = cpow(a, d_, 4 + d_, "E%d" % d_, scrU)
                cmul_f(d_, ere, eim, scrU)
                for part, src in enumerate((ere, eim)):
                    for b2 in range(2):
                        bank = 6 + (ie % 2)
                        et = ET[ie % 2]
                        ie += 1
                        op("pe", lambda e, src=src, b2=b2, bank=bank: e.transpose(
                            PS[bank][:, 0:128], src.ap(b2 * 128, [[1, 128]]), ident_f[:]),
                           r=[src.b[0], ident_f.b[0]], w=[PB[bank]])
                        op("act", lambda e, et=et, bank=bank: e.activation(out=et[:], in_=PS[bank][:, 0:128],
                                                                            func=AF.Identity), r=[PB[bank]], w=[et.b[0]])
                        for k4 in range(4):
                            s_ = b2 * 4 + k4
                            off = ((d_ * 2 + part) * 8 + s_) * 4096
                            dma("sp", AP(escr.tensor, off, [[64, 32], [2048, 2], [1, 64]]),
                                et.ap(0, [[64, 2], [1, 64]], p0=k4 * 32, np_=32), r=[et.b[0]])
            esc_b = Buf()
            for kc in range(8):
                dma("pool", wsi[:, kc, :], s5wind[kc * 128:(kc + 1) * 128, :], w=[wsi.b[0]])
            for ct in range(3):
                nm = 128 if ct < 2 else 32
                xc = Xc[ct % 2]
                for s_ in range(8):
                    for half in range(2):
                        bank = (s_ * 2 + half) % 4
                        for kc in range(8):
                            op("pe", lambda e, kc=kc, half=half, bank=bank, ct=ct, s_=s_, nm=nm: e.matmul(
                                PS[bank][0:nm, :], lhsT=aT.ap(kc * NTOK + ct * 1024 + s_, [[8, nm]]),
                                rhs=wsi[:, kc, half * 512:(half + 1) * 512], start=(kc == 0), stop=(kc == 7)),
                               r=aT.b + [wsi.b[0]], w=[PB[bank]])
                        eng = "act" if half == 0 else "dve"
                        dst = xc.ap(half * 32 * 128 + s_ * 16, [[128, 32], [1, 16]], np_=nm)
                        srcp = pap(PS[bank], 0, [[16, 32], [1, 16]], np_=nm)
                        if eng == "act":
                            op("act", lambda e, dst=dst, srcp=srcp: e.activation(out=dst, in_=srcp, func=AF.Identity),
                               r=[PB[bank]], w=[xc.b[0]])
                        else:
                            op("dve", lambda e, dst=dst, srcp=srcp: e.tensor_copy(out=dst, in_=srcp),
                               r=[PB[bank]], w=[xc.b[0]])
                for g4 in range(16):
                    bank = 4 + (g4 % 2)
                    for gi in range(4):
                        g = g4 * 4 + gi
                        op("pe", lambda e, g=g, gi=gi, bank=bank, xc=xc, nm=nm: e.matmul(
                            PS[bank][:, gi * 128:gi * 128 + nm], lhsT=xc.ap(g * 128, [[1, 128]], np_=nm),
                            rhs=ident_b[0:nm, 0:nm], start=True, stop=True),
                           r=[xc.b[0], ident_b.b[0]], w=[PB[bank]])
                    dstu = U.ap(g4 * 4 * NCH + ct * 128, [[NCH, 4], [1, nm]])
                    srcu = pap(PS[bank], 0, [[128, 4], [1, nm]])
                    if g4 % 2 == 0:
                        op("act", lambda e, dstu=dstu, srcu=srcu: e.activation(out=dstu, in_=srcu, func=AF.Identity),
                           r=[PB[bank]], w=U.b[g4 * 4:g4 * 4 + 4])
                    else:
                        op("dve", lambda e, dstu=dstu, srcu=srcu: e.tensor_copy(out=dstu, in_=srcu),
                           r=[PB[bank]], w=U.b[g4 * 4:g4 * 4 + 4])
            kb.barrier()

            a = Arena(*R_B)
            NH = 1024
            tabs = [[mk("ws_%d_%d" % (d_, i), [128, NH], F32, arena=a) for i in range(4)] for d_ in range(2)]
            tmp = [[mk("wt_%d_%d" % (d_, i), [128, NH], F32, arena=a) for i in range(1)] for d_ in range(2)]
            for hf in range(4):
                for d_ in range(2):
                    eng = "dve"
                    ere, eim, bre, bim = tabs[d_]
                    t1 = tmp[d_][0]
                    for part, et in enumerate((ere, eim)):
                        for s_ in range(8):
                            off = ((d_ * 2 + part) * 8 + s_) * 4096 + hf * NH
                            dma("sp", et.ap(0, [[1, NH]], p0=s_ * 16, np_=16),
                                AP(escr.tensor, off, [[0, 16], [1, NH]]), w=[et.b[0]])
                    dma("sp", bre[:], bsjd[d_, 0, :, hf * NH:(hf + 1) * NH], w=[bre.b[0]])
                    dma("sp", bim[:], bsjd[d_, 1, :, hf * NH:(hf + 1) * NH], w=[bim.b[0]])
                    wsd = WS[d_]
                    o_re = wsd.ap(hf * 16 * 128, [[128, 16], [1, 64]])
                    o_im = wsd.ap(hf * 16 * 128 + 64, [[128, 16], [1, 64]])
                    v = lambda t_: t_.ap(0, [[64, 16], [1, 64]])
                    op(eng, lambda e, t1=t1, ere=ere, bre=bre: e.tensor_tensor(out=t1[:], in0=ere[:], in1=bre[:],
                                                                               op=ALU.mult),
                       r=[ere.b[0], bre.b[0]], w=[t1.b[0]])
                    op(eng, lambda e, bre=bre, eim=eim: e.tensor_tensor(out=bre[:], in0=eim[:], in1=bre[:], op=ALU.mult),
                       r=[eim.b[0], bre.b[0]], w=[bre.b[0]])
                    op(eng, lambda e, ere=ere, bim=bim: e.tensor_tensor(out=ere[:], in0=ere[:], in1=bim[:], op=ALU.mult),
                       r=[ere.b[0], bim.b[0]], w=[ere.b[0]])
                    op(eng, lambda e, eim=eim, bim=bim: e.tensor_tensor(out=eim[:], in0=eim[:], in1=bim[:], op=ALU.mult),
                       r=[eim.b[0], bim.b[0]], w=[eim.b[0]])
                    op(eng, lambda e, t1=t1, eim=eim, o_re=o_re, v=v: e.tensor_tensor(out=o_re, in0=v(t1), in1=v(eim),
                                                                                       op=ALU.subtract),
                       r=[t1.b[0], eim.b[0]], w=[wsd.b[0]])
                    op(eng, lambda e, ere=ere, bre=bre, o_im=o_im, v=v: e.tensor_tensor(out=o_im, in0=v(ere), in1=v(bre),
                                                                                        op=ALU.add),
                       r=[ere.b[0], bre.b[0]], w=[wsd.b[0]])
            kb.barrier()

            a = Arena(*R_B)
            HSt = mk("HS", [128, 2, 64, 256], BF16, arena=a)
            HS = [None, None]
            ra = Arena(*R_A)
            Swt = mk("Sw", [128, 4, 64, 32], F32, nslots=4, arena=ra)
            Zt = mk("Z", [128, 2, 96], F32, arena=ra)
            T1t = mk("T1", [128, 2, 64], F32, arena=ra)
            T2t = mk("T2", [128, 2, 64], F32, arena=ra)
            op("dve", lambda e: e.memset(Zt[:], 0.0), w=[Zt.b[0]])
            worder = [list(range(9)), [0, 8, 7, 6, 5, 4, 3, 2, 1]]

            def s_window(d_, w_, slot):
                c0 = w_ * 32
                for idx in range(64):
                    part, gl = idx // 32, idx % 32
                    bank = d_ * 4 + idx // 16
                    col = (idx % 16) * 32
                    for gh in range(2):
                        g = gh * 32 + gl
                        op("pe", lambda e, bank=bank, col=col, gh=gh, g=g, part=part, d_=d_, c0=c0: e.matmul(
                            pap(PS[bank], col, [[1, 32]], p0=gh * 64, np_=64),
                            lhsT=WS[d_].ap(g * 128 + part * 64, [[1, 64]]), rhs=U.ap(g * NCH + c0, [[1, 32]]),
                            start=True, stop=True), r=[WS[d_].b[0], U.b[g]], w=[PB[bank]])
                for b4 in range(4):
                    bank = d_ * 4 + b4
                    op("act", lambda e, bank=bank, b4=b4, slot=slot: e.activation(
                        out=Swt.ap(slot * 2048 + b4 * 512, [[1, 512]]), in_=PS[bank][:, :], func=AF.Identity),
                       r=[PB[bank]], w=[Swt.b[slot]])

            for d_ in range(2):
                s_window(d_, worder[d_][0], d_ * 2 + 0)
            a1v = A12.ap(0, [[128, 2], [1, 64]])
            a2v = A12.ap(64, [[128, 2], [1, 64]])
            for wi in range(9):
                if wi + 1 < 9:
                    for d_ in range(2):
                        s_window(d_, worder[d_][wi + 1], d_ * 2 + (wi + 1) % 2)
                buf = wi % 2
                sbufs = [Swt.b[0 * 2 + buf], Swt.b[1 * 2 + buf]]
                for ci in range(32):
                    off_f = (0 * 2 + buf) * 2048 + ci
                    off_b = (1 * 2 + buf) * 2048 + 31 - ci
                    dstr = off_b - off_f
                    sc = Swt.ap(off_f, [[dstr, 2], [32, 64]])
                    sc_lo = Swt.ap(off_f, [[dstr, 2], [32, 32]])
                    op("dve", lambda e: e.tensor_tensor(out=T1t[:], in0=Zt.ap(0, [[96, 2], [1, 64]]), in1=a1v,
                                                        op=ALU.mult), r=[Zt.b[0], A12.b[0]], w=[T1t.b[0]])
                    op("dve", lambda e: e.tensor_tensor(out=T2t[:], in0=Zt.ap(32, [[96, 2], [1, 64]]), in1=a2v,
                                                        op=ALU.mult), r=[Zt.b[0], A12.b[0]], w=[T2t.b[0]])
                    op("dve", lambda e: e.tensor_tensor(out=T1t[:], in0=T1t[:], in1=T2t[:], op=ALU.add),
                       r=[T1t.b[0], T2t.b[0]], w=[T1t.b[0]])
                    op("dve", lambda e, sc=sc: e.tensor_tensor(out=Zt.ap(0, [[96, 2], [1, 64]]), in0=T1t[:], in1=sc,
                                                               op=ALU.add),
                       r=[T1t.b[0]] + sbufs, w=[Zt.b[0]])
                    op("dve", lambda e, sc_lo=sc_lo: e.tensor_tensor(
                        out=Zt.ap(64, [[96, 2], [1, 32]]), in0=T1t.ap(0, [[64, 2], [1, 32]]), in1=sc_lo, op=ALU.add),
                       r=[T1t.b[0]] + sbufs, w=[Zt.b[0]])
                    cf = worder[0][wi] * 32 + ci
                    cb = worder[1][wi] * 32 + 31 - ci
                    if wi == 0:
                        store = (ci == 31)
                        col_f, col_b = 0, 255
                    else:
                        store = not (wi == 8 and ci == 31)
                        col_f, col_b = cf + 1 - 32, cb - 1 - 32
                    if store:
                        op("act", lambda e, col_f=col_f, col_b=col_b: e.activation(
                            out=HSt.ap(col_f, [[64 * 256 + col_b - col_f, 2], [256, 64]]),
                            in_=Zt.ap(0, [[96, 2], [1, 64]]), func=AF.Identity), r=[Zt.b[0]], w=[HSt.b[0]])
            kb.barrier()

            NB = 4
            ha = Arena(U.end, R_H[1] - 4096)
            pwa = Arena(ha.take(8 * 1024), ha.top)
            pwa = Arena(pwa.lo, pwa.lo + 8 * 1024)
            sca = Arena(ha.top, R_H[1] - 4096)
            scrY = mk_scr(sca, "scrY")
            pwy = [cpow(pwa, d_, 0 + d_, "pwy%d" % d_, scrY) for d_ in range(2)]
            pwp = [cpow(pwa, d_, 2 + d_, "pwp%d" % d_, scrY) for d_ in range(2)]
            for d_ in range(2):
                cmul_f(d_, pwp[d_][0], pwp[d_][1], scrY)
            kb.barrier()
            sca = Arena(sca.lo, R_H[1] - 4096)
            cn = [[mk("cn%d_%d" % (d_, p_), [128, NB, 16], F32, arena=sca) for p_ in range(2)] for d_ in range(2)]
            bn = [[mk("bn%d_%d" % (d_, p_), [128, NB, 16], F32, arena=sca) for p_ in range(2)] for d_ in range(2)]
            WYb = [mk("WY%d" % d_, [128, 2, NB, 128], BF16, arena=sca) for d_ in range(2)]
            Pb = [mk("P%d" % d_, [128, 2, NB, 128], BF16, arena=sca) for d_ in range(2)]
            WT = mk("WT", [128, 2 * NB, 128], BF16, arena=sca)
            g1t = [mk("g1t%d" % d_, [128, NB * 128], F32, arena=sca) for d_ in range(2)]
            g2t = [mk("g2t%d" % d_, [128, NB * 128], F32, arena=sca) for d_ in range(2)]
            ba = Arena(a.top, R_B[1])
            m3 = mk("m3", [128, 3, 128], F32, arena=ba)
            dsk = mk("dsk", [128, 1024], F32, arena=ba)
            wtt = [mk("wtt%d" % i, [128, 512], F32, arena=ba) for i in range(1)]
            Yg = [mk("Yg%d" % i, [128, 256], F32, arena=ba) for i in range(2)]
            gx = [mk("gx%d" % i, [128, 512], F32, arena=ba) for i in range(2)]
            for i in range(3):
                dma("sp", m3[:, i, :], m3d[i, :, :], w=[m3.b[0]])
            dma("sp", dsk[:], dskd[:, :], w=[dsk.b[0]])
            C_G = 2.0 * math.sqrt(2.0 / math.pi)
            iy = 0
            for bt in range(32 // NB):
                gl0 = bt * NB
                for d_ in range(2):
                    for p_ in range(2):
                        dma("sp", cn[d_][p_][:], cnd[d_, p_, :, gl0:gl0 + NB, :], w=[cn[d_][p_].b[0]])
                        dma("sp", bn[d_][p_][:], bnd[d_, p_, :, gl0:gl0 + NB, :], w=[bn[d_][p_].b[0]])
                for d_ in range(2):
                    eng = "dve"
                    t1, t2 = g1t[d_], g2t[d_]
                    v4 = lambda t_: t_.ap(0, [[128, NB], [16, 8], [1, 16]])
                    for (dst, coef, pw_, is_wy) in ((WYb[d_], cn[d_], pwy[d_], True), (Pb[d_], bn[d_], pwp[d_], False)):
                        cre = coef[0].ap(0, [[16, NB], [0, 8], [1, 16]])
                        cim = coef[1].ap(0, [[16, NB], [0, 8], [1, 16]])
                        pre = pw_[0].ap(gl0, [[1, NB], [32, 8], [0, 16]])
                        pim = pw_[1].ap(gl0, [[1, NB], [32, 8], [0, 16]])
                        rr_ = [coef[0].b[0], coef[1].b[0], pw_[0].b[0], pw_[1].b[0]]
                        o0 = dst.ap(0, [[128, NB], [16, 8], [1, 16]])
                        o1 = dst.ap(NB * 128, [[128, NB], [16, 8], [1, 16]])
                        op(eng, lambda e, t1=t1, cre=cre, pre=pre, v4=v4: e.tensor_tensor(out=v4(t1), in0=cre, in1=pre,
                                                                                          op=ALU.mult),
                           r=rr_, w=[t1.b[0]])
                        op(eng, lambda e, t2=t2, cim=cim, pim=pim, v4=v4: e.tensor_tensor(out=v4(t2), in0=cim, in1=pim,
                                                                                          op=ALU.mult),
                           r=rr_, w=[t2.b[0]])
                        op(eng, lambda e, t1=t1, t2=t2, o0=o0, v4=v4: e.tensor_tensor(out=o0, in0=v4(t1), in1=v4(t2),
                                                                                       op=ALU.subtract),
                           r=[t1.b[0], t2.b[0]], w=[dst.b[0]])
                        op(eng, lambda e, t1=t1, cre=cre, pim=pim, v4=v4: e.tensor_tensor(out=v4(t1), in0=cre, in1=pim,
                                                                                          op=ALU.mult),
                           r=rr_ + [dst.b[0]], w=[t1.b[0]])
                        op(eng, lambda e, t2=t2, cim=cim, pre=pre, v4=v4: e.tensor_tensor(out=v4(t2), in0=cim, in1=pre,
                                                                                          op=ALU.mult),
                           r=rr_ + [dst.b[0]], w=[t2.b[0]])
                        if is_wy:
                            op(eng, lambda e, t1=t1, t2=t2: e.tensor_tensor(out=t1[:], in0=t1[:], in1=t2[:], op=ALU.add),
                               r=[t1.b[0], t2.b[0]], w=[t1.b[0]])
                            op(eng, lambda e, t1=t1, o1=o1, v4=v4: e.tensor_scalar(o1, v4(t1), -1.0, None, ALU.mult),
                               r=[t1.b[0]], w=[dst.b[0]])
                        else:
                            op(eng, lambda e, t1=t1, t2=t2, o1=o1, v4=v4: e.tensor_tensor(out=o1, in0=v4(t1),
                                                                                           in1=v4(t2), op=ALU.add),
                               r=[t1.b[0], t2.b[0]], w=[dst.b[0]])
                for gh in range(2):
                    bf_, bb_ = 0 + gh * 2, 1 + gh * 2
                    for gll in range(NB):
                        for d_, bank in ((0, bf_), (1, bb_)):
                            for part in range(2):
                                op("pe", lambda e, gh=gh, gll=gll, d_=d_, bank=bank, part=part: e.matmul(
                                    PS[bank][:, gll * 128:(gll + 1) * 128],
                                    lhsT=Pb[d_].ap((part * NB + gll) * 128, [[1, 128]], p0=gh * 64, np_=64),
                                    rhs=WYb[d_].ap((part * NB + gll) * 128, [[1, 128]], p0=gh * 64, np_=64),
                                    start=(part == 0), stop=(part == 1)),
                                   r=[Pb[d_].b[0], WYb[d_].b[0]], w=[PB[bank]])
                    wt_ = wtt[0]
                    gbase = gh * 32 + gl0
                    mf = m3.ap(0, [[0, NB], [1, 128]])
                    mb = m3.ap(128, [[0, NB], [1, 128]])
                    op("dve", lambda e, wt_=wt_, bf_=bf_, mf=mf: e.tensor_tensor(
                        out=wt_.ap(0, [[128, NB], [1, 128]]), in0=pap(PS[bf_], 0, [[128, NB], [1, 128]]), in1=mf,
                        op=ALU.mult), r=[PB[bf_], m3.b[0]], w=[wt_.b[0]])
                    op("dve", lambda e, bb_=bb_, mb=mb: e.tensor_tensor(
                        out=gx[0].ap(0, [[128, NB], [1, 128]]), in0=pap(PS[bb_], 0, [[128, NB], [1, 128]]), in1=mb,
                        op=ALU.mult), r=[PB[bb_], m3.b[0]], w=[gx[0].b[0]])
                    op("dve", lambda e, wt_=wt_: e.tensor_tensor(out=wt_[:], in0=wt_[:], in1=gx[0][:], op=ALU.add),
                       r=[wt_.b[0], gx[0].b[0]], w=[wt_.b[0]])
                    op("dve", lambda e, gbase=gbase: e.tensor_tensor(
                        out=gx[0].ap(0, [[128, NB], [16, 8], [1, 16]]),
                        in0=dsk.ap(gbase * 16, [[16, NB], [0, 8], [1, 16]]),
                        in1=m3.ap(256, [[0, NB], [16, 8], [1, 16]]), op=ALU.mult),
                       r=[dsk.b[0], m3.b[0]], w=[gx[0].b[0]])
                    op("dve", lambda e, wt_=wt_, gh=gh: e.tensor_tensor(
                        out=WT.ap(gh * NB * 128, [[1, NB * 128]]), in0=wt_[:], in1=gx[0][:], op=ALU.add),
                       r=[wt_.b[0], gx[0].b[0]], w=[WT.b[0]])
                for gidx in range(2 * NB):
                    gh, gll = gidx // NB, gidx % NB
                    gl = gl0 + gll
                    wslot = gh * NB + gll
                    g = gh * 32 + gl
                    yb_ = 4 + (iy % 2)
                    yg = Yg[iy % 2]
                    op("pe", lambda e, yb_=yb_, wslot=wslot, g=g: e.matmul(
                        PS[yb_][:, 0:256], lhsT=WT.ap(wslot * 128, [[1, 128]]), rhs=U.ap(g * NCH + 32, [[1, 256]]),
                        start=True, stop=False), r=[WT.b[0], U.b[g]], w=[PB[yb_]])
                    k_ = 0
                    for d_ in range(2):
                        for part in range(2):
                            k_ += 1
                            op("pe", lambda e, yb_=yb_, d_=d_, part=part, gh=gh, gll=gll, gl=gl, k_=k_: e.matmul(
                                PS[yb_][:, 0:256],
                                lhsT=WYb[d_].ap((part * NB + gll) * 128, [[1, 128]], p0=gh * 64, np_=64),
                                rhs=HSt.ap(((d_ * 2 + part) * 32 + gl) * 256, [[1, 256]], p0=gh * 64, np_=64),
                                start=False, stop=(k_ == 4)), r=[WYb[d_].b[0], HSt.b[0]], w=[PB[yb_]])
                    op("act", lambda e, yb_=yb_, yg=yg: e.activation(out=yg[:], in_=PS[yb_][:, 0:256], func=AF.Identity),
                       r=[PB[yb_]], w=[yg.b[0]])
                    tb_ = 6 + (iy % 2)
                    gx_ = gx[1]
                    for ct in range(2):
                        op("pe", lambda e, tb_=tb_, ct=ct, yg=yg: e.transpose(
                            PS[tb_][:, ct * 128:(ct + 1) * 128], yg[:, ct * 128:(ct + 1) * 128], ident_f[:]),
                           r=[yg.b[0], ident_f.b[0]], w=[PB[tb_]])
                    xps = PS[tb_][:, 0:256]
                    op("act", lambda e, xps=xps, gx_=gx_: e.activation(out=gx_[:, 0:256], in_=xps, func=AF.Square),
                       r=[PB[tb_]], w=[gx_.b[0]])
                    op("dve", lambda e, gx_=gx_: e.tensor_scalar(gx_[:, 0:256], gx_[:, 0:256], 0.044715, 1.0, ALU.mult,
                                                                 ALU.add), r=[gx_.b[0]], w=[gx_.b[0]])
                    op("dve", lambda e, xps=xps, gx_=gx_: e.tensor_tensor(out=gx_[:, 0:256], in0=xps, in1=gx_[:, 0:256],
                                                                          op=ALU.mult), r=[PB[tb_], gx_.b[0]],
                       w=[gx_.b[0]])
                    op("act", lambda e, gx_=gx_: e.activation(out=gx_[:, 0:256], in_=gx_[:, 0:256], func=AF.Sigmoid,
                                                              scale=C_G), r=[gx_.b[0]], w=[gx_.b[0]])
                    for ct in range(2):
                        op("dve", lambda e, ct=ct, g=g, tb_=tb_, gx_=gx_: e.tensor_tensor(
                            out=gc.ap((2 + ct * 8) * 1024 + g * 16, [[1024, 8], [1, 16]]),
                            in0=pap(PS[tb_], ct * 128, [[16, 8], [1, 16]]),
                            in1=gx_.ap(ct * 128, [[16, 8], [1, 16]]), op=ALU.mult),
                           r=[PB[tb_], gx_.b[0]], w=gc.b[2 + ct * 8:2 + ct * 8 + 8])
                    iy += 1
            kb.barrier()

            a = Arena(*R_B)
            gw = mk("gw", [128, 8, 1024], BF16, arena=a)
            gb = mk("gb", [1, 1024], F32, arena=a)
            gT = [mk("gT%d" % i, [128, 8, 128], BF16, arena=a) for i in range(2)]
            sg = [mk("sg%d" % i, [128, 512], F32, arena=a) for i in range(2)]
            for kc in range(8):
                dma("pool", gw[:, kc, :], gluwd[kc * 128:(kc + 1) * 128, :], w=[gw.b[0]])
            dma("sp", gb[:], glubd[:, :], w=[gb.b[0]])
            isg = 0
            for pt in range(16):
                tt = 2 + pt
                gt = gT[pt % 2]
                for half in range(2):
                    bank = 4 + ((pt * 2 + half) % 4)
                    for c4 in range(4):
                        c = half * 4 + c4
                        op("pe", lambda e, c=c, c4=c4, bank=bank, tt=tt: e.matmul(
                            PS[bank][:, c4 * 128:(c4 + 1) * 128], lhsT=gc[:, tt, c * 128:(c + 1) * 128], rhs=ident_b[:],
                            start=True, stop=True), r=[gc.b[tt], ident_b.b[0]], w=[PB[bank]])
                    if half == 0:
                        op("act", lambda e, half=half, bank=bank, gt=gt: e.activation(
                            out=gt.ap(half * 512, [[1, 512]]), in_=PS[bank][:, :], func=AF.Identity),
                           r=[PB[bank]], w=[gt.b[0]])
                    else:
                        op("dve", lambda e, half=half, bank=bank, gt=gt: e.tensor_copy(
                            out=gt.ap(half * 512, [[1, 512]]), in_=PS[bank][:, :]), r=[PB[bank]], w=[gt.b[0]])
                for half in range(2):
                    bank = (pt * 2 + half) % 4
                    for kc in range(8):
                        op("pe", lambda e, kc=kc, half=half, bank=bank, gt=gt: e.matmul(
                            PS[bank][:, :], lhsT=gt[:, kc, :], rhs=gw[:, kc, half * 512:(half + 1) * 512],
                            start=(kc == 0), stop=False), r=[gt.b[0], gw.b[0]], w=[PB[bank]])
                    op("pe", lambda e, half=half, bank=bank: e.matmul(
                        PS[bank][:, :], lhsT=ones_f[0:1, :], rhs=gb[0:1, half * 512:(half + 1) * 512],
                        start=False, stop=True), r=[ones_f.b[0], gb.b[0]], w=[PB[bank]])
                    s_ = sg[isg % 2]
                    isg += 1
                    op("act", lambda e, bank=bank, s_=s_: e.activation(out=s_[:], in_=PS[bank][:, :], func=AF.Sigmoid),
                       r=[PB[bank]], w=[s_.b[0]])
                    op("dve", lambda e, half=half, tt=tt, s_=s_: e.tensor_tensor(
                        out=gc[:, tt, half * 512:(half + 1) * 512], in0=gc[:, tt, half * 512:(half + 1) * 512],
                        in1=s_[:], op=ALU.mult), r=[gc.b[tt], s_.b[0]], w=[gc.b[tt]])
            kb.barrier()

        hT = T(kb, "hT", [128, NT, 1024], F32, R_H[0], nslots=NT)
        aT = T(kb, "aT", [128, 8, NTOK], BF16, R_A[0], nslots=NT)
        yy = T(kb, "yy", [128, NT, 1024], BF16, R_A[0], nslots=NT)
        resid = xin
        for li in layers:
            last = (li == 1)
            phase_mod(li)
            tiles_all = list(range(NT))
            phase_norm(li, 0, resid, None, aT, tiles_all)
            if li == 0:
                phase_attn(aT, yy)
                tiles = tiles_all
                phase_outproj(yy, hT, woutd, resid, tiles)
            else:
                phase_s5(aT, yy)
                tiles = list(range(2, NT))
                phase_outproj(yy, hT, s5woutd, resid, tiles, cmajor=True)
            phase_norm(li, 1, None, hT, aT, tiles)
            if last:
                blocks = [(2 + 4 * g, 4, 0) for g in range(4)]
            else:
                blocks = [(0, 2, 1)] + [(2 + 4 * g, 4, 0) for g in range(4)]
            phase_mlp(li, aT, hT, blocks)
            dst = hout if li == last_layer else hmid
            for tt in tiles:
                dma("sp", rows_ap(dst, tt, li == 1), hT[:, tt, :], r=[hT.b[tt]])
            kb.barrier()
            resid = hmid
        if dbg is not None:
            pass
    return nc


def _host_consts():
    inv = (10000.0 ** (-np.arange(16, dtype=np.float32) / np.float32(16))).astype(np.float32)
    pos = np.arange(2048)
    row = (pos // 64).astype(np.float32)
    col = (pos % 64).astype(np.float32)
    ang = np.concatenate([row[:, None] * inv[None], col[:, None] * inv[None]], axis=-1).astype(np.float32)
    cos = np.cos(ang).astype(np.float32).reshape(16, 128, 32).transpose(1, 0, 2)
    sin = np.sin(ang).astype(np.float32).reshape(16, 128, 32).transpose(1, 0, 2)
    k = np.arange(128)[:, None]
    q = np.arange(128)[None, :]
    masks = np.stack([(k >= q), (k <= q)]).astype(np.float32)
    return np.ascontiguousarray(cos), np.ascontiguousarray(sin), masks


_PROG = {}


def _get_prog(key):
    if key not in _PROG:
        _PROG[key] = build_program(list(key))
    return _PROG[key]


def _common_maps(inp, b):
    f = np.float32
    cvec = np.stack([np.asarray(inp["c"][b], f).reshape(8, 128).T, np.asarray(inp["c_ctx"], f).reshape(8, 128).T],
                    axis=-1)
    m = {
        "cvec": np.ascontiguousarray(cvec),
        "mod_w": np.asarray(inp["mod_w"], f),
        "modb": np.ascontiguousarray(np.asarray(inp["mod_b"], f).reshape(2, 48, 128).transpose(0, 2, 1)),
        "g1": np.ascontiguousarray(np.asarray(inp["norm1_g"], f).reshape(2, 8, 128).transpose(0, 2, 1)),
        "g2": np.ascontiguousarray(np.asarray(inp["norm2_g"], f).reshape(2, 8, 128).transpose(0, 2, 1)),
        "mlp_w1": np.asarray(inp["mlp_w1"], f),
        "mlp_w2": np.asarray(inp["mlp_w2"], f),
        "ident": np.eye(128, dtype=f),
    }
    return m


def _l0_maps(inp):
    f = np.float32
    cos, sin, masks = _host_consts()
    aq, ak = np.asarray(inp["a_q_norm"][0], f), np.asarray(inp["a_k_norm"][0], f)
    bq, bk = np.asarray(inp["b_q_norm"][0], f), np.asarray(inp["b_k_norm"][0], f)
    gains = np.stack([aq, ak, bq, bk])
    lqk = np.stack([inp["b_lq1"][0], inp["b_lk1"][0], inp["b_lq2"][0], inp["b_lk2"][0]]).astype(f)
    return {
        "attn_w_in": np.asarray(inp["attn_w_in"][0], f),
        "attn_w_out": np.asarray(inp["attn_w_out"][0], f),
        "rope_cos": cos, "rope_sin": sin,
        "qk_gain": np.ascontiguousarray(np.broadcast_to(gains[None], (128, 4, 64))),
        "a_sink": np.ascontiguousarray(np.broadcast_to(np.asarray(inp["a_sink"][0], f)[None], (128, 8))),
        "b_lqk": np.ascontiguousarray(np.broadcast_to(lqk[None], (128, 4, 64))),
        "b_subln": np.ascontiguousarray(np.broadcast_to(np.asarray(inp["b_subln"][0], f)[None], (128, 128))),
        "masks": masks,
    }


def _l1_maps(inp):
    f = np.float32
    maps = {
        "s5_w_in": np.asarray(inp["s5_w_in"][0], f),
        "s5_w_out": np.asarray(inp["s5_w_out"][0], f),
        "s5_glu_w": np.asarray(inp["s5_glu_w"][0], f),
        "s5_glu_b": np.ascontiguousarray(np.asarray(inp["s5_glu_b"][0], f).reshape(1, D)),
    }

    def nlay(x):
        return np.asarray(x, f).reshape(2, 32, 64).transpose(0, 2, 1).reshape(128, 32)

    lam_n = np.zeros((2, 3, 128, 32), f)
    bn = np.zeros((2, 2, 128, 32, 16), f)
    cn = np.zeros((2, 2, 128, 32, 16), f)
    bsj = np.zeros((2, 2, 128, 4096), f)
    for d in range(2):
        lam_n[d, 0] = nlay(inp["s5_lambda_re"][0][d])
        lam_n[d, 1] = nlay(inp["s5_lambda_im"][0][d])
        lam_n[d, 2] = nlay(np.broadcast_to(np.asarray(inp["s5_log_step"][0][d], f)[:, None], (64, 64)))
        for p, (bk, ck) in enumerate((("s5_b_re", "s5_c_re"), ("s5_b_im", "s5_c_im"))):
            b = np.asarray(inp[bk][0][d], f)
            c = np.asarray(inp[ck][0][d], f)
            bn[d, p] = b.reshape(2, 32, 64, 16).transpose(0, 2, 1, 3).reshape(128, 32, 16)
            cn[d, p] = c.reshape(2, 32, 16, 64).transpose(0, 3, 1, 2).reshape(128, 32, 16)
            bj = b.transpose(2, 0, 1).reshape(16, 4096)
            bsj[d, p] = np.broadcast_to(bj[None], (8, 16, 4096)).reshape(128, 4096)
    k = np.arange(8, dtype=f)
    ktab = np.stack([k + 1, 8 - k, -(k + 1), k - 8, 7 - k, k]).astype(f)
    s_ = np.arange(128) // 16
    j_ = np.arange(128) % 16
    mf = (s_[:, None] <= s_[None, :])
    mb = (s_[:, None] >= s_[None, :])
    md = (s_[:, None] == s_[None, :]) & (j_[:, None] == j_[None, :])
    maps.update({
        "s5_lam_n": lam_n, "s5_bn": bn, "s5_cn": cn, "s5_bsj": bsj,
        "s5_ktab": np.ascontiguousarray(np.broadcast_to(ktab[None], (128, 6, 8))),
        "s5_dsk": np.ascontiguousarray(np.broadcast_to(np.asarray(inp["s5_d"][0], f)[None], (128, D))),
        "s5_masks": np.stack([mf, mb, md]).astype(f),
    })
    return maps


def _run(layers, inp, xins, ncores=8):
    nc = _get_prog(tuple(layers))
    extra = {}
    if 0 in layers:
        extra.update(_l0_maps(inp))
    if 1 in layers:
        extra.update(_l1_maps(inp))
    maps = []
    for b in range(ncores):
        m = _common_maps(inp, b)
        m.update(extra)
        m["xin"] = np.ascontiguousarray(xins[b], dtype=np.float32)
        maps.append(m)
    res = run_bass_kernel_spmd(nc, maps, core_ids=list(range(ncores)))
    return [r["hout"] for r in res.results]


FUSED = True


def kernel(**inputs):
    inp = {k: np.asarray(v) for k, v in inputs.items()}
    nb = inp["x"].shape[0]
    xins = [np.concatenate([inp["ctx"][b], inp["x"][b]], axis=0) for b in range(nb)]
    if FUSED:
        outs = _run([0, 1], inp, xins, nb)
    else:
        h0 = _run([0], inp, xins, nb)
        outs = _run([1], inp, h0, nb)
    return np.stack(outs).astype(np.float32)
```

```python
import math
from contextlib import ExitStack

import numpy as np
import concourse.bass as bass
import concourse.mybir as mybir
from concourse.ap import AP
from concourse.bass_utils import run_bass_kernel_spmd

F32 = mybir.dt.float32
BF16 = mybir.dt.bfloat16
I32 = mybir.dt.int32
AF = mybir.ActivationFunctionType
ALU = mybir.AluOpType
AX = mybir.AxisListType

D = 1024
NT = 18
NTOK = NT * 128
EPS = 1e-6
SB_LO = 16640
SB_HI = 229376
SAME_ENG_SKIP_DIST = 2


def dsize(dt):
    return {F32: 4, BF16: 2, I32: 4}[dt]


class Buf:
    __slots__ = ("w", "r")

    def __init__(self):
        self.w = None
        self.r = {}


class KB:
    ENGS = ("pe", "dve", "act", "pool", "sp")
    NDS = 24

    def __init__(self, nc, es):
        self.nc = nc
        self.E = {"pe": nc.tensor, "dve": nc.vector, "act": nc.scalar, "pool": nc.gpsimd, "sp": nc.sync}
        self.sems = {}
        for e in self.ENGS:
            self.sems[e] = es.enter_context(nc.semaphore("s_" + e))
        for i in range(self.NDS):
            self.sems[("d", i)] = es.enter_context(nc.semaphore("s_d%d" % i))
        self.cnt = {e: 0 for e in self.ENGS}
        self.dcnt = [0] * self.NDS
        self.dma_i = 0
        self.seen = {e: {} for e in self.ENGS}
        self.nbuf = 0

    def _wait(self, eng, toks):
        need = {}
        for key, val in toks:
            if key == eng and (eng == "pe" or self.cnt[eng] - val >= SAME_ENG_SKIP_DIST):
                continue
            if self.seen[eng].get(key, 0) < val and need.get(key, 0) < val:
                need[key] = val
        for key, val in need.items():
            self.E[eng].wait_ge(self.sems[key], val)
            self.seen[eng][key] = val

    @staticmethod
    def _deps(r, w):
        toks = []
        for b in r:
            if b.w is not None:
                toks.append(b.w)
        for b in w:
            if b.w is not None:
                toks.append(b.w)
            toks.extend(b.r.items())
        return toks

    @staticmethod
    def _upd(tok, r, w):
        for b in r:
            if b.r.get(tok[0], 0) < tok[1]:
                b.r[tok[0]] = tok[1]
        for b in w:
            b.w = tok
            b.r = {}

    def op(self, eng, fn, r=(), w=(), nosync=False):
        toks = self._deps(r, w)
        if nosync:
            toks = [t for t in toks if t[0] != eng]
        self._wait(eng, toks)
        ins = fn(self.E[eng])
        self.cnt[eng] += 1
        ins.then_inc(self.sems[eng], 1)
        self._upd((eng, self.cnt[eng]), r, w)

    def dma(self, q, out, in_, r=(), w=()):
        toks = self._deps(r, w)
        i = self.dma_i % self.NDS
        self.dma_i += 1
        key = ("d", i)
        if self.dcnt[i] > 0:
            toks.append((key, self.dcnt[i]))
        self._wait(q, toks)
        ins = self.E[q].dma_start(out=out, in_=in_)
        self.dcnt[i] += 16
        ins.then_inc(self.sems[key], 16)
        self._upd((key, self.dcnt[i]), r, w)

    def barrier(self):
        toks = [(e, self.cnt[e]) for e in self.ENGS if self.cnt[e] > 0]
        toks += [(("d", i), self.dcnt[i]) for i in range(self.NDS) if self.dcnt[i] > 0]
        for e in self.ENGS:
            self._wait(e, toks)


class T:
    def __init__(self, kb, name, shape, dt, off, nslots=1):
        self.t = kb.nc.alloc_sbuf_tensor_at(name, list(shape), dt, offset=off)
        self.shape = list(shape)
        self.dt = dt
        self.row = int(np.prod(shape[1:]))
        self.bytes = self.row * dsize(dt)
        self.b = [Buf() for _ in range(nslots)]
        self.off = off
        self.end = off + ((self.bytes + 63) // 64) * 64

    def ap(self, off, free, p0=0, np_=128):
        return AP(self.t, p0 * self.row + off, [[self.row, np_]] + [list(f) for f in free])

    def __getitem__(self, k):
        return self.t[k]


class Arena:
    def __init__(self, lo, hi):
        self.lo, self.hi, self.top = lo, hi, lo

    def take(self, nbytes):
        off = self.top
        self.top += ((nbytes + 63) // 64) * 64
        assert self.top <= self.hi, ("arena overflow", self.top, self.hi)
        return off


def pap(t, off, free, p0=0, np_=128, row=512):
    return AP(t, p0 * row + off, [[row, np_]] + [list(f) for f in free])


def build_program(layers, n_out_rows_last=2048, debug=None):
    nc = bass.Bass("TRN2", target_bir_lowering=False)
    dr = {}

    def din(name, shape, dt=F32):
        dr[name] = nc.dram_tensor(name, list(shape), dt, kind="ExternalInput").ap()
        return dr[name]

    xin = din("xin", [NTOK, D])
    cvec = din("cvec", [128, 8, 2])
    mod_w = din("mod_w", [2, D, 6 * D])
    modb = din("modb", [2, 128, 48])
    g1d = din("g1", [2, 128, 8])
    g2d = din("g2", [2, 128, 8])
    w1d = din("mlp_w1", [2, D, 4 * D])
    w2d = din("mlp_w2", [2, 4 * D, D])
    identd = din("ident", [128, 128])
    if 0 in layers:
        wind = din("attn_w_in", [D, 2304])
        woutd = din("attn_w_out", [D, D])
        ropec = din("rope_cos", [128, 16, 32])
        ropes = din("rope_sin", [128, 16, 32])
        gaind = din("qk_gain", [128, 4, 64])
        sinkd = din("a_sink", [128, 8])
        lqkd = din("b_lqk", [128, 4, 64])
        sublnd = din("b_subln", [128, 128])
        maskd = din("masks", [2, 128, 128])
    if 1 in layers:
        s5wind = din("s5_w_in", [D, D])
        s5woutd = din("s5_w_out", [D, D])
        gluwd = din("s5_glu_w", [D, D])
        glubd = din("s5_glu_b", [1, D])
        lamnd = din("s5_lam_n", [2, 3, 128, 32])
        ktabd = din("s5_ktab", [128, 6, 8])
        bnd = din("s5_bn", [2, 2, 128, 32, 16])
        cnd = din("s5_cn", [2, 2, 128, 32, 16])
        bsjd = din("s5_bsj", [2, 2, 128, 4096])
        dskd = din("s5_dsk", [128, 1024])
        m3d = din("s5_masks", [3, 128, 128])
        escr = nc.dram_tensor("escr", [2, 2, 8, 4096], F32, kind="Internal").ap()
    last_layer = layers[-1]
    n_rows_out = NTOK if last_layer != 1 else 2048
    hout = nc.dram_tensor("hout", [n_rows_out, D], F32, kind="ExternalOutput").ap()
    hmid = None
    if len(layers) > 1:
        hmid = nc.dram_tensor("hmid", [NTOK, D], F32, kind="Internal").ap()
    dbg = None
    if debug is not None:
        dbg = nc.dram_tensor("dbg", list(debug), F32, kind="ExternalOutput").ap()

    with ExitStack() as es:
        kb = KB(nc, es)
        op, dma = kb.op, kb.dma
        PS = [es.enter_context(nc.psum_tensor("bank%d" % i, [128, 512], F32)) for i in range(8)]
        PB = [Buf() for _ in range(8)]

        ar = Arena(SB_LO, SB_HI)

        def mk(name, shape, dt, nslots=1, arena=None):
            a = arena or ar
            row = int(np.prod(shape[1:]))
            off = a.take(row * dsize(dt))
            return T(kb, name, shape, dt, off, nslots)

        ident_f = mk("ident_f", [128, 128], F32)
        ident_b = mk("ident_b", [128, 128], BF16)
        ones_f = mk("ones_f", [128, 128], F32)
        m_all = mk("m_all", [128, 48, 2], F32)
        gam = mk("gam", [128, 2, 8, 2], F32)
        gate_rep = mk("gate_rep", [128, 2, 2, 1024], F32, nslots=4)
        small = mk("small", [128, 64], F32, nslots=8)
        cv = mk("cv", [128, 8, 2], F32)
        sv = mk("sv", [128, 8, 2], F32)
        g1s = mk("g1s", [128, 8], F32)
        g2s = mk("g2s", [128, 8], F32)
        modbs = mk("modbs", [128, 48], F32)
        macc = mk("macc", [128, 96], F32)
        junk = mk("junk", [128, 1024], BF16)
        P_END = ar.top
        R_H = (P_END, P_END + 73728)
        R_A = (R_H[1], R_H[1] + 36864)
        R_B = (R_A[1], SB_HI)
        assert R_B[1] - R_B[0] >= 78500, (R_B, "region B too small")

        dma("sp", ident_f[:], identd[:, :], w=[ident_f.b[0]])
        dma("pool", ident_b[:], identd[:, :], w=[ident_b.b[0]])
        op("dve", lambda e: e.memset(ones_f[:], 1.0), w=[ones_f.b[0]])
        dma("sp", cv[:], cvec[:, :, :], w=[cv.b[0]])
        op("act", lambda e: e.activation(out=sv[:], in_=cv[:], func=AF.Silu), r=[cv.b[0]], w=[sv.b[0]])

        stat_i = [0]

        def stat_slot():
            i = stat_i[0] % 8
            stat_i[0] += 1
            return i

        def rstd_from_ss(ss_ap, out_ap, n, bufs_r, bufs_w, cols=1):
            op("dve", lambda e: e.tensor_scalar(out_ap, ss_ap, 1.0 / n, EPS, ALU.mult, ALU.add), r=bufs_r, w=bufs_w)
            op("act", lambda e: e.activation(out=out_ap, in_=out_ap, func=AF.Sqrt), r=bufs_w, w=bufs_w)
            op("dve", lambda e: e.reciprocal(out_ap, out_ap), r=bufs_w, w=bufs_w)

        def phase_mod(li):
            a = Arena(*R_B)
            mw = [mk("mw%d" % i, [128, 3072], F32, arena=a) for i in range(2)]
            mrow = mk("mrow", [2, 6144], F32, arena=a)
            dg = [mk("dg%d" % i, [128, 128], F32, arena=a) for i in range(2)]
            dma("sp", g1s[:], g1d[li, :, :], w=[g1s.b[0]])
            dma("sp", g2s[:], g2d[li, :, :], w=[g2s.b[0]])
            dma("sp", modbs[:], modb[li, :, :], w=[modbs.b[0]])
            for half in range(2):
                for kc in range(8):
                    m = mw[kc % 2]
                    dma("sp" if kc % 2 == 0 else "act", m[:],
                        mod_w[li, kc * 128:(kc + 1) * 128, half * 3072:(half + 1) * 3072], w=[m.b[0]])
                    for j in range(6):
                        op("pe", lambda e, j=j, m=m, kc=kc: e.matmul(
                            PS[j][0:2, :], lhsT=sv[:, kc, :], rhs=m[:, j * 512:(j + 1) * 512],
                            start=(kc == 0), stop=(kc == 7)), r=[m.b[0], sv.b[0]], w=[PB[j]])
                for j in range(6):
                    op("act", lambda e, j=j, half=half: e.activation(
                        out=mrow[0:2, half * 3072 + j * 512:half * 3072 + (j + 1) * 512], in_=PS[j][0:2, :],
                        func=AF.Identity), r=[PB[j]], w=[mrow.b[0]])
            for oc in range(48):
                op("pe", lambda e, oc=oc: e.matmul(
                    PS[7][:, oc * 2:oc * 2 + 2], lhsT=mrow[0:2, oc * 128:(oc + 1) * 128], rhs=ident_f[0:2, 0:2],
                    start=True, stop=True), r=[mrow.b[0], ident_f.b[0]], w=[PB[7]])
            op("dve", lambda e: e.tensor_copy(out=macc[:], in_=PS[7][:, 0:96]), r=[PB[7]], w=[macc.b[0]])
            op("dve", lambda e: e.tensor_tensor(
                out=m_all[:], in0=macc.ap(0, [[2, 48], [1, 2]]), in1=modbs.ap(0, [[1, 48], [0, 2]]), op=ALU.add),
               r=[macc.b[0], modbs.b[0]], w=[m_all.b[0]])
            for ni, (gs, oc0) in enumerate(((g1s, 8), (g2s, 32))):
                op("dve", lambda e, ni=ni, gs=gs, oc0=oc0: e.scalar_tensor_tensor(
                    out=gam.ap(ni * 16, [[2, 8], [1, 2]]), in0=m_all.ap(oc0 * 2, [[2, 8], [1, 2]]), scalar=1.0,
                    in1=gs.ap(0, [[1, 8], [0, 2]]), op0=ALU.add, op1=ALU.mult),
                   r=[m_all.b[0], gs.b[0]], w=[gam.b[0]])
            k = 0
            for gi, oc0 in enumerate((16, 40)):
                for r_ in range(2):
                    for half in range(2):
                        bank = 1 + (k % 2)
                        k += 1
                        for c4 in range(4):
                            c = half * 4 + c4
                            d_ = dg[c % 2]
                            op("dve", lambda e, d_=d_, c=c, oc0=oc0, r_=r_: e.tensor_scalar(
                                d_[:], ident_f[:], m_all[:, oc0 + c, r_:r_ + 1], None, ALU.mult),
                               r=[ident_f.b[0], m_all.b[0]], w=[d_.b[0]])
                            op("pe", lambda e, d_=d_, c4=c4, bank=bank: e.matmul(
                                PS[bank][:, c4 * 128:(c4 + 1) * 128], lhsT=ones_f[:], rhs=d_[:], start=True, stop=True),
                               r=[ones_f.b[0], d_.b[0]], w=[PB[bank]])
                        sl = gi * 2 + r_
                        op("act", lambda e, gi=gi, r_=r_, half=half, bank=bank: e.activation(
                            out=gate_rep[:, gi, r_, half * 512:(half + 1) * 512], in_=PS[bank][:, :], func=AF.Identity),
                           r=[PB[bank]], w=[gate_rep.b[sl]])
            kb.barrier()

        def phase_norm(li, ni, src_dram, hT, aT, tiles):
            a = Arena(*R_B)
            xs = [mk("nx%d" % i, [128, 1024], F32, arena=a) for i in range(3)]
            xh = [mk("nxh%d" % i, [128, 1024], F32, arena=a) for i in range(2)]
            n_ = len(tiles)
            stt = {}

            def stage_a(it):
                tt = tiles[it]
                if src_dram is not None:
                    x = xs[it % 3]
                    dma("sp", x[:], src_dram[tt * 128:(tt + 1) * 128, :], w=[x.b[0]])
                    xap, xb = x[:], x.b[0]
                else:
                    xap, xb = hT[:, tt, :], hT.b[tt]
                si = stat_slot()
                ss = small[:, si * 8:si * 8 + 1]
                sb_ = small.b[si]
                op("act", lambda e, xap=xap, ss=ss: e.activation(out=junk[:], in_=xap, func=AF.Square, accum_out=ss),
                   r=[xb], w=[junk.b[0], sb_])
                stt[it] = (xap, xb, ss, sb_)

            def stage_b(it):
                xap, xb, ss, sb_ = stt[it]
                rstd_from_ss(ss, ss, 1024.0, [sb_], [sb_])
                h_ = xh[it % 2]
                op("dve", lambda e, h_=h_, xap=xap, ss=ss: e.tensor_scalar(h_[:], xap, ss, None, ALU.mult),
                   r=[xb, sb_], w=[h_.b[0]])

            def stage_c(it):
                tt = tiles[it]
                r_ = 1 if tt < 2 else 0
                h_ = xh[it % 2]
                for half in range(2):
                    bank = 5 + ((it * 2 + half) % 3)
                    for c4 in range(4):
                        c = half * 4 + c4
                        op("pe", lambda e, h_=h_, c=c, c4=c4, bank=bank: e.transpose(
                            PS[bank][:, c4 * 128:(c4 + 1) * 128], h_[:, c * 128:(c + 1) * 128], ident_f[:]),
                           r=[h_.b[0], ident_f.b[0]], w=[PB[bank]])
                    for c4 in range(4):
                        c = half * 4 + c4
                        g_ap = gam[:, ni, c, r_:r_ + 1]
                        oc_shift = (0 if ni == 0 else 24) + c
                        b_ap = m_all[:, oc_shift, r_:r_ + 1]
                        o_ap = aT[:, c, tt * 128:(tt + 1) * 128]
                        i_ap = PS[bank][:, c4 * 128:(c4 + 1) * 128]
                        if c4 % 2 == 0:
                            op("act", lambda e, o_ap=o_ap, i_ap=i_ap, g_ap=g_ap, b_ap=b_ap: e.activation(
                                out=o_ap, in_=i_ap, func=AF.Identity, bias=b_ap, scale=g_ap),
                               r=[PB[bank], gam.b[0], m_all.b[0]], w=[aT.b[tt]])
                        else:
                            op("dve", lambda e, o_ap=o_ap, i_ap=i_ap, g_ap=g_ap, b_ap=b_ap: e.tensor_scalar(
                                o_ap, i_ap, g_ap, b_ap, ALU.mult, ALU.add),
                               r=[PB[bank], gam.b[0], m_all.b[0]], w=[aT.b[tt]])

            for it in range(n_ + 2):
                if it < n_:
                    stage_a(it)
                if 0 <= it - 1 < n_:
                    stage_b(it - 1)
                if 0 <= it - 2 < n_:
                    stage_c(it - 2)
            kb.barrier()

        def phase_attn(aT, y):
            a = Arena(*R_B)
            QKT = mk("QKT", [128, 13, NTOK], BF16, nslots=NT, arena=a)
            VB = mk("VB", [128, NT, 4, 129], BF16, nslots=NT, arena=a)
            ah = Arena(*R_H)
            VA = mk("VA", [128, NT, 2, 65], BF16, nslots=NT, arena=ah)
            win = mk("win", [128, 8, 2304], BF16, arena=ah)
            cosT = mk("cosT", [128, 16, 32], F32, arena=ah)
            sinT = mk("sinT", [128, 16, 32], F32, arena=ah)
            gains = mk("gains", [128, 4, 64], F32, arena=ah)
            scr = mk("scr", [128, 26, 64], F32, arena=ah)
            xraw = [mk("xraw%d" % i, [128, 26, 64], F32, arena=ah) for i in range(2)]
            qkt = [mk("qkt%d" % i, [128, 26, 64], BF16, arena=ah) for i in range(2)]
            ssb = mk("ssb", [128, 2, 32], F32, nslots=2, arena=ah)
            for kc in range(8):
                for b2 in range(2):
                    dma("pool", win.ap(kc * 2304 + b2 * 64, [[128, 4], [1, 64]]),
                        AP(wind.tensor, kc * 128 * 2304 + b2 * 256, [[2304, 128], [64, 4], [1, 64]]), w=[win.b[0]])
                dma("pool", win[:, kc, 512:2304], wind[kc * 128:(kc + 1) * 128, 512:2304], w=[win.b[0]])
            dma("sp", cosT[:], ropec[:, :, :], w=[cosT.b[0]])
            dma("sp", sinT[:], ropes[:, :, :], w=[sinT.b[0]])
            dma("sp", gains[:], gaind[:, :, :], w=[gains.b[0]])
            op("pool", lambda e: e.memset(VA.ap(64, [[130, NT], [65, 2], [1, 1]]), 1.0), w=VA.b)
            op("pool", lambda e: e.memset(VB.ap(128, [[516, NT], [129, 4], [1, 1]]), 1.0), w=VB.b)

            qk_ranges = [(0, 0, 512, 0), (1, 0, 128, 8), (1, 256, 256, 10), (2, 0, 512, 14), (3, 0, 256, 22)]
            gtypes = [(0, 8), (8, 2), (10, 8), (18, 8)]
            ncols = [512, 512, 512, 512, 256]

            def proj_mm(tt):
                for nb in range(5):
                    for kc in range(8):
                        op("pe", lambda e, nb=nb, kc=kc, tt=tt: e.matmul(
                            PS[nb][:, 0:ncols[nb]], lhsT=aT[:, kc, tt * 128:(tt + 1) * 128],
                            rhs=win[:, kc, nb * 512:nb * 512 + ncols[nb]], start=(kc == 0), stop=(kc == 7)),
                           r=[aT.b[tt], win.b[0]], w=[PB[nb]])

            def proj_evac(tt):
                x_ = xraw[tt % 2]
                for (bk, c0, n, g0) in qk_ranges:
                    op("act", lambda e, bk=bk, c0=c0, n=n, g0=g0: e.activation(
                        out=x_.ap(g0 * 64, [[1, n]]), in_=PS[bk][:, c0:c0 + n], func=AF.Identity),
                       r=[PB[bk]], w=[x_.b[0]])
                op("act", lambda e, tt=tt: e.activation(
                    out=VA.ap(tt * 130, [[65, 2], [1, 64]]), in_=pap(PS[1], 128, [[64, 2], [1, 64]]), func=AF.Identity),
                   r=[PB[1]], w=[VA.b[tt]])
                op("act", lambda e, tt=tt: e.activation(
                    out=VB.ap(tt * 516, [[129, 2], [1, 128]]), in_=pap(PS[3], 256, [[128, 2], [1, 128]]),
                    func=AF.Identity), r=[PB[3]], w=[VB.b[tt]])
                op("act", lambda e, tt=tt: e.activation(
                    out=VB.ap(tt * 516 + 258, [[129, 2], [1, 128]]), in_=pap(PS[4], 0, [[128, 2], [1, 128]]),
                    func=AF.Identity), r=[PB[4]], w=[VB.b[tt]])

            def proj_post(tt):
                s_ = scr
                x_ = xraw[tt % 2]
                q_ = qkt[tt % 2]
                sl = tt % 2
                op("act", lambda e: e.activation(out=s_[:], in_=x_[:], func=AF.Square), r=[x_.b[0]], w=[s_.b[0]])
                ssap = ssb.ap(sl * 32, [[1, 26]])
                op("dve", lambda e, ssap=ssap: e.tensor_reduce(out=ssap, in_=s_[:], axis=AX.X, op=ALU.add),
                   r=[s_.b[0]], w=[ssb.b[sl]])
                rstd_from_ss(ssap, ssap, 64.0, [ssb.b[sl]], [ssb.b[sl]])
                op("dve", lambda e: e.tensor_tensor(out=x_[:], in0=x_[:], in1=ssb.ap(sl * 32, [[1, 26], [0, 64]]),
                                                    op=ALU.mult), r=[x_.b[0], ssb.b[sl]], w=[x_.b[0]])
                for ty, (g0, ng) in enumerate(gtypes):
                    op("dve", lambda e, ty=ty, g0=g0, ng=ng: e.tensor_tensor(
                        out=x_.ap(g0 * 64, [[64, ng], [1, 64]]), in0=x_.ap(g0 * 64, [[64, ng], [1, 64]]),
                        in1=gains.ap(ty * 64, [[0, ng], [1, 64]]), op=ALU.mult),
                       r=[x_.b[0], gains.b[0]], w=[x_.b[0]])
                if tt >= 2:
                    lt = tt - 2
                    cb = cosT.ap(lt * 32, [[0, 26], [0, 2], [1, 32]])
                    sb2 = sinT.ap(lt * 32, [[0, 26], [1, 32]])
                    op("dve", lambda e, sb2=sb2: e.tensor_tensor(
                        out=scr.ap(0, [[32, 26], [1, 32]]), in0=x_.ap(32, [[64, 26], [1, 32]]), in1=sb2, op=ALU.mult),
                       r=[x_.b[0], sinT.b[0]], w=[scr.b[0]])
                    op("dve", lambda e, sb2=sb2: e.tensor_tensor(
                        out=scr.ap(832, [[32, 26], [1, 32]]), in0=x_.ap(0, [[64, 26], [1, 32]]), in1=sb2, op=ALU.mult),
                       r=[x_.b[0], sinT.b[0]], w=[scr.b[0]])
                    op("dve", lambda e, cb=cb: e.tensor_tensor(
                        out=x_.ap(0, [[64, 26], [32, 2], [1, 32]]), in0=x_.ap(0, [[64, 26], [32, 2], [1, 32]]),
                        in1=cb, op=ALU.mult), r=[x_.b[0], cosT.b[0]], w=[x_.b[0]])
                    op("dve", lambda e: e.tensor_tensor(
                        out=q_.ap(0, [[64, 26], [1, 32]]), in0=x_.ap(0, [[64, 26], [1, 32]]),
                        in1=scr.ap(0, [[32, 26], [1, 32]]), op=ALU.subtract), r=[x_.b[0], scr.b[0]], w=[q_.b[0]])
                    op("dve", lambda e: e.tensor_tensor(
                        out=q_.ap(32, [[64, 26], [1, 32]]), in0=x_.ap(32, [[64, 26], [1, 32]]),
                        in1=scr.ap(832, [[32, 26], [1, 32]]), op=ALU.add), r=[x_.b[0], scr.b[0]], w=[q_.b[0]])
                else:
                    op("dve", lambda e: e.tensor_copy(out=q_[:], in_=x_[:]), r=[x_.b[0]], w=[q_.b[0]])
                srcs = [q_.ap(h * 128, [[1, 128]]) for h in range(4)]
                srcs.append(q_.ap(8 * 64, [[1, 128]]))
                for h in range(4):
                    srcs.append(q_.ap((10 + 2 * h) * 64, [[1, 128]]))
                    srcs.append(q_.ap((18 + 2 * h) * 64, [[1, 128]]))
                for s0 in range(0, 13, 4):
                    bank = 5 + ((tt * 4 + s0 // 4) % 3)
                    ns = min(4, 13 - s0)
                    for j in range(ns):
                        op("pe", lambda e, j=j, s0=s0, bank=bank: e.matmul(
                            PS[bank][:, j * 128:(j + 1) * 128], lhsT=srcs[s0 + j], rhs=ident_b[:],
                            start=True, stop=True), r=[q_.b[0], ident_b.b[0]], w=[PB[bank]])
                    op("act", lambda e, s0=s0, ns=ns, bank=bank, tt=tt: e.activation(
                        out=QKT.ap(s0 * NTOK + tt * 128, [[NTOK, ns], [1, 128]]),
                        in_=pap(PS[bank], 0, [[128, ns], [1, 128]]), func=AF.Identity),
                       r=[PB[bank]], w=[QKT.b[tt]])

            proj_mm(0)
            proj_evac(0)
            proj_mm(1)
            for tt in range(NT):
                if tt + 1 < NT:
                    proj_evac(tt + 1)
                if tt + 2 < NT:
                    proj_mm(tt + 2)
                proj_post(tt)
            kb.barrier()

            ah = Arena(R_H[0] + VA.end - VA.off, R_H[1])
            PTA = mk("PTA", [128, 2, 5, 512], BF16, nslots=2, arena=ah)
            PTB = mk("PTB", [128, 2, NT, 512], BF16, nslots=2, arena=ah)
            mlo = mk("mlo", [128, 128], BF16, arena=ah)
            mhi = mk("mhi", [128, 128], BF16, arena=ah)
            esink = mk("esink", [128, 8], F32, arena=ah)
            lqk = mk("lqk", [128, 4, 64], F32, arena=ah)
            lam = mk("lam", [128, 8], F32, arena=ah)
            gsub = mk("gsub", [128, 128], F32, arena=ah)
            tmpy = [mk("tmpy%d" % i, [128, 128], F32, arena=ah) for i in range(2)]
            yb = [mk("yb%d" % i, [128, 128], F32, arena=ah) for i in range(2)]
            rr = mk("rr", [128, 8, 8], F32, nslots=8, arena=ah)
            dma("pool", mlo[:], maskd[0, :, :], w=[mlo.b[0]])
            dma("pool", mhi[:], maskd[1, :, :], w=[mhi.b[0]])
            dma("sp", esink[:], sinkd[:, :], w=[esink.b[0]])
            op("act", lambda e: e.activation(out=esink[:], in_=esink[:], func=AF.Exp), r=[esink.b[0]], w=[esink.b[0]])
            dma("sp", lqk[:], lqkd[:, :, :], w=[lqk.b[0]])
            dma("sp", gsub[:], sublnd[:, :], w=[gsub.b[0]])
            lam_init = 0.8 - 0.6 * math.exp(-0.3 * 0)
            op("dve", lambda e: e.tensor_scalar(gsub[:], gsub[:], 1.0 - lam_init, None, ALU.mult),
               r=[gsub.b[0]], w=[gsub.b[0]])
            op("dve", lambda e: e.tensor_tensor(out=tmpy[0].ap(0, [[64, 2], [1, 64]]), in0=lqk.ap(0, [[128, 2], [1, 64]]),
                                                in1=lqk.ap(64, [[128, 2], [1, 64]]), op=ALU.mult),
               r=[lqk.b[0]], w=[tmpy[0].b[0]])
            op("dve", lambda e: e.tensor_reduce(out=lam[:, 0:2], in_=tmpy[0].ap(0, [[64, 2], [1, 64]]), axis=AX.X,
                                                op=ALU.add), r=[tmpy[0].b[0]], w=[lam.b[0]])
            op("act", lambda e: e.activation(out=lam[:, 0:2], in_=lam[:, 0:2], func=AF.Exp), r=[lam.b[0]], w=[lam.b[0]])
            op("dve", lambda e: e.scalar_tensor_tensor(out=lam[:, 2:3], in0=lam[:, 1:2], scalar=-lam_init,
                                                       in1=lam[:, 0:1], op0=ALU.add, op1=ALU.subtract),
               r=[lam.b[0]], w=[lam.b[0]])

            sbank = [0]

            def next_sbank():
                b = sbank[0] % 5
                sbank[0] += 1
                return b

            itA = 0
            for kvh in range(2):
                p0 = 64 * kvh
                for qb in range(NT):
                    if qb < 2:
                        keys = [(0, None), (1, None)]
                    else:
                        keys = [(0, None), (1, None)]
                        if qb - 1 >= 2:
                            keys.append((qb - 1, mlo))
                        keys.append((qb, None))
                        if qb + 1 < NT:
                            keys.append((qb + 1, mhi))
                    sl = itA % 2
                    itA += 1
                    for j, (kt, msk) in enumerate(keys):
                        bk = next_sbank()
                        op("pe", lambda e, bk=bk, kt=kt, qb=qb, p0=p0: e.matmul(
                            PS[bk][:, :], lhsT=QKT.ap(4 * NTOK + kt * 128, [[1, 128]], p0=p0, np_=64),
                            rhs=QKT.ap(qb * 128, [[NTOK, 4], [1, 128]], p0=p0, np_=64), start=True, stop=True),
                           r=[QKT.b[kt], QKT.b[qb]], w=[PB[bk]])
                        pt_ap = PTA.ap((sl * 5 + j) * 512, [[1, 512]])
                        op("act", lambda e, bk=bk, pt_ap=pt_ap: e.activation(
                            out=pt_ap, in_=PS[bk][:, :], func=AF.Exp, scale=0.125), r=[PB[bk]], w=[PTA.b[sl]])
                        if msk is not None:
                            op("dve", lambda e, sl=sl, j=j, msk=msk: e.tensor_tensor(
                                out=PTA.ap((sl * 5 + j) * 512, [[128, 4], [1, 128]]),
                                in0=PTA.ap((sl * 5 + j) * 512, [[128, 4], [1, 128]]),
                                in1=msk.ap(0, [[0, 4], [1, 128]]), op=ALU.mult),
                               r=[PTA.b[sl], msk.b[0]], w=[PTA.b[sl]])
                    ob = 5 + (itA % 2)
                    for hq in range(4):
                        for j, (kt, msk) in enumerate(keys):
                            op("pe", lambda e, hq=hq, j=j, kt=kt, sl=sl, ob=ob, kvh=kvh: e.matmul(
                                PS[ob][:, hq * 65:(hq + 1) * 65],
                                lhsT=PTA.ap((sl * 5 + j) * 512 + hq * 128, [[1, 128]]),
                                rhs=VA.ap(kt * 130 + kvh * 65, [[1, 65]]),
                                start=(j == 0), stop=(j == len(keys) - 1)),
                               r=[PTA.b[sl], VA.b[kt]], w=[PB[ob]])
                    ri = stat_slot()
                    den = rr.ap(ri * 8, [[1, 4]])
                    op("dve", lambda e, ob=ob, den=den, kvh=kvh: e.tensor_tensor(
                        out=den, in0=pap(PS[ob], 64, [[65, 4]]), in1=esink[:, kvh * 4:kvh * 4 + 4], op=ALU.add),
                       r=[PB[ob], esink.b[0]], w=[rr.b[ri]])
                    op("dve", lambda e, den=den: e.reciprocal(den, den), r=[rr.b[ri]], w=[rr.b[ri]])
                    op("dve", lambda e, ob=ob, ri=ri, qb=qb, kvh=kvh: e.tensor_tensor(
                        out=y.ap(qb * 1024 + kvh * 256, [[64, 4], [1, 64]]), in0=pap(PS[ob], 0, [[65, 4], [1, 64]]),
                        in1=rr.ap(ri * 8, [[1, 4], [0, 64]]), op=ALU.mult),
                       r=[PB[ob], rr.b[ri]], w=[y.b[qb]])

            itB = 0
            groups = [([0, 1], [0, 1])] + [([2 + 4 * g + i for i in range(4)], list(range(NT))) for g in range(4)]
            for h in range(4):
                sq_, sk_ = 5 + 2 * h, 6 + 2 * h
                for (qbs, keys) in groups:
                    nq = len(qbs) * 128
                    q0 = qbs[0] * 128
                    for m in range(2):
                        for j, kt in enumerate(keys):
                            bk = next_sbank()
                            op("pe", lambda e, bk=bk, kt=kt, m=m, nq=nq, q0=q0, sq_=sq_, sk_=sk_: e.matmul(
                                PS[bk][:, 0:nq], lhsT=QKT.ap(sk_ * NTOK + kt * 128, [[1, 128]], p0=64 * m, np_=64),
                                rhs=QKT.ap(sq_ * NTOK + q0, [[1, nq]], p0=64 * m, np_=64), start=True, stop=True),
                               r=[QKT.b[kt]] + [QKT.b[q] for q in qbs], w=[PB[bk]])
                            op("act", lambda e, bk=bk, m=m, j=j, nq=nq: e.activation(
                                out=PTB.ap((m * NT + j) * 512, [[1, nq]]), in_=PS[bk][:, 0:nq], func=AF.Exp,
                                scale=0.125), r=[PB[bk]], w=[PTB.b[m]])
                    for qi, qb in enumerate(qbs):
                        ob = 5 + (itB % 3)
                        itB += 1
                        for m in range(2):
                            for j, kt in enumerate(keys):
                                op("pe", lambda e, ob=ob, m=m, j=j, kt=kt, qi=qi, h=h: e.matmul(
                                    PS[ob][:, m * 129:(m + 1) * 129],
                                    lhsT=PTB.ap((m * NT + j) * 512 + qi * 128, [[1, 128]]),
                                    rhs=VB.ap(kt * 516 + h * 129, [[1, 129]]),
                                    start=(j == 0), stop=(j == len(keys) - 1)),
                                   r=[PTB.b[m], VB.b[kt]], w=[PB[ob]])
                        ri = stat_slot()
                        r12 = rr.ap(ri * 8, [[1, 2]])
                        op("dve", lambda e, ob=ob, r12=r12: e.reciprocal(r12, pap(PS[ob], 128, [[129, 2]])),
                           r=[PB[ob]], w=[rr.b[ri]])
                        op("dve", lambda e, ri=ri: e.tensor_tensor(
                            out=rr.ap(ri * 8 + 1, [[1, 1]]), in0=rr.ap(ri * 8 + 1, [[1, 1]]), in1=lam[:, 2:3],
                            op=ALU.mult), r=[rr.b[ri], lam.b[0]], w=[rr.b[ri]])
                        t_ = tmpy[itB % 2]
                        y_ = yb[itB % 2]
                        op("dve", lambda e, ob=ob, t_=t_, ri=ri: e.tensor_scalar(
                            t_[:], PS[ob][:, 0:128], rr.ap(ri * 8, [[1, 1]]), None, ALU.mult),
                           r=[PB[ob], rr.b[ri]], w=[t_.b[0]])
                        op("dve", lambda e, ob=ob, t_=t_, y_=y_, ri=ri: e.scalar_tensor_tensor(
                            out=y_[:], in0=PS[ob][:, 129:257], scalar=rr.ap(ri * 8 + 1, [[1, 1]]), in1=t_[:],
                            op0=ALU.mult, op1=ALU.add), r=[PB[ob], rr.b[ri], t_.b[0]], w=[y_.b[0]])
                        ssq = rr.ap(ri * 8 + 2, [[1, 1]])
                        op("act", lambda e, y_=y_, ssq=ssq: e.activation(
                            out=junk[:, 0:128], in_=y_[:], func=AF.Square, accum_out=ssq),
                           r=[y_.b[0]], w=[junk.b[0], rr.b[ri]])
                        rstd_from_ss(ssq, ssq, 128.0, [rr.b[ri]], [rr.b[ri]])
                        op("dve", lambda e, y_=y_, ssq=ssq, qb=qb, h=h: e.scalar_tensor_tensor(
                            out=y.ap(qb * 1024 + 512 + h * 128, [[1, 128]]), in0=y_[:], scalar=ssq, in1=gsub[:],
                            op0=ALU.mult, op1=ALU.mult), r=[y_.b[0], rr.b[ri], gsub.b[0]], w=[y.b[qb]])
            kb.barrier()

        def rows_ap(dram, tt, cmajor):
            base = 0 if dram.shape[0] == NTOK else -256
            if tt < 2 or not cmajor:
                r0 = tt * 128 + base
                return dram[r0:r0 + 128, :]
            pt = tt - 2
            ct, t = pt // 8, pt % 8
            r0 = 256 + base + ct * 1024 + t
            return AP(dram.tensor, r0 * 1024, [[8 * 1024, 128], [1, 1024]])

        def phase_outproj(y, hT, wo_dram, resid_dram, tiles, cmajor=False):
            a = Arena(*R_B)
            wo = mk("wo", [128, 8, 1024], BF16, arena=a)
            wog = [mk("wog%d" % i, [128, 8, 1024], BF16, arena=a) for i in range(2)]
            yT = [mk("yT%d" % i, [128, 8, 128], BF16, arena=a) for i in range(2)]
            xs = [mk("ox%d" % i, [128, 1024], F32, arena=a) for i in range(3)]
            for kc in range(8):
                dma("pool", wo[:, kc, :], wo_dram[kc * 128:(kc + 1) * 128, :], w=[wo.b[0]])
            rs = sorted(set(1 if tt < 2 else 0 for tt in tiles))
            for r_ in rs:
                for half in range(2):
                    op("dve", lambda e, r_=r_, half=half: e.tensor_tensor(
                        out=wog[r_].ap(half * 512, [[1024, 8], [1, 512]]), in0=wo.ap(half * 512, [[1024, 8], [1, 512]]),
                        in1=gate_rep.ap((0 * 2 + r_) * 1024 + half * 512, [[0, 8], [1, 512]]), op=ALU.mult),
                       r=[wo.b[0], gate_rep.b[r_]], w=[wog[r_].b[0]])
            for it, tt in enumerate(tiles):
                r_ = 1 if tt < 2 else 0
                x = xs[it % 3]
                dma("sp", x[:], rows_ap(resid_dram, tt, cmajor), w=[x.b[0]])
                yt_ = yT[it % 2]
                for half in range(2):
                    bank = 5 + ((it * 2 + half) % 3)
                    for c4 in range(4):
                        c = half * 4 + c4
                        op("pe", lambda e, c=c, c4=c4, bank=bank, tt=tt: e.matmul(
                            PS[bank][:, c4 * 128:(c4 + 1) * 128], lhsT=y[:, tt, c * 128:(c + 1) * 128], rhs=ident_b[:],
                            start=True, stop=True), r=[y.b[tt], ident_b.b[0]], w=[PB[bank]])
                    eng = "act" if half == 0 else "dve"
                    if eng == "act":
                        op("act", lambda e, half=half, bank=bank: e.activation(
                            out=yt_.ap(half * 512, [[1, 512]]), in_=PS[bank][:, :], func=AF.Identity),
                           r=[PB[bank]], w=[yt_.b[0]])
                    else:
                        op("dve", lambda e, half=half, bank=bank: e.tensor_copy(
                            out=yt_.ap(half * 512, [[1, 512]]), in_=PS[bank][:, :]), r=[PB[bank]], w=[yt_.b[0]])
                for half in range(2):
                    bank = (it * 2 + half) % 4
                    for kc in range(8):
                        op("pe", lambda e, kc=kc, half=half, bank=bank, r_=r_: e.matmul(
                            PS[bank][:, :], lhsT=yt_[:, kc, :], rhs=wog[r_][:, kc, half * 512:(half + 1) * 512],
                            start=(kc == 0), stop=(kc == 7)), r=[yt_.b[0], wog[r_].b[0]], w=[PB[bank]])
                    op("dve", lambda e, half=half, bank=bank, tt=tt: e.tensor_tensor(
                        out=hT[:, tt, half * 512:(half + 1) * 512], in0=PS[bank][:, :],
                        in1=x[:, half * 512:(half + 1) * 512], op=ALU.add), r=[PB[bank], x.b[0]], w=[hT.b[tt]])
            kb.barrier()

        def phase_mlp(li, fT, hT, blocks):
            a = Arena(*R_B)
            w1s = [mk("w1s%d" % i, [128, 8, 512], BF16, arena=a) for i in range(2)]
            w2s = [mk("w2s%d" % i, [128, 4, 1024], BF16, arena=a) for i in range(2)]
            w2g = [[mk("w2g%d_%d" % (i, r_), [128, 4, 1024], BF16, arena=a) for r_ in range(2)] for i in range(2)]
            hid = [mk("hid%d" % i, [128, 4, 512], BF16, arena=a) for i in range(2)]
            rl = [mk("rl%d" % i, [128, 512], F32, arena=a) for i in range(2)]
            rs = sorted(set(b[2] for b in blocks))

            def load_w(hg):
                w1_, w2_ = w1s[hg % 2], w2s[hg % 2]
                for kc in range(8):
                    dma("pool", w1_[:, kc, :], w1d[li, kc * 128:(kc + 1) * 128, hg * 512:(hg + 1) * 512], w=[w1_.b[0]])
                for hc in range(4):
                    r0 = hg * 512 + hc * 128
                    dma("pool", w2_[:, hc, :], w2d[li, r0:r0 + 128, :], w=[w2_.b[0]])
                for r_ in rs:
                    for half in range(2):
                        op("dve", lambda e, r_=r_, half=half, w2_=w2_, hg=hg: e.tensor_tensor(
                            out=w2g[hg % 2][r_].ap(half * 512, [[1024, 4], [1, 512]]),
                            in0=w2_.ap(half * 512, [[1024, 4], [1, 512]]),
                            in1=gate_rep.ap((1 * 2 + r_) * 1024 + half * 512, [[0, 4], [1, 512]]), op=ALU.mult),
                           r=[w2_.b[0], gate_rep.b[2 + r_]], w=[w2g[hg % 2][r_].b[0]])

            ir = [0]

            def emit_hidden(k, hg, t0, ntl, r_):
                ntok = ntl * 128
                hd = hid[k % 2]
                w1_ = w1s[hg % 2]
                for hc in range(4):
                    bank = hc
                    for kc in range(8):
                        op("pe", lambda e, kc=kc, hc=hc, bank=bank, t0=t0, ntok=ntok, w1_=w1_: e.matmul(
                            PS[bank][:, 0:ntok], lhsT=w1_[:, kc, hc * 128:(hc + 1) * 128],
                            rhs=fT[:, kc, t0 * 128:t0 * 128 + ntok], start=(kc == 0), stop=(kc == 7)),
                           r=[w1_.b[0]] + [fT.b[t0 + i] for i in range(ntl)], w=[PB[bank]])
                    rl_ = rl[ir[0] % 2]
                    ir[0] += 1
                    op("act", lambda e, bank=bank, ntok=ntok, rl_=rl_: e.activation(
                        out=rl_[:, 0:ntok], in_=PS[bank][:, 0:ntok], func=AF.Relu), r=[PB[bank]], w=[rl_.b[0]])
                    op("act", lambda e, hc=hc, ntok=ntok, rl_=rl_, hd=hd: e.activation(
                        out=hd[:, hc, 0:ntok], in_=rl_[:, 0:ntok], func=AF.Square),
                       r=[rl_.b[0]], w=[hd.b[0]])

            def emit_out(k, hg, t0, ntl, r_):
                hd = hid[k % 2]
                for i in range(ntl):
                    tt = t0 + i
                    for half in range(2):
                        bank = 4 + ((i * 2 + half) % 4)
                        for hc in range(4):
                            op("pe", lambda e, hc=hc, half=half, bank=bank, i=i, hd=hd, r_=r_, hg=hg: e.matmul(
                                PS[bank][:, :], lhsT=hd[:, hc, i * 128:(i + 1) * 128],
                                rhs=w2g[hg % 2][r_][:, hc, half * 512:(half + 1) * 512],
                                start=(hc == 0), stop=(hc == 3)), r=[hd.b[0], w2g[hg % 2][r_].b[0]], w=[PB[bank]])
                        op("dve", lambda e, half=half, bank=bank, tt=tt: e.tensor_tensor(
                            out=hT[:, tt, half * 512:(half + 1) * 512], in0=PS[bank][:, :],
                            in1=hT[:, tt, half * 512:(half + 1) * 512], op=ALU.add),
                           r=[PB[bank], hT.b[tt]], w=[hT.b[tt]])

            work = [(hg,) + tuple(b) for hg in range(8) for b in blocks]
            load_w(0)
            loaded = 0
            emit_hidden(0, *work[0])
            for k in range(len(work)):
                if k + 1 < len(work):
                    nhg = work[k + 1][0]
                    if nhg > loaded:
                        load_w(nhg)
                        loaded = nhg
                    emit_hidden(k + 1, *work[k + 1])
                emit_out(k, *work[k])
            kb.barrier()

        def phase_s5(aT, gc):
            INV2PI = 1.0 / (2.0 * math.pi)
            NCH = 288
            sa = Arena(R_H[1] - 4096, R_H[1])
            lamn = mk("lamn", [128, 2, 3, 32], F32, arena=sa)
            ktab = mk("ktab", [128, 6, 8], F32, arena=sa)
            rhoe = mk("rhoe", [128, 2, 32], F32, arena=sa)
            omg = mk("omg", [128, 2, 32], F32, arena=sa)
            ff = mk("ff", [128, 2, 2, 32], F32, arena=sa)
            A12 = mk("A12", [128, 2, 2, 64], F32, arena=sa)
            tsm = mk("tsm", [128, 4, 32], F32, arena=sa)
            for d_ in range(2):
                for w_ in range(3):
                    dma("sp", lamn[:, d_, w_, :], lamnd[d_, w_, :, :], w=[lamn.b[0]])
            dma("sp", ktab[:], ktabd[:, :, :], w=[ktab.b[0]])
            op("act", lambda e: e.activation(out=lamn.ap(64, [[96, 2], [1, 32]]), in_=lamn.ap(64, [[96, 2], [1, 32]]),
                                             func=AF.Exp), r=[lamn.b[0]], w=[lamn.b[0]])
            op("dve", lambda e: e.tensor_tensor(out=rhoe[:], in0=lamn.ap(0, [[96, 2], [1, 32]]),
                                                in1=lamn.ap(64, [[96, 2], [1, 32]]), op=ALU.mult),
               r=[lamn.b[0]], w=[rhoe.b[0]])
            op("dve", lambda e: e.scalar_tensor_tensor(out=omg[:], in0=lamn.ap(32, [[96, 2], [1, 32]]), scalar=INV2PI,
                                                       in1=lamn.ap(64, [[96, 2], [1, 32]]), op0=ALU.mult, op1=ALU.mult),
               r=[lamn.b[0]], w=[omg.b[0]])

            def mk_scr(arena, name):
                return (mk(name + "x", [128, 8, 32], F32, arena=arena), mk(name + "xi", [128, 8, 32], I32, arena=arena),
                        mk(name + "t1", [128, 8, 32], F32, arena=arena), mk(name + "t2", [128, 8, 32], F32, arena=arena))

            def cpow(arena, d_, krow, name, scr):
                re_ = mk(name + "re", [128, 8, 32], F32, arena=arena)
                im_ = mk(name + "im", [128, 8, 32], F32, arena=arena)
                x_, xi_ = scr[0], scr[1]
                kap = ktab.ap(krow * 8, [[1, 8], [0, 32]])
                op("dve", lambda e: e.tensor_tensor(out=re_[:], in0=rhoe.ap(d_ * 32, [[0, 8], [1, 32]]), in1=kap,
                                                    op=ALU.mult), r=[rhoe.b[0], ktab.b[0]], w=[re_.b[0]])
                op("act", lambda e: e.activation(out=re_[:], in_=re_[:], func=AF.Exp), r=[re_.b[0]], w=[re_.b[0]])
                op("dve", lambda e: e.tensor_tensor(out=x_[:], in0=omg.ap(d_ * 32, [[0, 8], [1, 32]]), in1=kap,
                                                    op=ALU.mult), r=[omg.b[0], ktab.b[0]], w=[x_.b[0]])
                op("dve", lambda e: e.tensor_copy(out=xi_[:], in_=x_[:]), r=[x_.b[0]], w=[xi_.b[0]])
                op("dve", lambda e: e.tensor_tensor(out=x_[:], in0=x_[:], in1=xi_[:], op=ALU.subtract),
                   r=[x_.b[0], xi_.b[0]], w=[x_.b[0]])
                op("act", lambda e: e.activation(out=im_[:], in_=x_[:], func=AF.Sin, scale=2.0 * math.pi),
                   r=[x_.b[0]], w=[im_.b[0]])
                op("act", lambda e: e.activation(out=x_[:], in_=x_[:], func=AF.Sin, scale=math.pi),
                   r=[x_.b[0]], w=[x_.b[0]])
                op("act", lambda e: e.activation(out=x_[:], in_=x_[:], func=AF.Square), r=[x_.b[0]], w=[x_.b[0]])
                op("dve", lambda e: e.tensor_scalar(x_[:], x_[:], -2.0, 1.0, ALU.mult, ALU.add), r=[x_.b[0]], w=[x_.b[0]])
                op("dve", lambda e: e.tensor_tensor(out=im_[:], in0=im_[:], in1=re_[:], op=ALU.mult),
                   r=[im_.b[0], re_.b[0]], w=[im_.b[0]])
                op("dve", lambda e: e.tensor_tensor(out=re_[:], in0=re_[:], in1=x_[:], op=ALU.mult),
                   r=[re_.b[0], x_.b[0]], w=[re_.b[0]])
                return re_, im_

            def cmul_f(d_, re_, im_, scr):
                t1, t2 = scr[2], scr[3]
                fr = ff.ap((d_ * 2 + 0) * 32, [[0, 8], [1, 32]])
                fi = ff.ap((d_ * 2 + 1) * 32, [[0, 8], [1, 32]])
                op("dve", lambda e: e.tensor_tensor(out=t1[:], in0=re_[:], in1=fi, op=ALU.mult),
                   r=[re_.b[0], ff.b[0]], w=[t1.b[0]])
                op("dve", lambda e: e.tensor_tensor(out=t2[:], in0=im_[:], in1=fi, op=ALU.mult),
                   r=[im_.b[0], ff.b[0]], w=[t2.b[0]])
                op("dve", lambda e: e.tensor_tensor(out=re_[:], in0=re_[:], in1=fr, op=ALU.mult),
                   r=[re_.b[0], ff.b[0]], w=[re_.b[0]])
                op("dve", lambda e: e.tensor_tensor(out=im_[:], in0=im_[:], in1=fr, op=ALU.mult),
                   r=[im_.b[0], ff.b[0]], w=[im_.b[0]])
                op("dve", lambda e: e.tensor_tensor(out=re_[:], in0=re_[:], in1=t2[:], op=ALU.subtract),
                   r=[re_.b[0], t2.b[0]], w=[re_.b[0]])
                op("dve", lambda e: e.tensor_tensor(out=im_[:], in0=im_[:], in1=t1[:], op=ALU.add),
                   r=[im_.b[0], t1.b[0]], w=[im_.b[0]])

            U = T(kb, "U", [128, 64, NCH], BF16, R_H[0], nslots=64)
            WS = [T(kb, "WS%d" % d_, [128, 64, 2, 64], BF16, U.end + d_ * 16384, nslots=1) for d_ in range(2)]
            assert WS[1].end <= R_H[1] - 4096
            a = Arena(*R_B)
            wsi = mk("wsi", [128, 8, 1024], BF16, arena=a)
            Xc = [mk("Xc%d" % i, [128, 64, 8, 16], BF16, arena=a) for i in range(2)]
            scrU = mk_scr(a, "scrU")
            pw0 = [cpow(a, d_, 0, "pw0_%d" % d_, scrU) for d_ in range(2)]
            for d_ in range(2):
                re_, im_ = pw0[d_]
                lre = lamn[:, d_, 0, :]
                lim = lamn[:, d_, 1, :]
                a1r = re_[:, 0, :]
                a1i = im_[:, 0, :]
                nr, den, tA_, tB_ = tsm[:, 0, :], tsm[:, 1, :], tsm[:, 2, :], tsm[:, 3, :]
                rb = [re_.b[0], im_.b[0], lamn.b[0], tsm.b[0]]
                op("dve", lambda e, nr=nr, a1r=a1r: e.tensor_scalar(nr, a1r, -1.0, None, ALU.add), r=rb, w=[tsm.b[0]])
                op("dve", lambda e, tA_=tA_, lre=lre: e.tensor_tensor(out=tA_, in0=lre, in1=lre, op=ALU.mult),
                   r=rb, w=[tsm.b[0]])
                op("dve", lambda e, den=den, lim=lim: e.tensor_tensor(out=den, in0=lim, in1=lim, op=ALU.mult),
                   r=rb, w=[tsm.b[0]])
                op("dve", lambda e, den=den, tA_=tA_: e.tensor_tensor(out=den, in0=den, in1=tA_, op=ALU.add),
                   r=rb, w=[tsm.b[0]])
                op("dve", lambda e, den=den: e.reciprocal(den, den), r=rb, w=[tsm.b[0]])
                fr_ = ff[:, d_, 0, :]
                fi_ = ff[:, d_, 1, :]
                op("dve", lambda e, tA_=tA_, nr=nr, lre=lre: e.tensor_tensor(out=tA_, in0=nr, in1=lre, op=ALU.mult),
                   r=rb, w=[tsm.b[0]])
                op("dve", lambda e, tB_=tB_, a1i=a1i, lim=lim: e.tensor_tensor(out=tB_, in0=a1i, in1=lim, op=ALU.mult),
                   r=rb, w=[tsm.b[0]])
                op("dve", lambda e, tA_=tA_, tB_=tB_: e.tensor_tensor(out=tA_, in0=tA_, in1=tB_, op=ALU.add),
                   r=rb, w=[tsm.b[0]])
                op("dve", lambda e, fr_=fr_, tA_=tA_, den=den: e.tensor_tensor(out=fr_, in0=tA_, in1=den, op=ALU.mult),
                   r=rb, w=[ff.b[0]])
                op("dve", lambda e, tA_=tA_, a1i=a1i, lre=lre: e.tensor_tensor(out=tA_, in0=a1i, in1=lre, op=ALU.mult),
                   r=rb, w=[tsm.b[0]])
                op("dve", lambda e, tB_=tB_, nr=nr, lim=lim: e.tensor_tensor(out=tB_, in0=nr, in1=lim, op=ALU.mult),
                   r=rb, w=[tsm.b[0]])
                op("dve", lambda e, tA_=tA_, tB_=tB_: e.tensor_tensor(out=tA_, in0=tA_, in1=tB_, op=ALU.subtract),
                   r=rb, w=[tsm.b[0]])
                op("dve", lambda e, fi_=fi_, tA_=tA_, den=den: e.tensor_tensor(out=fi_, in0=tA_, in1=den, op=ALU.mult),
                   r=rb, w=[ff.b[0]])
                a8r = re_[:, 7, :]
                a8i = im_[:, 7, :]
                for hh in range(2):
                    op("dve", lambda e, hh=hh, a8r=a8r: e.tensor_copy(out=A12[:, d_, 0, hh * 32:(hh + 1) * 32], in_=a8r),
                       r=rb, w=[A12.b[0]])
                op("dve", lambda e, a8i=a8i: e.tensor_scalar(A12[:, d_, 1, 0:32], a8i, -1.0, None, ALU.mult),
                   r=rb, w=[A12.b[0]])
                op("dve", lambda e, a8i=a8i: e.tensor_copy(out=A12[:, d_, 1, 32:64], in_=a8i), r=rb, w=[A12.b[0]])
            ET = [mk("ET%d" % i, [128, 128], F32, arena=a) for i in range(2)]
            ie = 0
            for d_ in range(2):
                ere, eim = cpow(a, d_, 4 + d_, "E%d" % d_, scrU)
                cmul_f(d_, ere, eim, scrU)
                for part, src in enumerate((ere, eim)):
                    for b2 in range(2):
                        bank = 6 + (ie % 2)
                        et = ET[ie % 2]
                        ie += 1
                        op("pe", lambda e, src=src, b2=b2, bank=bank: e.transpose(
                            PS[bank][:, 0:128], src.ap(b2 * 128, [[1, 128]]), ident_f[:]),
                           r=[src.b[0], ident_f.b[0]], w=[PB[bank]])
                        op("act", lambda e, et=et, bank=bank: e.activation(out=et[:], in_=PS[bank][:, 0:128],
                                                                            func=AF.Identity), r=[PB[bank]], w=[et.b[0]])
                        for k4 in range(4):
                            s_ = b2 * 4 + k4
                            off = ((d_ * 2 + part) * 8 + s_) * 4096
                            dma("sp", AP(escr.tensor, off, [[64, 32], [2048, 2], [1, 64]]),
                                et.ap(0, [[64, 2], [1, 64]], p0=k4 * 32, np_=32), r=[et.b[0]])
            esc_b = Buf()
            for kc in range(8):
                dma("pool", wsi[:, kc, :], s5wind[kc * 128:(kc + 1) * 128, :], w=[wsi.b[0]])
            for ct in range(3):
                nm = 128 if ct < 2 else 32
                xc = Xc[ct % 2]
                for s_ in range(8):
                    for half in range(2):
                        bank = (s_ * 2 + half) % 4
                        for kc in range(8):
                            op("pe", lambda e, kc=kc, half=half, bank=bank, ct=ct, s_=s_, nm=nm: e.matmul(
                                PS[bank][0:nm, :], lhsT=aT.ap(kc * NTOK + ct * 1024 + s_, [[8, nm]]),
                                rhs=wsi[:, kc, half * 512:(half + 1) * 512], start=(kc == 0), stop=(kc == 7)),
                               r=aT.b + [wsi.b[0]], w=[PB[bank]])
                        eng = "act" if half == 0 else "dve"
                        dst = xc.ap(half * 32 * 128 + s_ * 16, [[128, 32], [1, 16]], np_=nm)
                        srcp = pap(PS[bank], 0, [[16, 32], [1, 16]], np_=nm)
                        if eng == "act":
                            op("act", lambda e, dst=dst, srcp=srcp: e.activation(out=dst, in_=srcp, func=AF.Identity),
                               r=[PB[bank]], w=[xc.b[0]])
                        else:
                            op("dve", lambda e, dst=dst, srcp=srcp: e.tensor_copy(out=dst, in_=srcp),
                               r=[PB[bank]], w=[xc.b[0]])
                for g4 in range(16):
                    bank = 4 + (g4 % 2)
                    for gi in range(4):
                        g = g4 * 4 + gi
                        op("pe", lambda e, g=g, gi=gi, bank=bank, xc=xc, nm=nm: e.matmul(
                            PS[bank][:, gi * 128:gi * 128 + nm], lhsT=xc.ap(g * 128, [[1, 128]], np_=nm),
                            rhs=ident_b[0:nm, 0:nm], start=True, stop=True),
                           r=[xc.b[0], ident_b.b[0]], w=[PB[bank]])
                    dstu = U.ap(g4 * 4 * NCH + ct * 128, [[NCH, 4], [1, nm]])
                    srcu = pap(PS[bank], 0, [[128, 4], [1, nm]])
                    if g4 % 2 == 0:
                        op("act", lambda e, dstu=dstu, srcu=srcu: e.activation(out=dstu, in_=srcu, func=AF.Identity),
                           r=[PB[bank]], w=U.b[g4 * 4:g4 * 4 + 4])
                    else:
                        op("dve", lambda e, dstu=dstu, srcu=srcu: e.tensor_copy(out=dstu, in_=srcu),
                           r=[PB[bank]], w=U.b[g4 * 4:g4 * 4 + 4])
            kb.barrier()

            a = Arena(*R_B)
            NH = 1024
            tabs = [[mk("ws_%d_%d" % (d_, i), [128, NH], F32, arena=a) for i in range(4)] for d_ in range(2)]
            tmp = [[mk("wt_%d_%d" % (d_, i), [128, NH], F32, arena=a) for i in range(1)] for d_ in range(2)]
            for hf in range(4):
                for d_ in range(2):
                    eng = "dve"
                    ere, eim, bre, bim = tabs[d_]
                    t1 = tmp[d_][0]
                    for part, et in enumerate((ere, eim)):
                        for s_ in range(8):
                            off = ((d_ * 2 + part) * 8 + s_) * 4096 + hf * NH
                            dma("sp", et.ap(0, [[1, NH]], p0=s_ * 16, np_=16),
                                AP(escr.tensor, off, [[0, 16], [1, NH]]), w=[et.b[0]])
                    dma("sp", bre[:], bsjd[d_, 0, :, hf * NH:(hf + 1) * NH], w=[bre.b[0]])
                    dma("sp", bim[:], bsjd[d_, 1, :, hf * NH:(hf + 1) * NH], w=[bim.b[0]])
                    wsd = WS[d_]
                    o_re = wsd.ap(hf * 16 * 128, [[128, 16], [1, 64]])
                    o_im = wsd.ap(hf * 16 * 128 + 64, [[128, 16], [1, 64]])
                    v = lambda t_: t_.ap(0, [[64, 16], [1, 64]])
                    op(eng, lambda e, t1=t1, ere=ere, bre=bre: e.tensor_tensor(out=t1[:], in0=ere[:], in1=bre[:],
                                                                               op=ALU.mult),
                       r=[ere.b[0], bre.b[0]], w=[t1.b[0]])
                    op(eng, lambda e, bre=bre, eim=eim: e.tensor_tensor(out=bre[:], in0=eim[:], in1=bre[:], op=ALU.mult),
                       r=[eim.b[0], bre.b[0]], w=[bre.b[0]])
                    op(eng, lambda e, ere=ere, bim=bim: e.tensor_tensor(out=ere[:], in0=ere[:], in1=bim[:], op=ALU.mult),
                       r=[ere.b[0], bim.b[0]], w=[ere.b[0]])
                    op(eng, lambda e, eim=eim, bim=bim: e.tensor_tensor(out=eim[:], in0=eim[:], in1=bim[:], op=ALU.mult),
                       r=[eim.b[0], bim.b[0]], w=[eim.b[0]])
                    op(eng, lambda e, t1=t1, eim=eim, o_re=o_re, v=v: e.tensor_tensor(out=o_re, in0=v(t1), in1=v(eim),
                                                                                       op=ALU.subtract),
                       r=[t1.b[0], eim.b[0]], w=[wsd.b[0]])
                    op(eng, lambda e, ere=ere, bre=bre, o_im=o_im, v=v: e.tensor_tensor(out=o_im, in0=v(ere), in1=v(bre),
                                                                                        op=ALU.add),
                       r=[ere.b[0], bre.b[0]], w=[wsd.b[0]])
            kb.barrier()

            a = Arena(*R_B)
            HSt = mk("HS", [128, 2, 64, 256], BF16, arena=a)
            HS = [None, None]
            ra = Arena(*R_A)
            Swt = mk("Sw", [128, 4, 64, 32], F32, nslots=4, arena=ra)
            Zt = mk("Z", [128, 2, 96], F32, arena=ra)
            T1t = mk("T1", [128, 2, 64], F32, arena=ra)
            T2t = mk("T2", [128, 2, 64], F32, arena=ra)
            op("dve", lambda e: e.memset(Zt[:], 0.0), w=[Zt.b[0]])
            worder = [list(range(9)), [0, 8, 7, 6, 5, 4, 3, 2, 1]]

            def s_window(d_, w_, slot):
                c0 = w_ * 32
                for idx in range(64):
                    part, gl = idx // 32, idx % 32
                    bank = d_ * 4 + idx // 16
                    col = (idx % 16) * 32
                    for gh in range(2):
                        g = gh * 32 + gl
                        op("pe", lambda e, bank=bank, col=col, gh=gh, g=g, part=part, d_=d_, c0=c0: e.matmul(
                            pap(PS[bank], col, [[1, 32]], p0=gh * 64, np_=64),
                            lhsT=WS[d_].ap(g * 128 + part * 64, [[1, 64]]), rhs=U.ap(g * NCH + c0, [[1, 32]]),
                            start=True, stop=True), r=[WS[d_].b[0], U.b[g]], w=[PB[bank]])
                for b4 in range(4):
                    bank = d_ * 4 + b4
                    op("act", lambda e, bank=bank, b4=b4, slot=slot: e.activation(
                        out=Swt.ap(slot * 2048 + b4 * 512, [[1, 512]]), in_=PS[bank][:, :], func=AF.Identity),
                       r=[PB[bank]], w=[Swt.b[slot]])

            for d_ in range(2):
                s_window(d_, worder[d_][0], d_ * 2 + 0)
            a1v = A12.ap(0, [[128, 2], [1, 64]])
            a2v = A12.ap(64, [[128, 2], [1, 64]])
            for wi in range(9):
                if wi + 1 < 9:
                    for d_ in range(2):
                        s_window(d_, worder[d_][wi + 1], d_ * 2 + (wi + 1) % 2)
                buf = wi % 2
                sbufs = [Swt.b[0 * 2 + buf], Swt.b[1 * 2 + buf]]
                for ci in range(32):
                    off_f = (0 * 2 + buf) * 2048 + ci
                    off_b = (1 * 2 + buf) * 2048 + 31 - ci
                    dstr = off_b - off_f
                    sc = Swt.ap(off_f, [[dstr, 2], [32, 64]])
                    sc_lo = Swt.ap(off_f, [[dstr, 2], [32, 32]])
                    op("dve", lambda e: e.tensor_tensor(out=T1t[:], in0=Zt.ap(0, [[96, 2], [1, 64]]), in1=a1v,
                                                        op=ALU.mult), r=[Zt.b[0], A12.b[0]], w=[T1t.b[0]])
                    op("dve", lambda e: e.tensor_tensor(out=T2t[:], in0=Zt.ap(32, [[96, 2], [1, 64]]), in1=a2v,
                                                        op=ALU.mult), r=[Zt.b[0], A12.b[0]], w=[T2t.b[0]])
                    op("dve", lambda e: e.tensor_tensor(out=T1t[:], in0=T1t[:], in1=T2t[:], op=ALU.add),
                       r=[T1t.b[0], T2t.b[0]], w=[T1t.b[0]])
                    op("dve", lambda e, sc=sc: e.tensor_tensor(out=Zt.ap(0, [[96, 2], [1, 64]]), in0=T1t[:], in1=sc,
                                                               op=ALU.add),
                       r=[T1t.b[0]] + sbufs, w=[Zt.b[0]])
                    op("dve", lambda e, sc_lo=sc_lo: e.tensor_tensor(
                        out=Zt.ap(64, [[96, 2], [1, 32]]), in0=T1t.ap(0, [[64, 2], [1, 32]]), in1=sc_lo, op=ALU.add),
                       r=[T1t.b[0]] + sbufs, w=[Zt.b[0]])
                    cf = worder[0][wi] * 32 + ci
                    cb = worder[1][wi] * 32 + 31 - ci
                    if wi == 0:
                        store = (ci == 31)
                        col_f, col_b = 0, 255
                    else:
                        store = not (wi == 8 and ci == 31)
                        col_f, col_b = cf + 1 - 32, cb - 1 - 32
                    if store:
                        op("act", lambda e, col_f=col_f, col_b=col_b: e.activation(
                            out=HSt.ap(col_f, [[64 * 256 + col_b - col_f, 2], [256, 64]]),
                            in_=Zt.ap(0, [[96, 2], [1, 64]]), func=AF.Identity), r=[Zt.b[0]], w=[HSt.b[0]])
            kb.barrier()

            NB = 4
            ha = Arena(U.end, R_H[1] - 4096)
            pwa = Arena(ha.take(8 * 1024), ha.top)
            pwa = Arena(pwa.lo, pwa.lo + 8 * 1024)
            sca = Arena(ha.top, R_H[1] - 4096)
            scrY = mk_scr(sca, "scrY")
            pwy = [cpow(pwa, d_, 0 + d_, "pwy%d" % d_, scrY) for d_ in range(2)]
            pwp = [cpow(pwa, d_, 2 + d_, "pwp%d" % d_, scrY) for d_ in range(2)]
            for d_ in range(2):
                cmul_f(d_, pwp[d_][0], pwp[d_][1], scrY)
            kb.barrier()
            sca = Arena(sca.lo, R_H[1] - 4096)
            cn = [[mk("cn%d_%d" % (d_, p_), [128, NB, 16], F32, arena=sca) for p_ in range(2)] for d_ in range(2)]
            bn = [[mk("bn%d_%d" % (d_, p_), [128, NB, 16], F32, arena=sca) for p_ in range(2)] for d_ in range(2)]
            WYb = [mk("WY%d" % d_, [128, 2, NB, 128], BF16, arena=sca) for d_ in range(2)]
            Pb = [mk("P%d" % d_, [128, 2, NB, 128], BF16, arena=sca) for d_ in range(2)]
            WT = mk("WT", [128, 2 * NB, 128], BF16, arena=sca)
            g1t = [mk("g1t%d" % d_, [128, NB * 128], F32, arena=sca) for d_ in range(2)]
            g2t = [mk("g2t%d" % d_, [128, NB * 128], F32, arena=sca) for d_ in range(2)]
            ba = Arena(a.top, R_B[1])
            m3 = mk("m3", [128, 3, 128], F32, arena=ba)
            dsk = mk("dsk", [128, 1024], F32, arena=ba)
            wtt = [mk("wtt%d" % i, [128, 512], F32, arena=ba) for i in range(1)]
            Yg = [mk("Yg%d" % i, [128, 256], F32, arena=ba) for i in range(2)]
            gx = [mk("gx%d" % i, [128, 512], F32, arena=ba) for i in range(2)]
            for i in range(3):
                dma("sp", m3[:, i, :], m3d[i, :, :], w=[m3.b[0]])
            dma("sp", dsk[:], dskd[:, :], w=[dsk.b[0]])
            C_G = 2.0 * math.sqrt(2.0 / math.pi)
            iy = 0
            for bt in range(32 // NB):
                gl0 = bt * NB
                for d_ in range(2):
                    for p_ in range(2):
                        dma("sp", cn[d_][p_][:], cnd[d_, p_, :, gl0:gl0 + NB, :], w=[cn[d_][p_].b[0]])
                        dma("sp", bn[d_][p_][:], bnd[d_, p_, :, gl0:gl0 + NB, :], w=[bn[d_][p_].b[0]])
                for d_ in range(2):
                    eng = "dve"
                    t1, t2 = g1t[d_], g2t[d_]
                    v4 = lambda t_: t_.ap(0, [[128, NB], [16, 8], [1, 16]])
                    for (dst, coef, pw_, is_wy) in ((WYb[d_], cn[d_], pwy[d_], True), (Pb[d_], bn[d_], pwp[d_], False)):
                        cre = coef[0].ap(0, [[16, NB], [0, 8], [1, 16]])
                        cim = coef[1].ap(0, [[16, NB], [0, 8], [1, 16]])
                        pre = pw_[0].ap(gl0, [[1, NB], [32, 8], [0, 16]])
                        pim = pw_[1].ap(gl0, [[1, NB], [32, 8], [0, 16]])
                        rr_ = [coef[0].b[0], coef[1].b[0], pw_[0].b[0], pw_[1].b[0]]
                        o0 = dst.ap(0, [[128, NB], [16, 8], [1, 16]])
                        o1 = dst.ap(NB * 128, [[128, NB], [16, 8], [1, 16]])
                        op(eng, lambda e, t1=t1, cre=cre, pre=pre, v4=v4: e.tensor_tensor(out=v4(t1), in0=cre, in1=pre,
                                                                                          op=ALU.mult),
                           r=rr_, w=[t1.b[0]])
                        op(eng, lambda e, t2=t2, cim=cim, pim=pim, v4=v4: e.tensor_tensor(out=v4(t2), in0=cim, in1=pim,
                                                                                          op=ALU.mult),
                           r=rr_, w=[t2.b[0]])
                        op(eng, lambda e, t1=t1, t2=t2, o0=o0, v4=v4: e.tensor_tensor(out=o0, in0=v4(t1), in1=v4(t2),
                                                                                       op=ALU.subtract),
                           r=[t1.b[0], t2.b[0]], w=[dst.b[0]])
                        op(eng, lambda e, t1=t1, cre=cre, pim=pim, v4=v4: e.tensor_tensor(out=v4(t1), in0=cre, in1=pim,
                                                                                          op=ALU.mult),
                           r=rr_ + [dst.b[0]], w=[t1.b[0]])
                        op(eng, lambda e, t2=t2, cim=cim, pre=pre, v4=v4: e.tensor_tensor(out=v4(t2), in0=cim, in1=pre,
                                                                                          op=ALU.mult),
                           r=rr_ + [dst.b[0]], w=[t2.b[0]])
                        if is_wy:
                            op(eng, lambda e, t1=t1, t2=t2: e.tensor_tensor(out=t1[:], in0=t1[:], in1=t2[:], op=ALU.add),
                               r=[t1.b[0], t2.b[0]], w=[t1.b[0]])
                            op(eng, lambda e, t1=t1, o1=o1, v4=v4: e.tensor_scalar(o1, v4(t1), -1.0, None, ALU.mult),
                               r=[t1.b[0]], w=[dst.b[0]])
                        else:
                            op(eng, lambda e, t1=t1, t2=t2, o1=o1, v4=v4: e.tensor_tensor(out=o1, in0=v4(t1),
                                                                                           in1=v4(t2), op=ALU.add),
                               r=[t1.b[0], t2.b[0]], w=[dst.b[0]])
                for gh in range(2):
                    bf_, bb_ = 0 + gh * 2, 1 + gh * 2
                    for gll in range(NB):
                        for d_, bank in ((0, bf_), (1, bb_)):
                            for part in range(2):
                                op("pe", lambda e, gh=gh, gll=gll, d_=d_, bank=bank, part=part: e.matmul(
                                    PS[bank][:, gll * 128:(gll + 1) * 128],
                                    lhsT=Pb[d_].ap((part * NB + gll) * 128, [[1, 128]], p0=gh * 64, np_=64),
                                    rhs=WYb[d_].ap((part * NB + gll) * 128, [[1, 128]], p0=gh * 64, np_=64),
                                    start=(part == 0), stop=(part == 1)),
                                   r=[Pb[d_].b[0], WYb[d_].b[0]], w=[PB[bank]])
                    wt_ = wtt[0]
                    gbase = gh * 32 + gl0
                    mf = m3.ap(0, [[0, NB], [1, 128]])
                    mb = m3.ap(128, [[0, NB], [1, 128]])
                    op("dve", lambda e, wt_=wt_, bf_=bf_, mf=mf: e.tensor_tensor(
                        out=wt_.ap(0, [[128, NB], [1, 128]]), in0=pap(PS[bf_], 0, [[128, NB], [1, 128]]), in1=mf,
                        op=ALU.mult), r=[PB[bf_], m3.b[0]], w=[wt_.b[0]])
                    op("dve", lambda e, bb_=bb_, mb=mb: e.tensor_tensor(
                        out=gx[0].ap(0, [[128, NB], [1, 128]]), in0=pap(PS[bb_], 0, [[128, NB], [1, 128]]), in1=mb,
                        op=ALU.mult), r=[PB[bb_], m3.b[0]], w=[gx[0].b[0]])
                    op("dve", lambda e, wt_=wt_: e.tensor_tensor(out=wt_[:], in0=wt_[:], in1=gx[0][:], op=ALU.add),
                       r=[wt_.b[0], gx[0].b[0]], w=[wt_.b[0]])
                    op("dve", lambda e, gbase=gbase: e.tensor_tensor(
                        out=gx[0].ap(0, [[128, NB], [16, 8], [1, 16]]),
                        in0=dsk.ap(gbase * 16, [[16, NB], [0, 8], [1, 16]]),
                        in1=m3.ap(256, [[0, NB], [16, 8], [1, 16]]), op=ALU.mult),
                       r=[dsk.b[0], m3.b[0]], w=[gx[0].b[0]])
                    op("dve", lambda e, wt_=wt_, gh=gh: e.tensor_tensor(
                        out=WT.ap(gh * NB * 128, [[1, NB * 128]]), in0=wt_[:], in1=gx[0][:], op=ALU.add),
                       r=[wt_.b[0], gx[0].b[0]], w=[WT.b[0]])
                for gidx in range(2 * NB):
                    gh, gll = gidx // NB, gidx % NB
                    gl = gl0 + gll
                    wslot = gh * NB + gll
                    g = gh * 32 + gl
                    yb_ = 4 + (iy % 2)
                    yg = Yg[iy % 2]
                    op("pe", lambda e, yb_=yb_, wslot=wslot, g=g: e.matmul(
                        PS[yb_][:, 0:256], lhsT=WT.ap(wslot * 128, [[1, 128]]), rhs=U.ap(g * NCH + 32, [[1, 256]]),
                        start=True, stop=False), r=[WT.b[0], U.b[g]], w=[PB[yb_]])
                    k_ = 0
                    for d_ in range(2):
                        for part in range(2):
                            k_ += 1
                            op("pe", lambda e, yb_=yb_, d_=d_, part=part, gh=gh, gll=gll, gl=gl, k_=k_: e.matmul(
                                PS[yb_][:, 0:256],
                                lhsT=WYb[d_].ap((part * NB + gll) * 128, [[1, 128]], p0=gh * 64, np_=64),
                                rhs=HSt.ap(((d_ * 2 + part) * 32 + gl) * 256, [[1, 256]], p0=gh * 64, np_=64),
                                start=False, stop=(k_ == 4)), r=[WYb[d_].b[0], HSt.b[0]], w=[PB[yb_]])
                    op("act", lambda e, yb_=yb_, yg=yg: e.activation(out=yg[:], in_=PS[yb_][:, 0:256], func=AF.Identity),
                       r=[PB[yb_]], w=[yg.b[0]])
                    tb_ = 6 + (iy % 2)
                    gx_ = gx[1]
                    for ct in range(2):
                        op("pe", lambda e, tb_=tb_, ct=ct, yg=yg: e.transpose(
                            PS[tb_][:, ct * 128:(ct + 1) * 128], yg[:, ct * 128:(ct + 1) * 128], ident_f[:]),
                           r=[yg.b[0], ident_f.b[0]], w=[PB[tb_]])
                    xps = PS[tb_][:, 0:256]
                    op("act", lambda e, xps=xps, gx_=gx_: e.activation(out=gx_[:, 0:256], in_=xps, func=AF.Square),
                       r=[PB[tb_]], w=[gx_.b[0]])
                    op("dve", lambda e, gx_=gx_: e.tensor_scalar(gx_[:, 0:256], gx_[:, 0:256], 0.044715, 1.0, ALU.mult,
                                                                 ALU.add), r=[gx_.b[0]], w=[gx_.b[0]])
                    op("dve", lambda e, xps=xps, gx_=gx_: e.tensor_tensor(out=gx_[:, 0:256], in0=xps, in1=gx_[:, 0:256],
                                                                          op=ALU.mult), r=[PB[tb_], gx_.b[0]],
                       w=[gx_.b[0]])
                    op("act", lambda e, gx_=gx_: e.activation(out=gx_[:, 0:256], in_=gx_[:, 0:256], func=AF.Sigmoid,
                                                              scale=C_G), r=[gx_.b[0]], w=[gx_.b[0]])
                    for ct in range(2):
                        op("dve", lambda e, ct=ct, g=g, tb_=tb_, gx_=gx_: e.tensor_tensor(
                            out=gc.ap((2 + ct * 8) * 1024 + g * 16, [[1024, 8], [1, 16]]),
                            in0=pap(PS[tb_], ct * 128, [[16, 8], [1, 16]]),
                            in1=gx_.ap(ct * 128, [[16, 8], [1, 16]]), op=ALU.mult),
                           r=[PB[tb_], gx_.b[0]], w=gc.b[2 + ct * 8:2 + ct * 8 + 8])
                    iy += 1
            kb.barrier()

            a = Arena(*R_B)
            gw = mk("gw", [128, 8, 1024], BF16, arena=a)
            gb = mk("gb", [1, 1024], F32, arena=a)
            gT = [mk("gT%d" % i, [128, 8, 128], BF16, arena=a) for i in range(2)]
            sg = [mk("sg%d" % i, [128, 512], F32, arena=a) for i in range(2)]
            for kc in range(8):
                dma("pool", gw[:, kc, :], gluwd[kc * 128:(kc + 1) * 128, :], w=[gw.b[0]])
            dma("sp", gb[:], glubd[:, :], w=[gb.b[0]])
            isg = 0
            for pt in range(16):
                tt = 2 + pt
                gt = gT[pt % 2]
                for half in range(2):
                    bank = 4 + ((pt * 2 + half) % 4)
                    for c4 in range(4):
                        c = half * 4 + c4
                        op("pe", lambda e, c=c, c4=c4, bank=bank, tt=tt: e.matmul(
                            PS[bank][:, c4 * 128:(c4 + 1) * 128], lhsT=gc[:, tt, c * 128:(c + 1) * 128], rhs=ident_b[:],
                            start=True, stop=True), r=[gc.b[tt], ident_b.b[0]], w=[PB[bank]])
                    if half == 0:
                        op("act", lambda e, half=half, bank=bank, gt=gt: e.activation(
                            out=gt.ap(half * 512, [[1, 512]]), in_=PS[bank][:, :], func=AF.Identity),
                           r=[PB[bank]], w=[gt.b[0]])
                    else:
                        op("dve", lambda e, half=half, bank=bank, gt=gt: e.tensor_copy(
                            out=gt.ap(half * 512, [[1, 512]]), in_=PS[bank][:, :]), r=[PB[bank]], w=[gt.b[0]])
                for half in range(2):
                    bank = (pt * 2 + half) % 4
                    for kc in range(8):
                        op("pe", lambda e, kc=kc, half=half, bank=bank, gt=gt: e.matmul(
                            PS[bank][:, :], lhsT=gt[:, kc, :], rhs=gw[:, kc, half * 512:(half + 1) * 512],
                            start=(kc == 0), stop=False), r=[gt.b[0], gw.b[0]], w=[PB[bank]])
                    op("pe", lambda e, half=half, bank=bank: e.matmul(
                        PS[bank][:, :], lhsT=ones_f[0:1, :], rhs=gb[0:1, half * 512:(half + 1) * 512],
                        start=False, stop=True), r=[ones_f.b[0], gb.b[0]], w=[PB[bank]])
                    s_ = sg[isg % 2]
                    isg += 1
                    op("act", lambda e, bank=bank, s_=s_: e.activation(out=s_[:], in_=PS[bank][:, :], func=AF.Sigmoid),
                       r=[PB[bank]], w=[s_.b[0]])
                    op("dve", lambda e, half=half, tt=tt, s_=s_: e.tensor_tensor(
                        out=gc[:, tt, half * 512:(half + 1) * 512], in0=gc[:, tt, half * 512:(half + 1) * 512],
                        in1=s_[:], op=ALU.mult), r=[gc.b[tt], s_.b[0]], w=[gc.b[tt]])
            kb.barrier()

        hT = T(kb, "hT", [128, NT, 1024], F32, R_H[0], nslots=NT)
        aT = T(kb, "aT", [128, 8, NTOK], BF16, R_A[0], nslots=NT)
        yy = T(kb, "yy", [128, NT, 1024], BF16, R_A[0], nslots=NT)
        resid = xin
        for li in layers:
            last = (li == 1)
            phase_mod(li)
            tiles_all = list(range(NT))
            phase_norm(li, 0, resid, None, aT, tiles_all)
            if li == 0:
                phase_attn(aT, yy)
                tiles = tiles_all
                phase_outproj(yy, hT, woutd, resid, tiles)
            else:
                phase_s5(aT, yy)
                tiles = list(range(2, NT))
                phase_outproj(yy, hT, s5woutd, resid, tiles, cmajor=True)
            phase_norm(li, 1, None, hT, aT, tiles)
            if last:
                blocks = [(2 + 4 * g, 4, 0) for g in range(4)]
            else:
                blocks = [(0, 2, 1)] + [(2 + 4 * g, 4, 0) for g in range(4)]
            phase_mlp(li, aT, hT, blocks)
            dst = hout if li == last_layer else hmid
            for tt in tiles:
                dma("sp", rows_ap(dst, tt, li == 1), hT[:, tt, :], r=[hT.b[tt]])
            kb.barrier()
            resid = hmid
        if dbg is not None:
            pass
    return nc


def _host_consts():
    inv = (10000.0 ** (-np.arange(16, dtype=np.float32) / np.float32(16))).astype(np.float32)
    pos = np.arange(2048)
    row = (pos // 64).astype(np.float32)
    col = (pos % 64).astype(np.float32)
    ang = np.concatenate([row[:, None] * inv[None], col[:, None] * inv[None]], axis=-1).astype(np.float32)
    cos = np.cos(ang).astype(np.float32).reshape(16, 128, 32).transpose(1, 0, 2)
    sin = np.sin(ang).astype(np.float32).reshape(16, 128, 32).transpose(1, 0, 2)
    k = np.arange(128)[:, None]
    q = np.arange(128)[None, :]
    masks = np.stack([(k >= q), (k <= q)]).astype(np.float32)
    return np.ascontiguousarray(cos), np.ascontiguousarray(sin), masks


_PROG = {}


def _get_prog(key):
    if key not in _PROG:
        _PROG[key] = build_program(list(key))
    return _PROG[key]


def _common_maps(inp, b):
    f = np.float32
    cvec = np.stack([np.asarray(inp["c"][b], f).reshape(8, 128).T, np.asarray(inp["c_ctx"], f).reshape(8, 128).T],
                    axis=-1)
    m = {
        "cvec": np.ascontiguousarray(cvec),
        "mod_w": np.asarray(inp["mod_w"], f),
        "modb": np.ascontiguousarray(np.asarray(inp["mod_b"], f).reshape(2, 48, 128).transpose(0, 2, 1)),
        "g1": np.ascontiguousarray(np.asarray(inp["norm1_g"], f).reshape(2, 8, 128).transpose(0, 2, 1)),
        "g2": np.ascontiguousarray(np.asarray(inp["norm2_g"], f).reshape(2, 8, 128).transpose(0, 2, 1)),
        "mlp_w1": np.asarray(inp["mlp_w1"], f),
        "mlp_w2": np.asarray(inp["mlp_w2"], f),
        "ident": np.eye(128, dtype=f),
    }
    return m


def _l0_maps(inp):
    f = np.float32
    cos, sin, masks = _host_consts()
    aq, ak = np.asarray(inp["a_q_norm"][0], f), np.asarray(inp["a_k_norm"][0], f)
    bq, bk = np.asarray(inp["b_q_norm"][0], f), np.asarray(inp["b_k_norm"][0], f)
    gains = np.stack([aq, ak, bq, bk])
    lqk = np.stack([inp["b_lq1"][0], inp["b_lk1"][0], inp["b_lq2"][0], inp["b_lk2"][0]]).astype(f)
    return {
        "attn_w_in": np.asarray(inp["attn_w_in"][0], f),
        "attn_w_out": np.asarray(inp["attn_w_out"][0], f),
        "rope_cos": cos, "rope_sin": sin,
        "qk_gain": np.ascontiguousarray(np.broadcast_to(gains[None], (128, 4, 64))),
        "a_sink": np.ascontiguousarray(np.broadcast_to(np.asarray(inp["a_sink"][0], f)[None], (128, 8))),
        "b_lqk": np.ascontiguousarray(np.broadcast_to(lqk[None], (128, 4, 64))),
        "b_subln": np.ascontiguousarray(np.broadcast_to(np.asarray(inp["b_subln"][0], f)[None], (128, 128))),
        "masks": masks,
    }


def _l1_maps(inp):
    f = np.float32
    maps = {
        "s5_w_in": np.asarray(inp["s5_w_in"][0], f),
        "s5_w_out": np.asarray(inp["s5_w_out"][0], f),
        "s5_glu_w": np.asarray(inp["s5_glu_w"][0], f),
        "s5_glu_b": np.ascontiguousarray(np.asarray(inp["s5_glu_b"][0], f).reshape(1, D)),
    }

    def nlay(x):
        return np.asarray(x, f).reshape(2, 32, 64).transpose(0, 2, 1).reshape(128, 32)

    lam_n = np.zeros((2, 3, 128, 32), f)
    bn = np.zeros((2, 2, 128, 32, 16), f)
    cn = np.zeros((2, 2, 128, 32, 16), f)
    bsj = np.zeros((2, 2, 128, 4096), f)
    for d in range(2):
        lam_n[d, 0] = nlay(inp["s5_lambda_re"][0][d])
        lam_n[d, 1] = nlay(inp["s5_lambda_im"][0][d])
        lam_n[d, 2] = nlay(np.broadcast_to(np.asarray(inp["s5_log_step"][0][d], f)[:, None], (64, 64)))
        for p, (bk, ck) in enumerate((("s5_b_re", "s5_c_re"), ("s5_b_im", "s5_c_im"))):
            b = np.asarray(inp[bk][0][d], f)
            c = np.asarray(inp[ck][0][d], f)
            bn[d, p] = b.reshape(2, 32, 64, 16).transpose(0, 2, 1, 3).reshape(128, 32, 16)
            cn[d, p] = c.reshape(2, 32, 16, 64).transpose(0, 3, 1, 2).reshape(128, 32, 16)
            bj = b.transpose(2, 0, 1).reshape(16, 4096)
            bsj[d, p] = np.broadcast_to(bj[None], (8, 16, 4096)).reshape(128, 4096)
    k = np.arange(8, dtype=f)
    ktab = np.stack([k + 1, 8 - k, -(k + 1), k - 8, 7 - k, k]).astype(f)
    s_ = np.arange(128) // 16
    j_ = np.arange(128) % 16
    mf = (s_[:, None] <= s_[None, :])
    mb = (s_[:, None] >= s_[None, :])
    md = (s_[:, None] == s_[None, :]) & (j_[:, None] == j_[None, :])
    maps.update({
        "s5_lam_n": lam_n, "s5_bn": bn, "s5_cn": cn, "s5_bsj": bsj,
        "s5_ktab": np.ascontiguousarray(np.broadcast_to(ktab[None], (128, 6, 8))),
        "s5_dsk": np.ascontiguousarray(np.broadcast_to(np.asarray(inp["s5_d"][0], f)[None], (128, D))),
        "s5_masks": np.stack([mf, mb, md]).astype(f),
    })
    return maps


def _run(layers, inp, xins, ncores=8):
    nc = _get_prog(tuple(layers))
    extra = {}
    if 0 in layers:
        extra.update(_l0_maps(inp))
    if 1 in layers:
        extra.update(_l1_maps(inp))
    maps = []
    for b in range(ncores):
        m = _common_maps(inp, b)
        m.update(extra)
        m["xin"] = np.ascontiguousarray(xins[b], dtype=np.float32)
        maps.append(m)
    res = run_bass_kernel_spmd(nc, maps, core_ids=list(range(ncores)))
    return [r["hout"] for r in res.results]


FUSED = True


def kernel(**inputs):
    inp = {k: np.asarray(v) for k, v in inputs.items()}
    nb = inp["x"].shape[0]
    xins = [np.concatenate([inp["ctx"][b], inp["x"][b]], axis=0) for b in range(nb)]
    if FUSED:
        outs = _run([0, 1], inp, xins, nb)
    else:
        h0 = _run([0], inp, xins, nb)
        outs = _run([1], inp, h0, nb)
    return np.stack(outs).astype(np.float32)
```

```python
import math
from contextlib import ExitStack

import numpy as np
import concourse.bass as bass
import concourse.mybir as mybir
from concourse.ap import AP
from concourse.bass_utils import run_bass_kernel_spmd

F32 = mybir.dt.float32
BF16 = mybir.dt.bfloat16
I32 = mybir.dt.int32
AF = mybir.ActivationFunctionType
ALU = mybir.AluOpType
AX = mybir.AxisListType

D = 1024
NT = 18
NTOK = NT * 128
EPS = 1e-6
SB_LO = 16640
SB_HI = 229376
SAME_ENG_SKIP_DIST = 2


def dsize(dt):
    return {F32: 4, BF16: 2, I32: 4}[dt]


class Buf:
    __slots__ = ("w", "r")

    def __init__(self):
        self.w = None
        self.r = {}


class KB:
    ENGS = ("pe", "dve", "act", "pool", "sp")
    NDS = 24

    def __init__(self, nc, es):
        self.nc = nc
        self.E = {"pe": nc.tensor, "dve": nc.vector, "act": nc.scalar, "pool": nc.gpsimd, "sp": nc.sync}
        self.sems = {}
        for e in self.ENGS:
            self.sems[e] = es.enter_context(nc.semaphore("s_" + e))
        for i in range(self.NDS):
            self.sems[("d", i)] = es.enter_context(nc.semaphore("s_d%d" % i))
        self.cnt = {e: 0 for e in self.ENGS}
        self.dcnt = [0] * self.NDS
        self.dma_i = 0
        self.seen = {e: {} for e in self.ENGS}
        self.nbuf = 0

    def _wait(self, eng, toks):
        need = {}
        for key, val in toks:
            if key == eng and (eng == "pe" or self.cnt[eng] - val >= SAME_ENG_SKIP_DIST):
                continue
            if self.seen[eng].get(key, 0) < val and need.get(key, 0) < val:
                need[key] = val
        for key, val in need.items():
            self.E[eng].wait_ge(self.sems[key], val)
            self.seen[eng][key] = val

    @staticmethod
    def _deps(r, w):
        toks = []
        for b in r:
            if b.w is not None:
                toks.append(b.w)
        for b in w:
            if b.w is not None:
                toks.append(b.w)
            toks.extend(b.r.items())
        return toks

    @staticmethod
    def _upd(tok, r, w):
        for b in r:
            if b.r.get(tok[0], 0) < tok[1]:
                b.r[tok[0]] = tok[1]
        for b in w:
            b.w = tok
            b.r = {}

    def op(self, eng, fn, r=(), w=(), nosync=False):
        toks = self._deps(r, w)
        if nosync:
            toks = [t for t in toks if t[0] != eng]
        self._wait(eng, toks)
        ins = fn(self.E[eng])
        self.cnt[eng] += 1
        ins.then_inc(self.sems[eng], 1)
        self._upd((eng, self.cnt[eng]), r, w)

    def dma(self, q, out, in_, r=(), w=()):
        toks = self._deps(r, w)
        i = self.dma_i % self.NDS
        self.dma_i += 1
        key = ("d", i)
        if self.dcnt[i] > 0:
            toks.append((key, self.dcnt[i]))
        self._wait(q, toks)
        ins = self.E[q].dma_start(out=out, in_=in_)
        self.dcnt[i] += 16
        ins.then_inc(self.sems[key], 16)
        self._upd((key, self.dcnt[i]), r, w)

    def barrier(self):
        toks = [(e, self.cnt[e]) for e in self.ENGS if self.cnt[e] > 0]
        toks += [(("d", i), self.dcnt[i]) for i in range(self.NDS) if self.dcnt[i] > 0]
        for e in self.ENGS:
            self._wait(e, toks)


class T:
    def __init__(self, kb, name, shape, dt, off, nslots=1):
        self.t = kb.nc.alloc_sbuf_tensor_at(name, list(shape), dt, offset=off)
        self.shape = list(shape)
        self.dt = dt
        self.row = int(np.prod(shape[1:]))
        self.bytes = self.row * dsize(dt)
        self.b = [Buf() for _ in range(nslots)]
        self.off = off
        self.end = off + ((self.bytes + 63) // 64) * 64

    def ap(self, off, free, p0=0, np_=128):
        return AP(self.t, p0 * self.row + off, [[self.row, np_]] + [list(f) for f in free])

    def __getitem__(self, k):
        return self.t[k]


class Arena:
    def __init__(self, lo, hi):
        self.lo, self.hi, self.top = lo, hi, lo

    def take(self, nbytes):
        off = self.top
        self.top += ((nbytes + 63) // 64) * 64
        assert self.top <= self.hi, ("arena overflow", self.top, self.hi)
        return off


def pap(t, off, free, p0=0, np_=128, row=512):
    return AP(t, p0 * row + off, [[row, np_]] + [list(f) for f in free])


def build_program(layers, n_out_rows_last=2048, debug=None):
    nc = bass.Bass("TRN2", target_bir_lowering=False)
    dr = {}

    def din(name, shape, dt=F32):
        dr[name] = nc.dram_tensor(name, list(shape), dt, kind="ExternalInput").ap()
        return dr[name]

    xin = din("xin", [NTOK, D])
    cvec = din("cvec", [128, 8, 2])
    mod_w = din("mod_w", [2, D, 6 * D])
    modb = din("modb", [2, 128, 48])
    g1d = din("g1", [2, 128, 8])
    g2d = din("g2", [2, 128, 8])
    w1d = din("mlp_w1", [2, D, 4 * D])
    w2d = din("mlp_w2", [2, 4 * D, D])
    identd = din("ident", [128, 128])
    if 0 in layers:
        wind = din("attn_w_in", [D, 2304])
        woutd = din("attn_w_out", [D, D])
        ropec = din("rope_cos", [128, 16, 32])
        ropes = din("rope_sin", [128, 16, 32])
        gaind = din("qk_gain", [128, 4, 64])
        sinkd = din("a_sink", [128, 8])
        lqkd = din("b_lqk", [128, 4, 64])
        sublnd = din("b_subln", [128, 128])
        maskd = din("masks", [2, 128, 128])
    if 1 in layers:
        s5wind = din("s5_w_in", [D, D])
        s5woutd = din("s5_w_out", [D, D])
        gluwd = din("s5_glu_w", [D, D])
        glubd = din("s5_glu_b", [1, D])
        lamnd = din("s5_lam_n", [2, 3, 128, 32])
        ktabd = din("s5_ktab", [128, 6, 8])
        bnd = din("s5_bn", [2, 2, 128, 32, 16])
        cnd = din("s5_cn", [2, 2, 128, 32, 16])
        bsjd = din("s5_bsj", [2, 2, 128, 4096])
        dskd = din("s5_dsk", [128, 1024])
        m3d = din("s5_masks", [3, 128, 128])
        escr = nc.dram_tensor("escr", [2, 2, 8, 4096], F32, kind="Internal").ap()
    last_layer = layers[-1]
    n_rows_out = NTOK if last_layer != 1 else 2048
    hout = nc.dram_tensor("hout", [n_rows_out, D], F32, kind="ExternalOutput").ap()
    hmid = None
    if len(layers) > 1:
        hmid = nc.dram_tensor("hmid", [NTOK, D], F32, kind="Internal").ap()
    dbg = None
    if debug is not None:
        dbg = nc.dram_tensor("dbg", list(debug), F32, kind="ExternalOutput").ap()

    with ExitStack() as es:
        kb = KB(nc, es)
        op, dma = kb.op, kb.dma
        PS = [es.enter_context(nc.psum_tensor("bank%d" % i, [128, 512], F32)) for i in range(8)]
        PB = [Buf() for _ in range(8)]

        ar = Arena(SB_LO, SB_HI)

        def mk(name, shape, dt, nslots=1, arena=None):
            a = arena or ar
            row = int(np.prod(shape[1:]))
            off = a.take(row * dsize(dt))
            return T(kb, name, shape, dt, off, nslots)

        ident_f = mk("ident_f", [128, 128], F32)
        ident_b = mk("ident_b", [128, 128], BF16)
        ones_f = mk("ones_f", [128, 128], F32)
        m_all = mk("m_all", [128, 48, 2], F32)
        gam = mk("gam", [128, 2, 8, 2], F32)
        gate_rep = mk("gate_rep", [128, 2, 2, 1024], F32, nslots=4)
        small = mk("small", [128, 64], F32, nslots=8)
        cv = mk("cv", [128, 8, 2], F32)
        sv = mk("sv", [128, 8, 2], F32)
        g1s = mk("g1s", [128, 8], F32)
        g2s = mk("g2s", [128, 8], F32)
        modbs = mk("modbs", [128, 48], F32)
        macc = mk("macc", [128, 96], F32)
        junk = mk("junk", [128, 1024], BF16)
        nhalf = mk("nhalf", [128, 32], F32)
        P_END = ar.top
        R_H = (P_END, P_END + 73728)
        R_A = (R_H[1], R_H[1] + 36864)
        R_B = (R_A[1], SB_HI)
        assert R_B[1] - R_B[0] >= 78500, (R_B, "region B too small")

        dma("sp", ident_f[:], identd[:, :], w=[ident_f.b[0]])
        dma("pool", ident_b[:], identd[:, :], w=[ident_b.b[0]])
        op("dve", lambda e: e.memset(ones_f[:], 1.0), w=[ones_f.b[0]])
        dma("sp", cv[:], cvec[:, :, :], w=[cv.b[0]])
        op("act", lambda e: e.activation(out=sv[:], in_=cv[:], func=AF.Silu), r=[cv.b[0]], w=[sv.b[0]])

        stat_i = [0]

        def stat_slot():
            i = stat_i[0] % 8
            stat_i[0] += 1
            return i

        op("pool", lambda e: e.memset(nhalf[:], -0.5), w=[nhalf.b[0]])

        def rstd_from_ss(ss_ap, out_ap, n, bufs_r, bufs_w, cols=1):
            op("dve", lambda e: e.tensor_scalar(out_ap, ss_ap, 1.0 / n, EPS, ALU.mult, ALU.add), r=bufs_r, w=bufs_w)
            op("pool", lambda e: e.tensor_tensor(out=out_ap, in0=out_ap, in1=nhalf[:, 0:cols], op=ALU.pow),
               r=list(bufs_w) + [nhalf.b[0]], w=bufs_w)

        def phase_mod(li):
            a = Arena(*R_B)
            mw = [mk("mw%d" % i, [128, 3072], F32, arena=a) for i in range(2)]
            mrow = mk("mrow", [2, 6144], F32, arena=a)
            dg = [mk("dg%d" % i, [128, 128], F32, arena=a) for i in range(2)]
            dma("sp", g1s[:], g1d[li, :, :], w=[g1s.b[0]])
            dma("sp", g2s[:], g2d[li, :, :], w=[g2s.b[0]])
            dma("sp", modbs[:], modb[li, :, :], w=[modbs.b[0]])
            for half in range(2):
                for kc in range(8):
                    m = mw[kc % 2]
                    dma("sp" if kc % 2 == 0 else "act", m[:],
                        mod_w[li, kc * 128:(kc + 1) * 128, half * 3072:(half + 1) * 3072], w=[m.b[0]])
                    for j in range(6):
                        op("pe", lambda e, j=j, m=m, kc=kc: e.matmul(
                            PS[j][0:2, :], lhsT=sv[:, kc, :], rhs=m[:, j * 512:(j + 1) * 512],
                            start=(kc == 0), stop=(kc == 7)), r=[m.b[0], sv.b[0]], w=[PB[j]])
                for j in range(6):
                    op("act", lambda e, j=j, half=half: e.activation(
                        out=mrow[0:2, half * 3072 + j * 512:half * 3072 + (j + 1) * 512], in_=PS[j][0:2, :],
                        func=AF.Identity), r=[PB[j]], w=[mrow.b[0]])
            for oc in range(48):
                op("pe", lambda e, oc=oc: e.matmul(
                    PS[7][:, oc * 2:oc * 2 + 2], lhsT=mrow[0:2, oc * 128:(oc + 1) * 128], rhs=ident_f[0:2, 0:2],
                    start=True, stop=True), r=[mrow.b[0], ident_f.b[0]], w=[PB[7]])
            op("dve", lambda e: e.tensor_copy(out=macc[:], in_=PS[7][:, 0:96]), r=[PB[7]], w=[macc.b[0]])
            op("dve", lambda e: e.tensor_tensor(
                out=m_all[:], in0=macc.ap(0, [[2, 48], [1, 2]]), in1=modbs.ap(0, [[1, 48], [0, 2]]), op=ALU.add),
               r=[macc.b[0], modbs.b[0]], w=[m_all.b[0]])
            for ni, (gs, oc0) in enumerate(((g1s, 8), (g2s, 32))):
                op("dve", lambda e, ni=ni, gs=gs, oc0=oc0: e.scalar_tensor_tensor(
                    out=gam.ap(ni * 16, [[2, 8], [1, 2]]), in0=m_all.ap(oc0 * 2, [[2, 8], [1, 2]]), scalar=1.0,
                    in1=gs.ap(0, [[1, 8], [0, 2]]), op0=ALU.add, op1=ALU.mult),
                   r=[m_all.b[0], gs.b[0]], w=[gam.b[0]])
            k = 0
            for gi, oc0 in enumerate((16, 40)):
                for r_ in range(2):
                    for half in range(2):
                        bank = 1 + (k % 2)
                        k += 1
                        for c4 in range(4):
                            c = half * 4 + c4
                            d_ = dg[c % 2]
                            op("dve", lambda e, d_=d_, c=c, oc0=oc0, r_=r_: e.tensor_scalar(
                                d_[:], ident_f[:], m_all[:, oc0 + c, r_:r_ + 1], None, ALU.mult),
                               r=[ident_f.b[0], m_all.b[0]], w=[d_.b[0]])
                            op("pe", lambda e, d_=d_, c4=c4, bank=bank: e.matmul(
                                PS[bank][:, c4 * 128:(c4 + 1) * 128], lhsT=ones_f[:], rhs=d_[:], start=True, stop=True),
                               r=[ones_f.b[0], d_.b[0]], w=[PB[bank]])
                        sl = gi * 2 + r_
                        op("act", lambda e, gi=gi, r_=r_, half=half, bank=bank: e.activation(
                            out=gate_rep[:, gi, r_, half * 512:(half + 1) * 512], in_=PS[bank][:, :], func=AF.Identity),
                           r=[PB[bank]], w=[gate_rep.b[sl]])
            kb.barrier()

        def phase_norm(li, ni, src_dram, hT, aT, tiles):
            a = Arena(*R_B)
            xs = [mk("nx%d" % i, [128, 1024], F32, arena=a) for i in range(3)]
            xh = [mk("nxh%d" % i, [128, 1024], F32, arena=a) for i in range(2)]
            n_ = len(tiles)
            stt = {}

            def stage_a(it):
                tt = tiles[it]
                if src_dram is not None:
                    x = xs[it % 3]
                    dma("sp", x[:], src_dram[tt * 128:(tt + 1) * 128, :], w=[x.b[0]])
                    xap, xb = x[:], x.b[0]
                else:
                    xap, xb = hT[:, tt, :], hT.b[tt]
                si = stat_slot()
                ss = small[:, si * 8:si * 8 + 1]
                sb_ = small.b[si]
                op("act", lambda e, xap=xap, ss=ss: e.activation(out=junk[:], in_=xap, func=AF.Square, accum_out=ss),
                   r=[xb], w=[junk.b[0], sb_])
                stt[it] = (xap, xb, ss, sb_)

            def stage_b(it):
                xap, xb, ss, sb_ = stt[it]
                rstd_from_ss(ss, ss, 1024.0, [sb_], [sb_])
                h_ = xh[it % 2]
                op("dve", lambda e, h_=h_, xap=xap, ss=ss: e.tensor_scalar(h_[:], xap, ss, None, ALU.mult),
                   r=[xb, sb_], w=[h_.b[0]])

            def stage_c(it):
                tt = tiles[it]
                r_ = 1 if tt < 2 else 0
                h_ = xh[it % 2]
                for half in range(2):
                    bank = (it * 2 + half) % 8
                    for c4 in range(4):
                        c = half * 4 + c4
                        op("pe", lambda e, h_=h_, c=c, c4=c4, bank=bank: e.transpose(
                            PS[bank][:, c4 * 128:(c4 + 1) * 128], h_[:, c * 128:(c + 1) * 128], ident_f[:]),
                           r=[h_.b[0], ident_f.b[0]], w=[PB[bank]])
                    for c4 in range(4):
                        c = half * 4 + c4
                        g_ap = gam[:, ni, c, r_:r_ + 1]
                        oc_shift = (0 if ni == 0 else 24) + c
                        b_ap = m_all[:, oc_shift, r_:r_ + 1]
                        o_ap = aT[:, c, tt * 128:(tt + 1) * 128]
                        i_ap = PS[bank][:, c4 * 128:(c4 + 1) * 128]
                        if c4 % 2 == 0:
                            op("act", lambda e, o_ap=o_ap, i_ap=i_ap, g_ap=g_ap, b_ap=b_ap: e.activation(
                                out=o_ap, in_=i_ap, func=AF.Identity, bias=b_ap, scale=g_ap),
                               r=[PB[bank], gam.b[0], m_all.b[0]], w=[aT.b[tt]])
                        else:
                            op("dve", lambda e, o_ap=o_ap, i_ap=i_ap, g_ap=g_ap, b_ap=b_ap: e.tensor_scalar(
                                o_ap, i_ap, g_ap, b_ap, ALU.mult, ALU.add),
                               r=[PB[bank], gam.b[0], m_all.b[0]], w=[aT.b[tt]])

            for it in range(n_ + 2):
                if it < n_:
                    stage_a(it)
                if 0 <= it - 1 < n_:
                    stage_b(it - 1)
                if 0 <= it - 2 < n_:
                    stage_c(it - 2)
            kb.barrier()

        def phase_attn(aT, y):
            a = Arena(*R_B)
            QKT = mk("QKT", [128, 13, NTOK], BF16, nslots=NT, arena=a)
            VB = mk("VB", [128, NT, 4, 129], BF16, nslots=NT, arena=a)
            ah = Arena(*R_H)
            VA = mk("VA", [128, NT, 2, 65], BF16, nslots=NT, arena=ah)
            win = mk("win", [128, 8, 2304], BF16, arena=ah)
            cosT = mk("cosT", [128, 16, 32], F32, arena=ah)
            sinT = mk("sinT", [128, 16, 32], F32, arena=ah)
            gains = mk("gains", [128, 4, 64], F32, arena=ah)
            scr = mk("scr", [128, 26, 64], F32, arena=ah)
            xraw = [mk("xraw%d" % i, [128, 26, 64], F32, arena=ah) for i in range(2)]
            qkt = [mk("qkt%d" % i, [128, 26, 64], BF16, arena=ah) for i in range(2)]
            ssb = mk("ssb", [128, 2, 32], F32, nslots=2, arena=ah)
            for kc in range(8):
                for b2 in range(2):
                    dma("pool", win.ap(kc * 2304 + b2 * 64, [[128, 4], [1, 64]]),
                        AP(wind.tensor, kc * 128 * 2304 + b2 * 256, [[2304, 128], [64, 4], [1, 64]]), w=[win.b[0]])
                dma("pool", win[:, kc, 512:2304], wind[kc * 128:(kc + 1) * 128, 512:2304], w=[win.b[0]])
            dma("sp", cosT[:], ropec[:, :, :], w=[cosT.b[0]])
            dma("sp", sinT[:], ropes[:, :, :], w=[sinT.b[0]])
            dma("sp", gains[:], gaind[:, :, :], w=[gains.b[0]])
            op("pool", lambda e: e.memset(VA.ap(64, [[130, NT], [65, 2], [1, 1]]), 1.0), w=VA.b)
            op("pool", lambda e: e.memset(VB.ap(128, [[516, NT], [129, 4], [1, 1]]), 1.0), w=VB.b)

            qk_ranges = [(0, 0, 512, 0), (1, 0, 128, 8), (1, 256, 256, 10), (2, 0, 512, 14), (3, 0, 256, 22)]
            gtypes = [(0, 8), (8, 2), (10, 8), (18, 8)]
            ncols = [512, 512, 512, 512, 256]

            def proj_mm(tt):
                for nb in range(5):
                    for kc in range(8):
                        op("pe", lambda e, nb=nb, kc=kc, tt=tt: e.matmul(
                            PS[nb][:, 0:ncols[nb]], lhsT=aT[:, kc, tt * 128:(tt + 1) * 128],
                            rhs=win[:, kc, nb * 512:nb * 512 + ncols[nb]], start=(kc == 0), stop=(kc == 7)),
                           r=[aT.b[tt], win.b[0]], w=[PB[nb]])

            def proj_evac(tt):
                x_ = xraw[tt % 2]
                for (bk, c0, n, g0) in qk_ranges:
                    op("act", lambda e, bk=bk, c0=c0, n=n, g0=g0: e.activation(
                        out=x_.ap(g0 * 64, [[1, n]]), in_=PS[bk][:, c0:c0 + n], func=AF.Identity),
                       r=[PB[bk]], w=[x_.b[0]])
                sl = tt % 2
                for g in range(26):
                    op("act", lambda e, g=g, sl=sl: e.activation(
                        out=junk[:, 0:64], in_=x_.ap(g * 64, [[1, 64]]), func=AF.Square,
                        accum_out=ssb.ap(sl * 32 + g, [[1, 1]])), r=[x_.b[0]], w=[junk.b[0], ssb.b[sl]])
                op("act", lambda e, tt=tt: e.activation(
                    out=VA.ap(tt * 130, [[65, 2], [1, 64]]), in_=pap(PS[1], 128, [[64, 2], [1, 64]]), func=AF.Identity),
                   r=[PB[1]], w=[VA.b[tt]])
                op("act", lambda e, tt=tt: e.activation(
                    out=VB.ap(tt * 516, [[129, 2], [1, 128]]), in_=pap(PS[3], 256, [[128, 2], [1, 128]]),
                    func=AF.Identity), r=[PB[3]], w=[VB.b[tt]])
                op("act", lambda e, tt=tt: e.activation(
                    out=VB.ap(tt * 516 + 258, [[129, 2], [1, 128]]), in_=pap(PS[4], 0, [[128, 2], [1, 128]]),
                    func=AF.Identity), r=[PB[4]], w=[VB.b[tt]])

            def proj_post(tt):
                s_ = scr
                x_ = xraw[tt % 2]
                q_ = qkt[tt % 2]
                sl = tt % 2
                ssap = ssb.ap(sl * 32, [[1, 26]])
                rstd_from_ss(ssap, ssap, 64.0, [ssb.b[sl]], [ssb.b[sl]], cols=26)
                op("dve", lambda e: e.tensor_tensor(out=x_[:], in0=x_[:], in1=ssb.ap(sl * 32, [[1, 26], [0, 64]]),
                                                    op=ALU.mult), r=[x_.b[0], ssb.b[sl]], w=[x_.b[0]])
                for ty, (g0, ng) in enumerate(gtypes):
                    op("dve", lambda e, ty=ty, g0=g0, ng=ng: e.tensor_tensor(
                        out=x_.ap(g0 * 64, [[64, ng], [1, 64]]), in0=x_.ap(g0 * 64, [[64, ng], [1, 64]]),
                        in1=gains.ap(ty * 64, [[0, ng], [1, 64]]), op=ALU.mult),
                       r=[x_.b[0], gains.b[0]], w=[x_.b[0]])
                if tt >= 2:
                    lt = tt - 2
                    cb = cosT.ap(lt * 32, [[0, 26], [0, 2], [1, 32]])
                    sb2 = sinT.ap(lt * 32, [[0, 26], [1, 32]])
                    op("dve", lambda e, sb2=sb2: e.tensor_tensor(
                        out=scr.ap(0, [[32, 26], [1, 32]]), in0=x_.ap(32, [[64, 26], [1, 32]]), in1=sb2, op=ALU.mult),
                       r=[x_.b[0], sinT.b[0]], w=[scr.b[0]])
                    op("dve", lambda e, sb2=sb2: e.tensor_tensor(
                        out=scr.ap(832, [[32, 26], [1, 32]]), in0=x_.ap(0, [[64, 26], [1, 32]]), in1=sb2, op=ALU.mult),
                       r=[x_.b[0], sinT.b[0]], w=[scr.b[0]])
                    op("dve", lambda e, cb=cb: e.tensor_tensor(
                        out=x_.ap(0, [[64, 26], [32, 2], [1, 32]]), in0=x_.ap(0, [[64, 26], [32, 2], [1, 32]]),
                        in1=cb, op=ALU.mult), r=[x_.b[0], cosT.b[0]], w=[x_.b[0]])
                    op("dve", lambda e: e.tensor_tensor(
                        out=q_.ap(0, [[64, 26], [1, 32]]), in0=x_.ap(0, [[64, 26], [1, 32]]),
                        in1=scr.ap(0, [[32, 26], [1, 32]]), op=ALU.subtract), r=[x_.b[0], scr.b[0]], w=[q_.b[0]])
                    op("dve", lambda e: e.tensor_tensor(
                        out=q_.ap(32, [[64, 26], [1, 32]]), in0=x_.ap(32, [[64, 26], [1, 32]]),
                        in1=scr.ap(832, [[32, 26], [1, 32]]), op=ALU.add), r=[x_.b[0], scr.b[0]], w=[q_.b[0]])
                else:
                    op("dve", lambda e: e.tensor_copy(out=q_[:], in_=x_[:]), r=[x_.b[0]], w=[q_.b[0]])
                srcs = [q_.ap(h * 128, [[1, 128]]) for h in range(4)]
                srcs.append(q_.ap(8 * 64, [[1, 128]]))
                for h in range(4):
                    srcs.append(q_.ap((10 + 2 * h) * 64, [[1, 128]]))
                    srcs.append(q_.ap((18 + 2 * h) * 64, [[1, 128]]))
                for s0 in range(0, 13, 4):
                    bank = 5 + ((tt * 4 + s0 // 4) % 3)
                    ns = min(4, 13 - s0)
                    for j in range(ns):
                        op("pe", lambda e, j=j, s0=s0, bank=bank: e.matmul(
                            PS[bank][:, j * 128:(j + 1) * 128], lhsT=srcs[s0 + j], rhs=ident_b[:],
                            start=True, stop=True), r=[q_.b[0], ident_b.b[0]], w=[PB[bank]])
                    op("act", lambda e, s0=s0, ns=ns, bank=bank, tt=tt: e.activation(
                        out=QKT.ap(s0 * NTOK + tt * 128, [[NTOK, ns], [1, 128]]),
                        in_=pap(PS[bank], 0, [[128, ns], [1, 128]]), func=AF.Identity),
                       r=[PB[bank]], w=[QKT.b[tt]])

            proj_mm(0)
            proj_evac(0)
            proj_mm(1)
            for tt in range(NT):
                if tt + 1 < NT:
                    proj_evac(tt + 1)
                if tt + 2 < NT:
                    proj_mm(tt + 2)
                proj_post(tt)
            kb.barrier()

            ah = Arena(R_H[0] + VA.end - VA.off, R_H[1])
            PTB = mk("PTB", [128, 3, NT, 512], BF16, nslots=3, arena=ah)
            PTA = T(kb, "PTA", [128, 2, 5, 512], BF16, PTB.off, nslots=2)
            mlo = mk("mlo", [128, 128], BF16, arena=ah)
            mhi = mk("mhi", [128, 128], BF16, arena=ah)
            esink = mk("esink", [128, 8], F32, arena=ah)
            lqk = mk("lqk", [128, 4, 64], F32, arena=ah)
            lam = mk("lam", [128, 8], F32, arena=ah)
            gsub = mk("gsub", [128, 128], F32, arena=ah)
            tmpy = [mk("tmpy%d" % i, [128, 128], F32, arena=ah) for i in range(4)]
            yb = [mk("yb%d" % i, [128, 128], F32, arena=ah) for i in range(4)]
            rr = mk("rr", [128, 8, 8], F32, nslots=8, arena=ah)
            dma("pool", mlo[:], maskd[0, :, :], w=[mlo.b[0]])
            dma("pool", mhi[:], maskd[1, :, :], w=[mhi.b[0]])
            dma("sp", esink[:], sinkd[:, :], w=[esink.b[0]])
            op("act", lambda e: e.activation(out=esink[:], in_=esink[:], func=AF.Exp), r=[esink.b[0]], w=[esink.b[0]])
            dma("sp", lqk[:], lqkd[:, :, :], w=[lqk.b[0]])
            dma("sp", gsub[:], sublnd[:, :], w=[gsub.b[0]])
            lam_init = 0.8 - 0.6 * math.exp(-0.3 * 0)
            op("dve", lambda e: e.tensor_scalar(gsub[:], gsub[:], 1.0 - lam_init, None, ALU.mult),
               r=[gsub.b[0]], w=[gsub.b[0]])
            op("dve", lambda e: e.tensor_tensor(out=tmpy[0].ap(0, [[64, 2], [1, 64]]), in0=lqk.ap(0, [[128, 2], [1, 64]]),
                                                in1=lqk.ap(64, [[128, 2], [1, 64]]), op=ALU.mult),
               r=[lqk.b[0]], w=[tmpy[0].b[0]])
            op("dve", lambda e: e.tensor_reduce(out=lam[:, 0:2], in_=tmpy[0].ap(0, [[64, 2], [1, 64]]), axis=AX.X,
                                                op=ALU.add), r=[tmpy[0].b[0]], w=[lam.b[0]])
            op("act", lambda e: e.activation(out=lam[:, 0:2], in_=lam[:, 0:2], func=AF.Exp), r=[lam.b[0]], w=[lam.b[0]])
            op("dve", lambda e: e.scalar_tensor_tensor(out=lam[:, 2:3], in0=lam[:, 1:2], scalar=-lam_init,
                                                       in1=lam[:, 0:1], op0=ALU.add, op1=ALU.subtract),
               r=[lam.b[0]], w=[lam.b[0]])

            sbank = [0]

            def next_sbank():
                b = sbank[0] % 5
                sbank[0] += 1
                return b

            itA = 0
            for kvh in range(2):
                p0 = 64 * kvh
                for qb in range(NT):
                    if qb < 2:
                        keys = [(0, None), (1, None)]
                    else:
                        keys = [(0, None), (1, None)]
                        if qb - 1 >= 2:
                            keys.append((qb - 1, mlo))
                        keys.append((qb, None))
                        if qb + 1 < NT:
                            keys.append((qb + 1, mhi))
                    sl = itA % 2
                    itA += 1
                    for j, (kt, msk) in enumerate(keys):
                        bk = next_sbank()
                        op("pe", lambda e, bk=bk, kt=kt, qb=qb, p0=p0: e.matmul(
                            PS[bk][:, :], lhsT=QKT.ap(4 * NTOK + kt * 128, [[1, 128]], p0=p0, np_=64),
                            rhs=QKT.ap(qb * 128, [[NTOK, 4], [1, 128]], p0=p0, np_=64), start=True, stop=True),
                           r=[QKT.b[kt], QKT.b[qb]], w=[PB[bk]])
                        pt_ap = PTA.ap((sl * 5 + j) * 512, [[1, 512]])
                        op("act", lambda e, bk=bk, pt_ap=pt_ap: e.activation(
                            out=pt_ap, in_=PS[bk][:, :], func=AF.Exp, scale=0.125), r=[PB[bk]], w=[PTA.b[sl]])
                        if msk is not None:
                            op("dve", lambda e, sl=sl, j=j, msk=msk: e.tensor_tensor(
                                out=PTA.ap((sl * 5 + j) * 512, [[128, 4], [1, 128]]),
                                in0=PTA.ap((sl * 5 + j) * 512, [[128, 4], [1, 128]]),
                                in1=msk.ap(0, [[0, 4], [1, 128]]), op=ALU.mult),
                               r=[PTA.b[sl], msk.b[0]], w=[PTA.b[sl]])
                    ob = 5 + (itA % 2)
                    for hq in range(4):
                        for j, (kt, msk) in enumerate(keys):
                            op("pe", lambda e, hq=hq, j=j, kt=kt, sl=sl, ob=ob, kvh=kvh: e.matmul(
                                PS[ob][:, hq * 65:(hq + 1) * 65],
                                lhsT=PTA.ap((sl * 5 + j) * 512 + hq * 128, [[1, 128]]),
                                rhs=VA.ap(kt * 130 + kvh * 65, [[1, 65]]),
                                start=(j == 0), stop=(j == len(keys) - 1)),
                               r=[PTA.b[sl], VA.b[kt]], w=[PB[ob]])
                    ri = stat_slot()
                    den = rr.ap(ri * 8, [[1, 4]])
                    op("dve", lambda e, ob=ob, den=den, kvh=kvh: e.tensor_tensor(
                        out=den, in0=pap(PS[ob], 64, [[65, 4]]), in1=esink[:, kvh * 4:kvh * 4 + 4], op=ALU.add),
                       r=[PB[ob], esink.b[0]], w=[rr.b[ri]])
                    op("dve", lambda e, den=den: e.reciprocal(den, den), r=[rr.b[ri]], w=[rr.b[ri]])
                    op("dve", lambda e, ob=ob, ri=ri, qb=qb, kvh=kvh: e.tensor_tensor(
                        out=y.ap(qb * 1024 + kvh * 256, [[64, 4], [1, 64]]), in0=pap(PS[ob], 0, [[65, 4], [1, 64]]),
                        in1=rr.ap(ri * 8, [[1, 4], [0, 64]]), op=ALU.mult),
                       r=[PB[ob], rr.b[ri]], w=[y.b[qb]])

            kb.barrier()
            groups = [([0, 1], [0, 1])] + [([2 + 4 * g + i for i in range(4)], list(range(NT))) for g in range(4)]
            units = [(h, m, qbs, keys) for h in range(4) for (qbs, keys) in groups for m in range(2)]
            sb4 = [0]

            def emit_score(u, j):
                h, m, qbs, keys = units[u]
                kt = keys[j]
                nq = len(qbs) * 128
                q0 = qbs[0] * 128
                sq_, sk_ = 5 + 2 * h, 6 + 2 * h
                bk = sb4[0] % 4
                sb4[0] += 1
                slot = u % 3
                op("pe", lambda e: e.matmul(
                    PS[bk][:, 0:nq], lhsT=QKT.ap(sk_ * NTOK + kt * 128, [[1, 128]], p0=64 * m, np_=64),
                    rhs=QKT.ap(sq_ * NTOK + q0, [[1, nq]], p0=64 * m, np_=64), start=True, stop=True),
                   r=[QKT.b[kt]] + [QKT.b[q] for q in qbs], w=[PB[bk]])
                op("act", lambda e: e.activation(
                    out=PTB.ap((slot * NT + j) * 512, [[1, nq]]), in_=PS[bk][:, 0:nq], func=AF.Exp, scale=0.125),
                   r=[PB[bk]], w=[PTB.b[slot]])

            def emit_pv(u, j):
                h, m, qbs, keys = units[u]
                kt = keys[j]
                slot = u % 3
                for qi in range(len(qbs)):
                    ob = 4 + qi
                    op("pe", lambda e, qi=qi, ob=ob: e.matmul(
                        PS[ob][:, m * 129:(m + 1) * 129],
                        lhsT=PTB.ap((slot * NT + j) * 512 + qi * 128, [[1, 128]]),
                        rhs=VB.ap(kt * 516 + h * 129, [[1, 129]]),
                        start=(j == 0), stop=(j == len(keys) - 1)),
                       r=[PTB.b[slot], VB.b[kt]], w=[PB[ob]])

            def emit_combine(u):
                h, m, qbs, keys = units[u]
                ris = []
                for qi, qb in enumerate(qbs):
                    ob = 4 + qi
                    ri = stat_slot()
                    ris.append(ri)
                    r12 = rr.ap(ri * 8, [[1, 2]])
                    op("dve", lambda e, ob=ob, r12=r12: e.reciprocal(r12, pap(PS[ob], 128, [[129, 2]])),
                       r=[PB[ob]], w=[rr.b[ri]])
                    op("dve", lambda e, ri=ri: e.tensor_tensor(
                        out=rr.ap(ri * 8 + 1, [[1, 1]]), in0=rr.ap(ri * 8 + 1, [[1, 1]]), in1=lam[:, 2:3],
                        op=ALU.mult), r=[rr.b[ri], lam.b[0]], w=[rr.b[ri]])
                    t_, y_ = tmpy[qi], yb[qi]
                    op("dve", lambda e, ob=ob, t_=t_, ri=ri: e.tensor_scalar(
                        t_[:], PS[ob][:, 0:128], rr.ap(ri * 8, [[1, 1]]), None, ALU.mult),
                       r=[PB[ob], rr.b[ri]], w=[t_.b[0]])
                    op("dve", lambda e, ob=ob, t_=t_, y_=y_, ri=ri: e.scalar_tensor_tensor(
                        out=y_[:], in0=PS[ob][:, 129:257], scalar=rr.ap(ri * 8 + 1, [[1, 1]]), in1=t_[:],
                        op0=ALU.mult, op1=ALU.add), r=[PB[ob], rr.b[ri], t_.b[0]], w=[y_.b[0]])
                for qi, qb in enumerate(qbs):
                    ri = ris[qi]
                    t_, y_ = tmpy[qi], yb[qi]
                    ssq = rr.ap(ri * 8 + 2, [[1, 1]])
                    op("dve", lambda e, t_=t_, y_=y_, ssq=ssq: e.scalar_tensor_tensor(
                        out=t_[:], in0=y_[:], scalar=1.0, in1=y_[:], op0=ALU.mult, op1=ALU.mult, accum_out=ssq),
                       r=[y_.b[0]], w=[t_.b[0], rr.b[ri]])
                    rstd_from_ss(ssq, ssq, 128.0, [rr.b[ri]], [rr.b[ri]])
                    op("dve", lambda e, y_=y_, ssq=ssq, qb=qb, h=h: e.scalar_tensor_tensor(
                        out=y.ap(qb * 1024 + 512 + h * 128, [[1, 128]]), in0=y_[:], scalar=ssq, in1=gsub[:],
                        op0=ALU.mult, op1=ALU.mult), r=[y_.b[0], rr.b[ri], gsub.b[0]], w=[y.b[qb]])

            for u in (0, 1):
                for j in range(len(units[u][3])):
                    emit_score(u, j)
            for u in range(len(units)):
                su = u + 2
                nsj = len(units[su][3]) if su < len(units) else 0
                npj = len(units[u][3])
                for j in range(max(nsj, npj)):
                    if j < nsj:
                        emit_score(su, j)
                    if j < npj:
                        emit_pv(u, j)
                if units[u][1] == 1:
                    emit_combine(u)
            kb.barrier()

        def rows_ap(dram, tt, cmajor):
            base = 0 if dram.shape[0] == NTOK else -256
            if tt < 2 or not cmajor:
                r0 = tt * 128 + base
                return dram[r0:r0 + 128, :]
            pt = tt - 2
            ct, t = pt // 8, pt % 8
            r0 = 256 + base + ct * 1024 + t
            return AP(dram.tensor, r0 * 1024, [[8 * 1024, 128], [1, 1024]])

        def phase_outproj(y, hT, wo_dram, resid_dram, tiles, cmajor=False):
            a = Arena(*R_B)
            wo = mk("wo", [128, 8, 1024], BF16, arena=a)
            wog = [mk("wog%d" % i, [128, 8, 1024], BF16, arena=a) for i in range(2)]
            yT = [mk("yT%d" % i, [128, 8, 128], BF16, arena=a) for i in range(2)]
            xs = [mk("ox%d" % i, [128, 1024], F32, arena=a) for i in range(3)]
            for kc in range(8):
                dma("pool", wo[:, kc, :], wo_dram[kc * 128:(kc + 1) * 128, :], w=[wo.b[0]])
            rs = sorted(set(1 if tt < 2 else 0 for tt in tiles))
            for r_ in rs:
                for half in range(2):
                    op("dve", lambda e, r_=r_, half=half: e.tensor_tensor(
                        out=wog[r_].ap(half * 512, [[1024, 8], [1, 512]]), in0=wo.ap(half * 512, [[1024, 8], [1, 512]]),
                        in1=gate_rep.ap((0 * 2 + r_) * 1024 + half * 512, [[0, 8], [1, 512]]), op=ALU.mult),
                       r=[wo.b[0], gate_rep.b[r_]], w=[wog[r_].b[0]])
            for it, tt in enumerate(tiles):
                r_ = 1 if tt < 2 else 0
                x = xs[it % 3]
                dma("sp", x[:], rows_ap(resid_dram, tt, cmajor), w=[x.b[0]])
                yt_ = yT[it % 2]
                for half in range(2):
                    bank = 5 + ((it * 2 + half) % 3)
                    for c4 in range(4):
                        c = half * 4 + c4
                        op("pe", lambda e, c=c, c4=c4, bank=bank, tt=tt: e.matmul(
                            PS[bank][:, c4 * 128:(c4 + 1) * 128], lhsT=y[:, tt, c * 128:(c + 1) * 128], rhs=ident_b[:],
                            start=True, stop=True), r=[y.b[tt], ident_b.b[0]], w=[PB[bank]])
                    eng = "act" if half == 0 else "dve"
                    if eng == "act":
                        op("act", lambda e, half=half, bank=bank: e.activation(
                            out=yt_.ap(half * 512, [[1, 512]]), in_=PS[bank][:, :], func=AF.Identity),
                           r=[PB[bank]], w=[yt_.b[0]])
                    else:
                        op("dve", lambda e, half=half, bank=bank: e.tensor_copy(
                            out=yt_.ap(half * 512, [[1, 512]]), in_=PS[bank][:, :]), r=[PB[bank]], w=[yt_.b[0]])
                for half in range(2):
                    bank = (it * 2 + half) % 4
                    for kc in range(8):
                        op("pe", lambda e, kc=kc, half=half, bank=bank, r_=r_: e.matmul(
                            PS[bank][:, :], lhsT=yt_[:, kc, :], rhs=wog[r_][:, kc, half * 512:(half + 1) * 512],
                            start=(kc == 0), stop=(kc == 7)), r=[yt_.b[0], wog[r_].b[0]], w=[PB[bank]])
                    op("dve", lambda e, half=half, bank=bank, tt=tt: e.tensor_tensor(
                        out=hT[:, tt, half * 512:(half + 1) * 512], in0=PS[bank][:, :],
                        in1=x[:, half * 512:(half + 1) * 512], op=ALU.add), r=[PB[bank], x.b[0]], w=[hT.b[tt]])
            kb.barrier()

        def phase_mlp(li, fT, hT, blocks):
            a = Arena(*R_B)
            w1s = [mk("w1s%d" % i, [128, 8, 512], BF16, arena=a) for i in range(2)]
            w2s = [mk("w2s%d" % i, [128, 4, 1024], BF16, arena=a) for i in range(2)]
            w2g = [[mk("w2g%d_%d" % (i, r_), [128, 4, 1024], BF16, arena=a) for r_ in range(2)] for i in range(2)]
            hid = [mk("hid%d" % i, [128, 4, 512], BF16, arena=a) for i in range(2)]
            rl = [mk("rl%d" % i, [128, 512], F32, arena=a) for i in range(2)]
            rs = sorted(set(b[2] for b in blocks))

            def load_w(hg):
                w1_, w2_ = w1s[hg % 2], w2s[hg % 2]
                for kc in range(8):
                    dma("pool", w1_[:, kc, :], w1d[li, kc * 128:(kc + 1) * 128, hg * 512:(hg + 1) * 512], w=[w1_.b[0]])
                for hc in range(4):
                    r0 = hg * 512 + hc * 128
                    dma("pool", w2_[:, hc, :], w2d[li, r0:r0 + 128, :], w=[w2_.b[0]])
                for r_ in rs:
                    for half in range(2):
                        op("dve", lambda e, r_=r_, half=half, w2_=w2_, hg=hg: e.tensor_tensor(
                            out=w2g[hg % 2][r_].ap(half * 512, [[1024, 4], [1, 512]]),
                            in0=w2_.ap(half * 512, [[1024, 4], [1, 512]]),
                            in1=gate_rep.ap((1 * 2 + r_) * 1024 + half * 512, [[0, 4], [1, 512]]), op=ALU.mult),
                           r=[w2_.b[0], gate_rep.b[2 + r_]], w=[w2g[hg % 2][r_].b[0]])

            ir = [0]

            def emit_hidden(k, hg, t0, ntl, r_):
                ntok = ntl * 128
                hd = hid[k % 2]
                w1_ = w1s[hg % 2]
                for hc in range(4):
                    bank = hc
                    for kc in range(8):
                        op("pe", lambda e, kc=kc, hc=hc, bank=bank, t0=t0, ntok=ntok, w1_=w1_: e.matmul(
                            PS[bank][:, 0:ntok], lhsT=w1_[:, kc, hc * 128:(hc + 1) * 128],
                            rhs=fT[:, kc, t0 * 128:t0 * 128 + ntok], start=(kc == 0), stop=(kc == 7)),
                           r=[w1_.b[0]] + [fT.b[t0 + i] for i in range(ntl)], w=[PB[bank]])
                    rl_ = rl[ir[0] % 2]
                    ir[0] += 1
                    op("act", lambda e, bank=bank, ntok=ntok, rl_=rl_: e.activation(
                        out=rl_[:, 0:ntok], in_=PS[bank][:, 0:ntok], func=AF.Relu), r=[PB[bank]], w=[rl_.b[0]])
                    op("act", lambda e, hc=hc, ntok=ntok, rl_=rl_, hd=hd: e.activation(
                        out=hd[:, hc, 0:ntok], in_=rl_[:, 0:ntok], func=AF.Square),
                       r=[rl_.b[0]], w=[hd.b[0]])

            def emit_out(k, hg, t0, ntl, r_):
                hd = hid[k % 2]
                for i in range(ntl):
                    tt = t0 + i
                    for half in range(2):
                        bank = 4 + ((i * 2 + half) % 4)
                        for hc in range(4):
                            op("pe", lambda e, hc=hc, half=half, bank=bank, i=i, hd=hd, r_=r_, hg=hg: e.matmul(
                                PS[bank][:, :], lhsT=hd[:, hc, i * 128:(i + 1) * 128],
                                rhs=w2g[hg % 2][r_][:, hc, half * 512:(half + 1) * 512],
                                start=(hc == 0), stop=(hc == 3)), r=[hd.b[0], w2g[hg % 2][r_].b[0]], w=[PB[bank]])
                        op("dve", lambda e, half=half, bank=bank, tt=tt: e.tensor_tensor(
                            out=hT[:, tt, half * 512:(half + 1) * 512], in0=PS[bank][:, :],
                            in1=hT[:, tt, half * 512:(half + 1) * 512], op=ALU.add),
                           r=[PB[bank], hT.b[tt]], w=[hT.b[tt]])

            work = [(hg,) + tuple(b) for hg in range(8) for b in blocks]
            load_w(0)
            loaded = 0
            emit_hidden(0, *work[0])
            for k in range(len(work)):
                if k + 1 < len(work):
                    nhg = work[k + 1][0]
                    if nhg > loaded:
                        load_w(nhg)
                        loaded = nhg
                    emit_hidden(k + 1, *work[k + 1])
                emit_out(k, *work[k])
            kb.barrier()

        def phase_s5(aT, gc):
            INV2PI = 1.0 / (2.0 * math.pi)
            NCH = 288
            sa = Arena(R_H[1] - 4096, R_H[1])
            lamn = mk("lamn", [128, 2, 3, 32], F32, arena=sa)
            ktab = mk("ktab", [128, 6, 8], F32, arena=sa)
            rhoe = mk("rhoe", [128, 2, 32], F32, arena=sa)
            omg = mk("omg", [128, 2, 32], F32, arena=sa)
            ff = mk("ff", [128, 2, 2, 32], F32, arena=sa)
            A12 = mk("A12", [128, 2, 2, 64], F32, arena=sa)
            tsm = mk("tsm", [128, 4, 32], F32, arena=sa)
            for d_ in range(2):
                for w_ in range(3):
                    dma("sp", lamn[:, d_, w_, :], lamnd[d_, w_, :, :], w=[lamn.b[0]])
            dma("sp", ktab[:], ktabd[:, :, :], w=[ktab.b[0]])
            op("act", lambda e: e.activation(out=lamn.ap(64, [[96, 2], [1, 32]]), in_=lamn.ap(64, [[96, 2], [1, 32]]),
                                             func=AF.Exp), r=[lamn.b[0]], w=[lamn.b[0]])
            op("dve", lambda e: e.tensor_tensor(out=rhoe[:], in0=lamn.ap(0, [[96, 2], [1, 32]]),
                                                in1=lamn.ap(64, [[96, 2], [1, 32]]), op=ALU.mult),
               r=[lamn.b[0]], w=[rhoe.b[0]])
            op("dve", lambda e: e.scalar_tensor_tensor(out=omg[:], in0=lamn.ap(32, [[96, 2], [1, 32]]), scalar=INV2PI,
                                                       in1=lamn.ap(64, [[96, 2], [1, 32]]), op0=ALU.mult, op1=ALU.mult),
               r=[lamn.b[0]], w=[omg.b[0]])

            def mk_scr(arena, name):
                return (mk(name + "x", [128, 8, 32], F32, arena=arena), mk(name + "xi", [128, 8, 32], I32, arena=arena),
                        mk(name + "t1", [128, 8, 32], F32, arena=arena), mk(name + "t2", [128, 8, 32], F32, arena=arena))

            def cpow(arena, d_, krow, name, scr):
                re_ = mk(name + "re", [128, 8, 32], F32, arena=arena)
                im_ = mk(name + "im", [128, 8, 32], F32, arena=arena)
                x_, xi_ = scr[0], scr[1]
                kap = ktab.ap(krow * 8, [[1, 8], [0, 32]])
                op("dve", lambda e: e.tensor_tensor(out=re_[:], in0=rhoe.ap(d_ * 32, [[0, 8], [1, 32]]), in1=kap,
                                                    op=ALU.mult), r=[rhoe.b[0], ktab.b[0]], w=[re_.b[0]])
                op("act", lambda e: e.activation(out=re_[:], in_=re_[:], func=AF.Exp), r=[re_.b[0]], w=[re_.b[0]])
                op("dve", lambda e: e.tensor_tensor(out=x_[:], in0=omg.ap(d_ * 32, [[0, 8], [1, 32]]), in1=kap,
                                                    op=ALU.mult), r=[omg.b[0], ktab.b[0]], w=[x_.b[0]])
                op("dve", lambda e: e.tensor_copy(out=xi_[:], in_=x_[:]), r=[x_.b[0]], w=[xi_.b[0]])
                op("dve", lambda e: e.tensor_tensor(out=x_[:], in0=x_[:], in1=xi_[:], op=ALU.subtract),
                   r=[x_.b[0], xi_.b[0]], w=[x_.b[0]])
                op("act", lambda e: e.activation(out=im_[:], in_=x_[:], func=AF.Sin, scale=2.0 * math.pi),
                   r=[x_.b[0]], w=[im_.b[0]])
                op("act", lambda e: e.activation(out=x_[:], in_=x_[:], func=AF.Sin, scale=math.pi),
                   r=[x_.b[0]], w=[x_.b[0]])
                op("act", lambda e: e.activation(out=x_[:], in_=x_[:], func=AF.Square), r=[x_.b[0]], w=[x_.b[0]])
                op("dve", lambda e: e.tensor_scalar(x_[:], x_[:], -2.0, 1.0, ALU.mult, ALU.add), r=[x_.b[0]], w=[x_.b[0]])
                op("dve", lambda e: e.tensor_tensor(out=im_[:], in0=im_[:], in1=re_[:], op=ALU.mult),
                   r=[im_.b[0], re_.b[0]], w=[im_.b[0]])
                op("dve", lambda e: e.tensor_tensor(out=re_[:], in0=re_[:], in1=x_[:], op=ALU.mult),
                   r=[re_.b[0], x_.b[0]], w=[re_.b[0]])
                return re_, im_

            def cmul_f(d_, re_, im_, scr):
                t1, t2 = scr[2], scr[3]
                fr = ff.ap((d_ * 2 + 0) * 32, [[0, 8], [1, 32]])
                fi = ff.ap((d_ * 2 + 1) * 32, [[0, 8], [1, 32]])
                op("dve", lambda e: e.tensor_tensor(out=t1[:], in0=re_[:], in1=fi, op=ALU.mult),
                   r=[re_.b[0], ff.b[0]], w=[t1.b[0]])
                op("dve", lambda e: e.tensor_tensor(out=t2[:], in0=im_[:], in1=fi, op=ALU.mult),
                   r=[im_.b[0], ff.b[0]], w=[t2.b[0]])
                op("dve", lambda e: e.tensor_tensor(out=re_[:], in0=re_[:], in1=fr, op=ALU.mult),
                   r=[re_.b[0], ff.b[0]], w=[re_.b[0]])
                op("dve", lambda e: e.tensor_tensor(out=im_[:], in0=im_[:], in1=fr, op=ALU.mult),
                   r=[im_.b[0], ff.b[0]], w=[im_.b[0]])
                op("dve", lambda e: e.tensor_tensor(out=re_[:], in0=re_[:], in1=t2[:], op=ALU.subtract),
                   r=[re_.b[0], t2.b[0]], w=[re_.b[0]])
                op("dve", lambda e: e.tensor_tensor(out=im_[:], in0=im_[:], in1=t1[:], op=ALU.add),
                   r=[im_.b[0], t1.b[0]], w=[im_.b[0]])

            U = T(kb, "U", [128, 64, NCH], BF16, R_H[0], nslots=64)
            WS = [T(kb, "WS%d" % d_, [128, 64, 2, 64], BF16, U.end + d_ * 16384, nslots=1) for d_ in range(2)]
            assert WS[1].end <= R_H[1] - 4096
            a = Arena(*R_B)
            wsi = mk("wsi", [128, 8, 1024], BF16, arena=a)
            Xc = [mk("Xc%d" % i, [128, 64, 8, 16], BF16, arena=a) for i in range(2)]
            scrU = mk_scr(a, "scrU")
            pw0 = [cpow(a, d_, 0, "pw0_%d" % d_, scrU) for d_ in range(2)]
            for d_ in range(2):
                re_, im_ = pw0[d_]
                lre = lamn[:, d_, 0, :]
                lim = lamn[:, d_, 1, :]
                a1r = re_[:, 0, :]
                a1i = im_[:, 0, :]
                nr, den, tA_, tB_ = tsm[:, 0, :], tsm[:, 1, :], tsm[:, 2, :], tsm[:, 3, :]
                rb = [re_.b[0], im_.b[0], lamn.b[0], tsm.b[0]]
                op("dve", lambda e, nr=nr, a1r=a1r: e.tensor_scalar(nr, a1r, -1.0, None, ALU.add), r=rb, w=[tsm.b[0]])
                op("dve", lambda e, tA_=tA_, lre=lre: e.tensor_tensor(out=tA_, in0=lre, in1=lre, op=ALU.mult),
                   r=rb, w=[tsm.b[0]])
                op("dve", lambda e, den=den, lim=lim: e.tensor_tensor(out=den, in0=lim, in1=lim, op=ALU.mult),
                   r=rb, w=[tsm.b[0]])
                op("dve", lambda e, den=den, tA_=tA_: e.tensor_tensor(out=den, in0=den, in1=tA_, op=ALU.add),
                   r=rb, w=[tsm.b[0]])
                op("dve", lambda e, den=den: e.reciprocal(den, den), r=rb, w=[tsm.b[0]])
                fr_ = ff[:, d_, 0, :]
                fi_ = ff[:, d_, 1, :]
                op("dve", lambda e, tA_=tA_, nr=nr, lre=lre: e.tensor_tensor(out=tA_, in0=nr, in1=lre, op=ALU.mult),
                   r=rb, w=[tsm.b[0]])
                op("dve", lambda e, tB_=tB_, a1i=a1i, lim=lim: e.tensor_tensor(out=tB_, in0=a1i, in1=lim, op=ALU.mult),
                   r=rb, w=[tsm.b[0]])
                op("dve", lambda e, tA_=tA_, tB_=tB_: e.tensor_tensor(out=tA_, in0=tA_, in1=tB_, op=ALU.add),
                   r=rb, w=[tsm.b[0]])
                op("dve", lambda e, fr_=fr_, tA_=tA_, den=den: e.tensor_tensor(out=fr_, in0=tA_, in1=den, op=ALU.mult),
                   r=rb, w=[ff.b[0]])
                op("dve", lambda e, tA_=tA_, a1i=a1i, lre=lre: e.tensor_tensor(out=tA_, in0=a1i, in1=lre, op=ALU.mult),
                   r=rb, w=[tsm.b[0]])
                op("dve", lambda e, tB_=tB_, nr=nr, lim=lim: e.tensor_tensor(out=tB_, in0=nr, in1=lim, op=ALU.mult),
                   r=rb, w=[tsm.b[0]])
                op("dve", lambda e, tA_=tA_, tB_=tB_: e.tensor_tensor(out=tA_, in0=tA_, in1=tB_, op=ALU.subtract),
                   r=rb, w=[tsm.b[0]])
                op("dve", lambda e, fi_=fi_, tA_=tA_, den=den: e.tensor_tensor(out=fi_, in0=tA_, in1=den, op=ALU.mult),
                   r=rb, w=[ff.b[0]])
                a8r = re_[:, 7, :]
                a8i = im_[:, 7, :]
                for hh in range(2):
                    op("dve", lambda e, hh=hh, a8r=a8r: e.tensor_copy(out=A12[:, d_, 0, hh * 32:(hh + 1) * 32], in_=a8r),
                       r=rb, w=[A12.b[0]])
                op("dve", lambda e, a8i=a8i: e.tensor_scalar(A12[:, d_, 1, 0:32], a8i, -1.0, None, ALU.mult),
                   r=rb, w=[A12.b[0]])
                op("dve", lambda e, a8i=a8i: e.tensor_copy(out=A12[:, d_, 1, 32:64], in_=a8i), r=rb, w=[A12.b[0]])
            ET = [mk("ET%d" % i, [128, 128], F32, arena=a) for i in range(2)]
            ie = 0
            for d_ in range(2):
                ere, eim = cpow(a, d_, 4 + d_, "E%d" % d_, scrU)
                cmul_f(d_, ere, eim, scrU)
                for part, src in enumerate((ere, eim)):
                    for b2 in range(2):
                        bank = 6 + (ie % 2)
                        et = ET[ie % 2]
                        ie += 1
                        op("pe", lambda e, src=src, b2=b2, bank=bank: e.transpose(
                            PS[bank][:, 0:128], src.ap(b2 * 128, [[1, 128]]), ident_f[:]),
                           r=[src.b[0], ident_f.b[0]], w=[PB[bank]])
                        op("act", lambda e, et=et, bank=bank: e.activation(out=et[:], in_=PS[bank][:, 0:128],
                                                                            func=AF.Identity), r=[PB[bank]], w=[et.b[0]])
                        for k4 in range(4):
                            s_ = b2 * 4 + k4
                            off = ((d_ * 2 + part) * 8 + s_) * 4096
                            dma("sp", AP(escr.tensor, off, [[64, 32], [2048, 2], [1, 64]]),
                                et.ap(0, [[64, 2], [1, 64]], p0=k4 * 32, np_=32), r=[et.b[0]])
            esc_b = Buf()
            for kc in range(8):
                dma("pool", wsi[:, kc, :], s5wind[kc * 128:(kc + 1) * 128, :], w=[wsi.b[0]])
            for ct in range(3):
                nm = 128 if ct < 2 else 32
                xc = Xc[ct % 2]
                for s_ in range(8):
                    for half in range(2):
                        bank = (s_ * 2 + half) % 4
                        for kc in range(8):
                            op("pe", lambda e, kc=kc, half=half, bank=bank, ct=ct, s_=s_, nm=nm: e.matmul(
                                PS[bank][0:nm, :], lhsT=aT.ap(kc * NTOK + ct * 1024 + s_, [[8, nm]]),
                                rhs=wsi[:, kc, half * 512:(half + 1) * 512], start=(kc == 0), stop=(kc == 7)),
                               r=aT.b + [wsi.b[0]], w=[PB[bank]])
                        eng = "act" if half == 0 else "dve"
                        dst = xc.ap(half * 32 * 128 + s_ * 16, [[128, 32], [1, 16]], np_=nm)
                        srcp = pap(PS[bank], 0, [[16, 32], [1, 16]], np_=nm)
                        if eng == "act":
                            op("act", lambda e, dst=dst, srcp=srcp: e.activation(out=dst, in_=srcp, func=AF.Identity),
                               r=[PB[bank]], w=[xc.b[0]])
                        else:
                            op("dve", lambda e, dst=dst, srcp=srcp: e.tensor_copy(out=dst, in_=srcp),
                               r=[PB[bank]], w=[xc.b[0]])
                for g4 in range(16):
                    bank = 4 + (g4 % 2)
                    for gi in range(4):
                        g = g4 * 4 + gi
                        op("pe", lambda e, g=g, gi=gi, bank=bank, xc=xc, nm=nm: e.matmul(
                            PS[bank][:, gi * 128:gi * 128 + nm], lhsT=xc.ap(g * 128, [[1, 128]], np_=nm),
                            rhs=ident_b[0:nm, 0:nm], start=True, stop=True),
                           r=[xc.b[0], ident_b.b[0]], w=[PB[bank]])
                    dstu = U.ap(g4 * 4 * NCH + ct * 128, [[NCH, 4], [1, nm]])
                    srcu = pap(PS[bank], 0, [[128, 4], [1, nm]])
                    if g4 % 2 == 0:
                        op("act", lambda e, dstu=dstu, srcu=srcu: e.activation(out=dstu, in_=srcu, func=AF.Identity),
                           r=[PB[bank]], w=U.b[g4 * 4:g4 * 4 + 4])
                    else:
                        op("dve", lambda e, dstu=dstu, srcu=srcu: e.tensor_copy(out=dstu, in_=srcu),
                           r=[PB[bank]], w=U.b[g4 * 4:g4 * 4 + 4])
            kb.barrier()

            a = Arena(*R_B)
            NH = 1024
            tabs = [[mk("ws_%d_%d" % (d_, i), [128, NH], F32, arena=a) for i in range(4)] for d_ in range(2)]
            tmp = [[mk("wt_%d_%d" % (d_, i), [128, NH], F32, arena=a) for i in range(1)] for d_ in range(2)]
            for hf in range(4):
                for d_ in range(2):
                    eng = "dve"
                    ere, eim, bre, bim = tabs[d_]
                    t1 = tmp[d_][0]
                    for part, et in enumerate((ere, eim)):
                        for s_ in range(8):
                            off = ((d_ * 2 + part) * 8 + s_) * 4096 + hf * NH
                            dma("sp", et.ap(0, [[1, NH]], p0=s_ * 16, np_=16),
                                AP(escr.tensor, off, [[0, 16], [1, NH]]), w=[et.b[0]])
                    dma("sp", bre[:], bsjd[d_, 0, :, hf * NH:(hf + 1) * NH], w=[bre.b[0]])
                    dma("sp", bim[:], bsjd[d_, 1, :, hf * NH:(hf + 1) * NH], w=[bim.b[0]])
                    wsd = WS[d_]
                    o_re = wsd.ap(hf * 16 * 128, [[128, 16], [1, 64]])
                    o_im = wsd.ap(hf * 16 * 128 + 64, [[128, 16], [1, 64]])
                    v = lambda t_: t_.ap(0, [[64, 16], [1, 64]])
                    op(eng, lambda e, t1=t1, ere=ere, bre=bre: e.tensor_tensor(out=t1[:], in0=ere[:], in1=bre[:],
                                                                               op=ALU.mult),
                       r=[ere.b[0], bre.b[0]], w=[t1.b[0]])
                    op(eng, lambda e, bre=bre, eim=eim: e.tensor_tensor(out=bre[:], in0=eim[:], in1=bre[:], op=ALU.mult),
                       r=[eim.b[0], bre.b[0]], w=[bre.b[0]])
                    op(eng, lambda e, ere=ere, bim=bim: e.tensor_tensor(out=ere[:], in0=ere[:], in1=bim[:], op=ALU.mult),
                       r=[ere.b[0], bim.b[0]], w=[ere.b[0]])
                    op(eng, lambda e, eim=eim, bim=bim: e.tensor_tensor(out=eim[:], in0=eim[:], in1=bim[:], op=ALU.mult),
                       r=[eim.b[0], bim.b[0]], w=[eim.b[0]])
                    op(eng, lambda e, t1=t1, eim=eim, o_re=o_re, v=v: e.tensor_tensor(out=o_re, in0=v(t1), in1=v(eim),
                                                                                       op=ALU.subtract),
                       r=[t1.b[0], eim.b[0]], w=[wsd.b[0]])
                    op(eng, lambda e, ere=ere, bre=bre, o_im=o_im, v=v: e.tensor_tensor(out=o_im, in0=v(ere), in1=v(bre),
                                                                                        op=ALU.add),
                       r=[ere.b[0], bre.b[0]], w=[wsd.b[0]])
            kb.barrier()

            a = Arena(*R_B)
            HSt = mk("HS", [128, 2, 64, 256], BF16, arena=a)
            HS = [None, None]
            ra = Arena(*R_A)
            Swt = mk("Sw", [128, 4, 64, 32], F32, nslots=4, arena=ra)
            Zt = mk("Z", [128, 2, 96], F32, arena=ra)
            T1t = mk("T1", [128, 2, 64], F32, arena=ra)
            T2t = mk("T2", [128, 2, 64], F32, arena=ra)
            op("dve", lambda e: e.memset(Zt[:], 0.0), w=[Zt.b[0]])
            worder = [list(range(9)), [0, 8, 7, 6, 5, 4, 3, 2, 1]]

            def s_window(d_, w_, slot):
                c0 = w_ * 32
                for idx in range(64):
                    part, gl = idx // 32, idx % 32
                    bank = d_ * 4 + idx // 16
                    col = (idx % 16) * 32
                    for gh in range(2):
                        g = gh * 32 + gl
                        op("pe", lambda e, bank=bank, col=col, gh=gh, g=g, part=part, d_=d_, c0=c0: e.matmul(
                            pap(PS[bank], col, [[1, 32]], p0=gh * 64, np_=64),
                            lhsT=WS[d_].ap(g * 128 + part * 64, [[1, 64]]), rhs=U.ap(g * NCH + c0, [[1, 32]]),
                            start=True, stop=True), r=[WS[d_].b[0], U.b[g]], w=[PB[bank]])
                for b4 in range(4):
                    bank = d_ * 4 + b4
                    op("act", lambda e, bank=bank, b4=b4, slot=slot: e.activation(
                        out=Swt.ap(slot * 2048 + b4 * 512, [[1, 512]]), in_=PS[bank][:, :], func=AF.Identity),
                       r=[PB[bank]], w=[Swt.b[slot]])

            for d_ in range(2):
                s_window(d_, worder[d_][0], d_ * 2 + 0)
            a1v = A12.ap(0, [[128, 2], [1, 64]])
            a2v = A12.ap(64, [[128, 2], [1, 64]])
            for wi in range(9):
                if wi + 1 < 9:
                    for d_ in range(2):
                        s_window(d_, worder[d_][wi + 1], d_ * 2 + (wi + 1) % 2)
                buf = wi % 2
                sbufs = [Swt.b[0 * 2 + buf], Swt.b[1 * 2 + buf]]
                for ci in range(32):
                    off_f = (0 * 2 + buf) * 2048 + ci
                    off_b = (1 * 2 + buf) * 2048 + 31 - ci
                    dstr = off_b - off_f
                    sc = Swt.ap(off_f, [[dstr, 2], [32, 64]])
                    sc_lo = Swt.ap(off_f, [[dstr, 2], [32, 32]])
                    op("dve", lambda e: e.tensor_tensor(out=T1t[:], in0=Zt.ap(0, [[96, 2], [1, 64]]), in1=a1v,
                                                        op=ALU.mult), r=[Zt.b[0], A12.b[0]], w=[T1t.b[0]])
                    op("dve", lambda e: e.tensor_tensor(out=T2t[:], in0=Zt.ap(32, [[96, 2], [1, 64]]), in1=a2v,
                                                        op=ALU.mult), r=[Zt.b[0], A12.b[0]], w=[T2t.b[0]])
                    op("dve", lambda e: e.tensor_tensor(out=T1t[:], in0=T1t[:], in1=T2t[:], op=ALU.add),
                       r=[T1t.b[0], T2t.b[0]], w=[T1t.b[0]])
                    op("dve", lambda e, sc=sc: e.tensor_tensor(out=Zt.ap(0, [[96, 2], [1, 64]]), in0=T1t[:], in1=sc,
                                                               op=ALU.add),
                       r=[T1t.b[0]] + sbufs, w=[Zt.b[0]])
                    op("dve", lambda e, sc_lo=sc_lo: e.tensor_tensor(
                        out=Zt.ap(64, [[96, 2], [1, 32]]), in0=T1t.ap(0, [[64, 2], [1, 32]]), in1=sc_lo, op=ALU.add),
                       r=[T1t.b[0]] + sbufs, w=[Zt.b[0]])
                    cf = worder[0][wi] * 32 + ci
                    cb = worder[1][wi] * 32 + 31 - ci
                    if wi == 0:
                        store = (ci == 31)
                        col_f, col_b = 0, 255
                    else:
                        store = not (wi == 8 and ci == 31)
                        col_f, col_b = cf + 1 - 32, cb - 1 - 32
                    if store:
                        op("act", lambda e, col_f=col_f, col_b=col_b: e.activation(
                            out=HSt.ap(col_f, [[64 * 256 + col_b - col_f, 2], [256, 64]]),
                            in_=Zt.ap(0, [[96, 2], [1, 64]]), func=AF.Identity), r=[Zt.b[0]], w=[HSt.b[0]])
            kb.barrier()

            NB = 4
            ha = Arena(U.end, R_H[1] - 4096)
            pwa = Arena(ha.take(8 * 1024), ha.top)
            pwa = Arena(pwa.lo, pwa.lo + 8 * 1024)
            sca = Arena(ha.top, R_H[1] - 4096)
            scrY = mk_scr(sca, "scrY")
            pwy = [cpow(pwa, d_, 0 + d_, "pwy%d" % d_, scrY) for d_ in range(2)]
            pwp = [cpow(pwa, d_, 2 + d_, "pwp%d" % d_, scrY) for d_ in range(2)]
            for d_ in range(2):
                cmul_f(d_, pwp[d_][0], pwp[d_][1], scrY)
            kb.barrier()
            sca = Arena(sca.lo, R_H[1] - 4096)
            cn = [[mk("cn%d_%d" % (d_, p_), [128, NB, 16], F32, arena=sca) for p_ in range(2)] for d_ in range(2)]
            bn = [[mk("bn%d_%d" % (d_, p_), [128, NB, 16], F32, arena=sca) for p_ in range(2)] for d_ in range(2)]
            WYb = [mk("WY%d" % d_, [128, 2, NB, 128], BF16, arena=sca) for d_ in range(2)]
            Pb = [mk("P%d" % d_, [128, 2, NB, 128], BF16, arena=sca) for d_ in range(2)]
            WT = mk("WT", [128, 2 * NB, 128], BF16, arena=sca)
            g1t = [mk("g1t%d" % d_, [128, NB * 128], F32, arena=sca) for d_ in range(2)]
            g2t = [mk("g2t%d" % d_, [128, NB * 128], F32, arena=sca) for d_ in range(2)]
            ba = Arena(a.top, R_B[1])
            m3 = mk("m3", [128, 3, 128], F32, arena=ba)
            dsk = mk("dsk", [128, 1024], F32, arena=ba)
            wtt = [mk("wtt%d" % i, [128, 512], F32, arena=ba) for i in range(1)]
            Yg = [mk("Yg%d" % i, [128, 256], F32, arena=ba) for i in range(2)]
            gx = [mk("gx%d" % i, [128, 512], F32, arena=ba) for i in range(1)]
            gxh = [mk("gxh%d" % i, [128, 256], F32, arena=ba) for i in range(2)]
            for i in range(3):
                dma("sp", m3[:, i, :], m3d[i, :, :], w=[m3.b[0]])
            dma("sp", dsk[:], dskd[:, :], w=[dsk.b[0]])
            C_G = 2.0 * math.sqrt(2.0 / math.pi)
            iy = 0
            for bt in range(32 // NB):
                gl0 = bt * NB
                for d_ in range(2):
                    for p_ in range(2):
                        dma("sp", cn[d_][p_][:], cnd[d_, p_, :, gl0:gl0 + NB, :], w=[cn[d_][p_].b[0]])
                        dma("sp", bn[d_][p_][:], bnd[d_, p_, :, gl0:gl0 + NB, :], w=[bn[d_][p_].b[0]])
                for d_ in range(2):
                    eng = "dve"
                    t1, t2 = g1t[d_], g2t[d_]
                    v4 = lambda t_: t_.ap(0, [[128, NB], [16, 8], [1, 16]])
                    for (dst, coef, pw_, is_wy) in ((WYb[d_], cn[d_], pwy[d_], True), (Pb[d_], bn[d_], pwp[d_], False)):
                        cre = coef[0].ap(0, [[16, NB], [0, 8], [1, 16]])
                        cim = coef[1].ap(0, [[16, NB], [0, 8], [1, 16]])
                        pre = pw_[0].ap(gl0, [[1, NB], [32, 8], [0, 16]])
                        pim = pw_[1].ap(gl0, [[1, NB], [32, 8], [0, 16]])
                        rr_ = [coef[0].b[0], coef[1].b[0], pw_[0].b[0], pw_[1].b[0]]
                        o0 = dst.ap(0, [[128, NB], [16, 8], [1, 16]])
                        o1 = dst.ap(NB * 128, [[128, NB], [16, 8], [1, 16]])
                        op(eng, lambda e, t1=t1, cre=cre, pre=pre, v4=v4: e.tensor_tensor(out=v4(t1), in0=cre, in1=pre,
                                                                                          op=ALU.mult),
                           r=rr_, w=[t1.b[0]])
                        op(eng, lambda e, t2=t2, cim=cim, pim=pim, v4=v4: e.tensor_tensor(out=v4(t2), in0=cim, in1=pim,
                                                                                          op=ALU.mult),
                           r=rr_, w=[t2.b[0]])
                        op(eng, lambda e, t1=t1, t2=t2, o0=o0, v4=v4: e.tensor_tensor(out=o0, in0=v4(t1), in1=v4(t2),
                                                                                       op=ALU.subtract),
                           r=[t1.b[0], t2.b[0]], w=[dst.b[0]])
                        op(eng, lambda e, t1=t1, cre=cre, pim=pim, v4=v4: e.tensor_tensor(out=v4(t1), in0=cre, in1=pim,
                                                                                          op=ALU.mult),
                           r=rr_ + [dst.b[0]], w=[t1.b[0]])
                        op(eng, lambda e, t2=t2, cim=cim, pre=pre, v4=v4: e.tensor_tensor(out=v4(t2), in0=cim, in1=pre,
                                                                                          op=ALU.mult),
                           r=rr_ + [dst.b[0]], w=[t2.b[0]])
                        if is_wy:
                            op(eng, lambda e, t1=t1, t2=t2: e.tensor_tensor(out=t1[:], in0=t1[:], in1=t2[:], op=ALU.add),
                               r=[t1.b[0], t2.b[0]], w=[t1.b[0]])
                            op(eng, lambda e, t1=t1, o1=o1, v4=v4: e.tensor_scalar(o1, v4(t1), -1.0, None, ALU.mult),
                               r=[t1.b[0]], w=[dst.b[0]])
                        else:
                            op(eng, lambda e, t1=t1, t2=t2, o1=o1, v4=v4: e.tensor_tensor(out=o1, in0=v4(t1),
                                                                                           in1=v4(t2), op=ALU.add),
                               r=[t1.b[0], t2.b[0]], w=[dst.b[0]])
                for gh in range(2):
                    bf_, bb_ = 0 + gh * 2, 1 + gh * 2
                    for gll in range(NB):
                        for d_, bank in ((0, bf_), (1, bb_)):
                            for part in range(2):
                                op("pe", lambda e, gh=gh, gll=gll, d_=d_, bank=bank, part=part: e.matmul(
                                    PS[bank][:, gll * 128:(gll + 1) * 128],
                                    lhsT=Pb[d_].ap((part * NB + gll) * 128, [[1, 128]], p0=gh * 64, np_=64),
                                    rhs=WYb[d_].ap((part * NB + gll) * 128, [[1, 128]], p0=gh * 64, np_=64),
                                    start=(part == 0), stop=(part == 1)),
                                   r=[Pb[d_].b[0], WYb[d_].b[0]], w=[PB[bank]])
                    wt_ = wtt[0]
                    gbase = gh * 32 + gl0
                    mf = m3.ap(0, [[0, NB], [1, 128]])
                    mb = m3.ap(128, [[0, NB], [1, 128]])
                    op("dve", lambda e, wt_=wt_, bf_=bf_, mf=mf: e.tensor_tensor(
                        out=wt_.ap(0, [[128, NB], [1, 128]]), in0=pap(PS[bf_], 0, [[128, NB], [1, 128]]), in1=mf,
                        op=ALU.mult), r=[PB[bf_], m3.b[0]], w=[wt_.b[0]])
                    op("dve", lambda e, bb_=bb_, mb=mb: e.tensor_tensor(
                        out=gx[0].ap(0, [[128, NB], [1, 128]]), in0=pap(PS[bb_], 0, [[128, NB], [1, 128]]), in1=mb,
                        op=ALU.mult), r=[PB[bb_], m3.b[0]], w=[gx[0].b[0]])
                    op("dve", lambda e, wt_=wt_: e.tensor_tensor(out=wt_[:], in0=wt_[:], in1=gx[0][:], op=ALU.add),
                       r=[wt_.b[0], gx[0].b[0]], w=[wt_.b[0]])
                    op("dve", lambda e, gbase=gbase: e.tensor_tensor(
                        out=gx[0].ap(0, [[128, NB], [16, 8], [1, 16]]),
                        in0=dsk.ap(gbase * 16, [[16, NB], [0, 8], [1, 16]]),
                        in1=m3.ap(256, [[0, NB], [16, 8], [1, 16]]), op=ALU.mult),
                       r=[dsk.b[0], m3.b[0]], w=[gx[0].b[0]])
                    op("dve", lambda e, wt_=wt_, gh=gh: e.tensor_tensor(
                        out=WT.ap(gh * NB * 128, [[1, NB * 128]]), in0=wt_[:], in1=gx[0][:], op=ALU.add),
                       r=[wt_.b[0], gx[0].b[0]], w=[WT.b[0]])
                for gidx in range(2 * NB):
                    gh, gll = gidx // NB, gidx % NB
                    gl = gl0 + gll
                    wslot = gh * NB + gll
                    g = gh * 32 + gl
                    yb_ = 4 + (iy % 2)
                    yg = Yg[iy % 2]
                    op("pe", lambda e, yb_=yb_, wslot=wslot, g=g: e.matmul(
                        PS[yb_][:, 0:256], lhsT=WT.ap(wslot * 128, [[1, 128]]), rhs=U.ap(g * NCH + 32, [[1, 256]]),
                        start=True, stop=False), r=[WT.b[0], U.b[g]], w=[PB[yb_]])
                    k_ = 0
                    for d_ in range(2):
                        for part in range(2):
                            k_ += 1
                            op("pe", lambda e, yb_=yb_, d_=d_, part=part, gh=gh, gll=gll, gl=gl, k_=k_: e.matmul(
                                PS[yb_][:, 0:256],
                                lhsT=WYb[d_].ap((part * NB + gll) * 128, [[1, 128]], p0=gh * 64, np_=64),
                                rhs=HSt.ap(((d_ * 2 + part) * 32 + gl) * 256, [[1, 256]], p0=gh * 64, np_=64),
                                start=False, stop=(k_ == 4)), r=[WYb[d_].b[0], HSt.b[0]], w=[PB[yb_]])
                    op("act", lambda e, yb_=yb_, yg=yg: e.activation(out=yg[:], in_=PS[yb_][:, 0:256], func=AF.Identity),
                       r=[PB[yb_]], w=[yg.b[0]])
                    tb_ = 6 + (iy % 2)
                    gx_ = gxh[iy % 2]
                    for ct in range(2):
                        op("pe", lambda e, tb_=tb_, ct=ct, yg=yg: e.transpose(
                            PS[tb_][:, ct * 128:(ct + 1) * 128], yg[:, ct * 128:(ct + 1) * 128], ident_f[:]),
                           r=[yg.b[0], ident_f.b[0]], w=[PB[tb_]])
                    xps = PS[tb_][:, 0:256]
                    op("act", lambda e, xps=xps, gx_=gx_: e.activation(out=gx_[:, 0:256], in_=xps, func=AF.Square),
                       r=[PB[tb_]], w=[gx_.b[0]])
                    op("dve", lambda e, gx_=gx_: e.tensor_scalar(gx_[:, 0:256], gx_[:, 0:256], 0.044715, 1.0, ALU.mult,
                                                                 ALU.add), r=[gx_.b[0]], w=[gx_.b[0]])
                    op("dve", lambda e, xps=xps, gx_=gx_: e.tensor_tensor(out=gx_[:, 0:256], in0=xps, in1=gx_[:, 0:256],
                                                                          op=ALU.mult), r=[PB[tb_], gx_.b[0]],
                       w=[gx_.b[0]])
                    op("act", lambda e, gx_=gx_: e.activation(out=gx_[:, 0:256], in_=gx_[:, 0:256], func=AF.Sigmoid,
                                                              scale=C_G), r=[gx_.b[0]], w=[gx_.b[0]])
                    for ct in range(2):
                        op("dve", lambda e, ct=ct, g=g, tb_=tb_, gx_=gx_: e.tensor_tensor(
                            out=gc.ap((2 + ct * 8) * 1024 + g * 16, [[1024, 8], [1, 16]]),
                            in0=pap(PS[tb_], ct * 128, [[16, 8], [1, 16]]),
                            in1=gx_.ap(ct * 128, [[16, 8], [1, 16]]), op=ALU.mult),
                           r=[PB[tb_], gx_.b[0]], w=gc.b[2 + ct * 8:2 + ct * 8 + 8])
                    iy += 1
            kb.barrier()

            a = Arena(*R_B)
            gw = mk("gw", [128, 8, 1024], BF16, arena=a)
            gb = mk("gb", [1, 1024], F32, arena=a)
            gT = [mk("gT%d" % i, [128, 8, 128], BF16, arena=a) for i in range(2)]
            sg = [mk("sg%d" % i, [128, 512], F32, arena=a) for i in range(2)]
            for kc in range(8):
                dma("pool", gw[:, kc, :], gluwd[kc * 128:(kc + 1) * 128, :], w=[gw.b[0]])
            dma("sp", gb[:], glubd[:, :], w=[gb.b[0]])
            isg = 0
            for pt in range(16):
                tt = 2 + pt
                gt = gT[pt % 2]
                for half in range(2):
                    bank = 4 + ((pt * 2 + half) % 4)
                    for c4 in range(4):
                        c = half * 4 + c4
                        op("pe", lambda e, c=c, c4=c4, bank=bank, tt=tt: e.matmul(
                            PS[bank][:, c4 * 128:(c4 + 1) * 128], lhsT=gc[:, tt, c * 128:(c + 1) * 128], rhs=ident_b[:],
                            start=True, stop=True), r=[gc.b[tt], ident_b.b[0]], w=[PB[bank]])
                    if half == 0:
                        op("act", lambda e, half=half, bank=bank, gt=gt: e.activation(
                            out=gt.ap(half * 512, [[1, 512]]), in_=PS[bank][:, :], func=AF.Identity),
                           r=[PB[bank]], w=[gt.b[0]])
                    else:
                        op("dve", lambda e, half=half, bank=bank, gt=gt: e.tensor_copy(
                            out=gt.ap(half * 512, [[1, 512]]), in_=PS[bank][:, :]), r=[PB[bank]], w=[gt.b[0]])
                for half in range(2):
                    bank = (pt * 2 + half) % 4
                    for kc in range(8):
                        op("pe", lambda e, kc=kc, half=half, bank=bank, gt=gt: e.matmul(
                            PS[bank][:, :], lhsT=gt[:, kc, :], rhs=gw[:, kc, half * 512:(half + 1) * 512],
                            start=(kc == 0), stop=False), r=[gt.b[0], gw.b[0]], w=[PB[bank]])
                    op("pe", lambda e, half=half, bank=bank: e.matmul(
                        PS[bank][:, :], lhsT=ones_f[0:1, :], rhs=gb[0:1, half * 512:(half + 1) * 512],
                        start=False, stop=True), r=[ones_f.b[0], gb.b[0]], w=[PB[bank]])
                    s_ = sg[isg % 2]
                    isg += 1
                    op("act", lambda e, bank=bank, s_=s_: e.activation(out=s_[:], in_=PS[bank][:, :], func=AF.Sigmoid),
                       r=[PB[bank]], w=[s_.b[0]])
                    op("dve", lambda e, half=half, tt=tt, s_=s_: e.tensor_tensor(
                        out=gc[:, tt, half * 512:(half + 1) * 512], in0=gc[:, tt, half * 512:(half + 1) * 512],
                        in1=s_[:], op=ALU.mult), r=[gc.b[tt], s_.b[0]], w=[gc.b[tt]])
            kb.barrier()

        hT = T(kb, "hT", [128, NT, 1024], F32, R_H[0], nslots=NT)
        aT = T(kb, "aT", [128, 8, NTOK], BF16, R_A[0], nslots=NT)
        yy = T(kb, "yy", [128, NT, 1024], BF16, R_A[0], nslots=NT)
        resid = xin
        for li in layers:
            last = (li == 1)
            phase_mod(li)
            tiles_all = list(range(NT))
            phase_norm(li, 0, resid, None, aT, tiles_all)
            if li == 0:
                phase_attn(aT, yy)
                tiles = tiles_all
                phase_outproj(yy, hT, woutd, resid, tiles)
            else:
                phase_s5(aT, yy)
                tiles = list(range(2, NT))
                phase_outproj(yy, hT, s5woutd, resid, tiles, cmajor=True)
            phase_norm(li, 1, None, hT, aT, tiles)
            if last:
                blocks = [(2 + 4 * g, 4, 0) for g in range(4)]
            else:
                blocks = [(0, 2, 1)] + [(2 + 4 * g, 4, 0) for g in range(4)]
            phase_mlp(li, aT, hT, blocks)
            dst = hout if li == last_layer else hmid
            for tt in tiles:
                dma("sp", rows_ap(dst, tt, li == 1), hT[:, tt, :], r=[hT.b[tt]])
            kb.barrier()
            resid = hmid
        if dbg is not None:
            pass
    return nc


def _host_consts():
    inv = (10000.0 ** (-np.arange(16, dtype=np.float32) / np.float32(16))).astype(np.float32)
    pos = np.arange(2048)
    row = (pos // 64).astype(np.float32)
    col = (pos % 64).astype(np.float32)
    ang = np.concatenate([row[:, None] * inv[None], col[:, None] * inv[None]], axis=-1).astype(np.float32)
    cos = np.cos(ang).astype(np.float32).reshape(16, 128, 32).transpose(1, 0, 2)
    sin = np.sin(ang).astype(np.float32).reshape(16, 128, 32).transpose(1, 0, 2)
    k = np.arange(128)[:, None]
    q = np.arange(128)[None, :]
    masks = np.stack([(k >= q), (k <= q)]).astype(np.float32)
    return np.ascontiguousarray(cos), np.ascontiguousarray(sin), masks


_PROG = {}


def _get_prog(key):
    if key not in _PROG:
        _PROG[key] = build_program(list(key))
    return _PROG[key]


def _common_maps(inp, b):
    f = np.float32
    cvec = np.stack([np.asarray(inp["c"][b], f).reshape(8, 128).T, np.asarray(inp["c_ctx"], f).reshape(8, 128).T],
                    axis=-1)
    m = {
        "cvec": np.ascontiguousarray(cvec),
        "mod_w": np.asarray(inp["mod_w"], f),
        "modb": np.ascontiguousarray(np.asarray(inp["mod_b"], f).reshape(2, 48, 128).transpose(0, 2, 1)),
        "g1": np.ascontiguousarray(np.asarray(inp["norm1_g"], f).reshape(2, 8, 128).transpose(0, 2, 1)),
        "g2": np.ascontiguousarray(np.asarray(inp["norm2_g"], f).reshape(2, 8, 128).transpose(0, 2, 1)),
        "mlp_w1": np.asarray(inp["mlp_w1"], f),
        "mlp_w2": np.asarray(inp["mlp_w2"], f),
        "ident": np.eye(128, dtype=f),
    }
    return m


def _l0_maps(inp):
    f = np.float32
    cos, sin, masks = _host_consts()
    aq, ak = np.asarray(inp["a_q_norm"][0], f), np.asarray(inp["a_k_norm"][0], f)
    bq, bk = np.asarray(inp["b_q_norm"][0], f), np.asarray(inp["b_k_norm"][0], f)
    gains = np.stack([aq, ak, bq, bk])
    lqk = np.stack([inp["b_lq1"][0], inp["b_lk1"][0], inp["b_lq2"][0], inp["b_lk2"][0]]).astype(f)
    return {
        "attn_w_in": np.asarray(inp["attn_w_in"][0], f),
        "attn_w_out": np.asarray(inp["attn_w_out"][0], f),
        "rope_cos": cos, "rope_sin": sin,
        "qk_gain": np.ascontiguousarray(np.broadcast_to(gains[None], (128, 4, 64))),
        "a_sink": np.ascontiguousarray(np.broadcast_to(np.asarray(inp["a_sink"][0], f)[None], (128, 8))),
        "b_lqk": np.ascontiguousarray(np.broadcast_to(lqk[None], (128, 4, 64))),
        "b_subln": np.ascontiguousarray(np.broadcast_to(np.asarray(inp["b_subln"][0], f)[None], (128, 128))),
        "masks": masks,
    }


def _l1_maps(inp):
    f = np.float32
    maps = {
        "s5_w_in": np.asarray(inp["s5_w_in"][0], f),
        "s5_w_out": np.asarray(inp["s5_w_out"][0], f),
        "s5_glu_w": np.asarray(inp["s5_glu_w"][0], f),
        "s5_glu_b": np.ascontiguousarray(np.asarray(inp["s5_glu_b"][0], f).reshape(1, D)),
    }

    def nlay(x):
        return np.asarray(x, f).reshape(2, 32, 64).transpose(0, 2, 1).reshape(128, 32)

    lam_n = np.zeros((2, 3, 128, 32), f)
    bn = np.zeros((2, 2, 128, 32, 16), f)
    cn = np.zeros((2, 2, 128, 32, 16), f)
    bsj = np.zeros((2, 2, 128, 4096), f)
    for d in range(2):
        lam_n[d, 0] = nlay(inp["s5_lambda_re"][0][d])
        lam_n[d, 1] = nlay(inp["s5_lambda_im"][0][d])
        lam_n[d, 2] = nlay(np.broadcast_to(np.asarray(inp["s5_log_step"][0][d], f)[:, None], (64, 64)))
        for p, (bk, ck) in enumerate((("s5_b_re", "s5_c_re"), ("s5_b_im", "s5_c_im"))):
            b = np.asarray(inp[bk][0][d], f)
            c = np.asarray(inp[ck][0][d], f)
            bn[d, p] = b.reshape(2, 32, 64, 16).transpose(0, 2, 1, 3).reshape(128, 32, 16)
            cn[d, p] = c.reshape(2, 32, 16, 64).transpose(0, 3, 1, 2).reshape(128, 32, 16)
            bj = b.transpose(2, 0, 1).reshape(16, 4096)
            bsj[d, p] = np.broadcast_to(bj[None], (8, 16, 4096)).reshape(128, 4096)
    k = np.arange(8, dtype=f)
    ktab = np.stack([k + 1, 8 - k, -(k + 1), k - 8, 7 - k, k]).astype(f)
    s_ = np.arange(128) // 16
    j_ = np.arange(128) % 16
    mf = (s_[:, None] <= s_[None, :])
    mb = (s_[:, None] >= s_[None, :])
    md = (s_[:, None] == s_[None, :]) & (j_[:, None] == j_[None, :])
    maps.update({
        "s5_lam_n": lam_n, "s5_bn": bn, "s5_cn": cn, "s5_bsj": bsj,
        "s5_ktab": np.ascontiguousarray(np.broadcast_to(ktab[None], (128, 6, 8))),
        "s5_dsk": np.ascontiguousarray(np.broadcast_to(np.asarray(inp["s5_d"][0], f)[None], (128, D))),
        "s5_masks": np.stack([mf, mb, md]).astype(f),
    })
    return maps


def _run(layers, inp, xins, ncores=8):
    nc = _get_prog(tuple(layers))
    extra = {}
    if 0 in layers:
        extra.update(_l0_maps(inp))
    if 1 in layers:
        extra.update(_l1_maps(inp))
    maps = []
    for b in range(ncores):
        m = _common_maps(inp, b)
        m.update(extra)
        m["xin"] = np.ascontiguousarray(xins[b], dtype=np.float32)
        maps.append(m)
    res = run_bass_kernel_spmd(nc, maps, core_ids=list(range(ncores)))
    return [r["hout"] for r in res.results]


FUSED = True


def kernel(**inputs):
    inp = {k: np.asarray(v) for k, v in inputs.items()}
    nb = inp["x"].shape[0]
    xins = [np.concatenate([inp["ctx"][b], inp["x"][b]], axis=0) for b in range(nb)]
    if FUSED:
        outs = _run([0, 1], inp, xins, nb)
    else:
        h0 = _run([0], inp, xins, nb)
        outs = _run([1], inp, h0, nb)
    return np.stack(outs).astype(np.float32)
```
